# Optimizing a Trainium2 kernel written in Bass

```python
import math
import jax, jax.numpy as jnp
from jax import lax
import numpy as np

D_MODEL = 1024
BATCH = 4
SEQ = 4096
DEPTH = 4
DEC_BATCH = 32
DEC_SEQ = 16
PAST_LEN = 4096

CHUNK = 64
N_EVEN = (DEPTH + 1) // 2
N_ODD = DEPTH // 2
N_VRES = max(N_ODD - 1, 0)
MIX_WIDTH = D_MODEL
POOL_WIDTH = MIX_WIDTH // 2
POOL_GROUPS = 4
POOL_GW = POOL_WIDTH // POOL_GROUPS
POOL_WINDOWS = (2, 4, 8, 16)
POOL_HIST = max(POOL_WINDOWS) - 1
DIFF_WIDTH = MIX_WIDTH - POOL_WIDTH
DIFF_DH = 64
DIFF_HEADS = DIFF_WIDTH // (2 * DIFF_DH)
DIFF_VD = 2 * DIFF_DH
EVEN_IN = POOL_WIDTH + 3 * DIFF_WIDTH
ATTN_Q_BLOCK = 128
RW_N = 64
RW_HEADS = D_MODEL // RW_N
RW_DECAY_LORA = 64
RW_A_LORA = 64
RW_V_LORA = 32
RW_G_LORA = 160
RW_LN_EPS = 64e-5
N_MEM = 256
XA_HEADS = 4
XA_DH = D_MODEL // XA_HEADS
D_FF = -(-8 * D_MODEL // (3 * 256)) * 256
NORM_EPS = 1e-6
NEG_INF = -1e30

kernel_name = 'hybrid_pool_diffattn_rwkv7_stream_step'


def rmsnorm(x, g):
    x32 = x.astype(jnp.float32)
    y = x32 * lax.rsqrt(jnp.mean(x32 * x32, axis=-1, keepdims=True) + NORM_EPS)
    return (y * g.astype(jnp.float32)).astype(x.dtype)


def swiglu(h, wg, wu, wd):
    return (jax.nn.silu(h @ wg) * (h @ wu)) @ wd


def pool_mixer(u, hist, pos, w_grp, scale):
    T = u.shape[1]
    full = jnp.concatenate([hist.astype(u.dtype), u], axis=1)
    csum = jnp.cumsum(full.astype(jnp.float32), axis=1)
    csum = jnp.concatenate([jnp.zeros_like(csum[:, :1]), csum], axis=1)
    end = csum[:, POOL_HIST + 1:POOL_HIST + 1 + T]
    outs = []
    for g, w in enumerate(POOL_WINDOWS):
        sl = slice(g * POOL_GW, (g + 1) * POOL_GW)
        start = csum[:, POOL_HIST + 1 - w:POOL_HIST + 1 - w + T, sl]
        cnt = jnp.minimum(pos + 1, w).astype(jnp.float32)[None, :, None]
        pooled = ((end[..., sl] - start) / cnt - u[..., sl].astype(jnp.float32)).astype(u.dtype)
        outs.append(pooled @ w_grp[g])
    y = jnp.concatenate(outs, axis=-1) * scale
    return y, full[:, -POOL_HIST:]


def diff_attend(q, k, v, q_pos, k_pos, lam):
    s = jnp.einsum('bqhmd,bkhmd->bhmqk', q, k).astype(jnp.float32) * (DIFF_DH ** -0.5)
    mask = (k_pos[None, :] // CHUNK) <= (q_pos[:, None] // CHUNK)
    p = jax.nn.softmax(jnp.where(mask, s, NEG_INF), axis=-1)
    p = p[:, :, 0] - lam * p[:, :, 1]
    return jnp.einsum('bhqk,bkhe->bqhe', p.astype(v.dtype), v)


def diff_attention(q, k, v, q_pos, k_pos, lam):
    B, Tq = q.shape[:2]
    if Tq <= ATTN_Q_BLOCK:
        return diff_attend(q, k, v, q_pos, k_pos, lam)
    nb = Tq // ATTN_Q_BLOCK
    qb = jnp.moveaxis(q.reshape(B, nb, ATTN_Q_BLOCK, *q.shape[2:]), 1, 0)
    pb = q_pos.reshape(nb, ATTN_Q_BLOCK)
    ob = lax.map(lambda a: diff_attend(a[0], k, v, a[1], k_pos, lam), (qb, pb))
    return jnp.moveaxis(ob, 0, 1).reshape(B, Tq, *ob.shape[3:])


def even_mixer(h, pos, k_pos, k_past, v_past, pool_hist, layer_idx, w_in, pool_w, pool_scale,
               lq1, lk1, lq2, lk2, subln_g, w_out):
    B, T, _ = h.shape
    z = h @ w_in
    o1, o2, o3 = POOL_WIDTH, POOL_WIDTH + DIFF_WIDTH, POOL_WIDTH + 2 * DIFF_WIDTH
    u = z[..., :o1]
    q = z[..., o1:o2].reshape(B, T, DIFF_HEADS, 2, DIFF_DH)
    k_new = z[..., o2:o3].reshape(B, T, DIFF_HEADS, 2 * DIFF_DH)
    v_new = z[..., o3:].reshape(B, T, DIFF_HEADS, DIFF_VD)
    pool_out, pool_state = pool_mixer(u, pool_hist, pos, pool_w, pool_scale)
    if k_past is None:
        k_all, v_all = k_new, v_new
    else:
        k_all = jnp.concatenate([k_past.astype(k_new.dtype), k_new], axis=1)
        v_all = jnp.concatenate([v_past.astype(v_new.dtype), v_new], axis=1)
    lam_init = 0.8 - 0.6 * math.exp(-0.3 * layer_idx)
    f32 = jnp.float32
    lam = (jnp.exp(jnp.sum(lq1.astype(f32) * lk1.astype(f32)))
           - jnp.exp(jnp.sum(lq2.astype(f32) * lk2.astype(f32))) + lam_init)
    a = diff_attention(q, k_all.reshape(B, -1, DIFF_HEADS, 2, DIFF_DH), v_all, pos, k_pos, lam)
    a = rmsnorm(a, subln_g) * (1.0 - lam_init)
    y = jnp.concatenate([pool_out, a.reshape(B, T, DIFF_WIDTH).astype(pool_out.dtype)], axis=-1) @ w_out
    return y, k_new, v_new, pool_state


def rwkv_mixer(h, shift_prev, S0, v_first, vres, mu, wr, wk, wv, wo, w0, w1, w2, a0, a1, a2,
               g1, g2, k_k, k_a, r_k, lnx_g, lnx_b):
    B, T, D = h.shape
    f32 = jnp.float32
    h_prev = jnp.concatenate([shift_prev[:, None].astype(h.dtype), h[:, :-1]], axis=1)
    xx = h_prev - h
    xr, xw, xk, xv, xa, xg = [h + xx * mu[i] for i in range(6)]
    r = xr @ wr
    w = -jax.nn.softplus(-(w0 + jnp.tanh(xw @ w1) @ w2)) - 0.5
    k = xk @ wk
    v = xv @ wv
    if vres is None:
        v_first = v
    else:
        v0, v1, v2 = vres
        v = v + (v_first - v) * jax.nn.sigmoid(v0 + (xv @ v1) @ v2)
    a = jax.nn.sigmoid(a0 + (xa @ a1) @ a2)
    g = jax.nn.sigmoid(xg @ g1) @ g2
    heads = lambda t: t.reshape(B, T, RW_HEADS, RW_N).astype(f32)
    kk = heads(k * k_k)
    kk = kk / jnp.maximum(jnp.sqrt(jnp.sum(kk * kk, axis=-1, keepdims=True)), 1e-12)
    k = k * (1 + (a - 1) * k_a)
    decay = jnp.exp(-jnp.exp(w.astype(f32)))
    rh, kh, vh, ah, dh = heads(r), heads(k), heads(v), heads(a), heads(decay)

    def step(S, inp):
        r_t, d_t, k_t, v_t, kk_t, a_t = inp
        sa = jnp.einsum('bhvk,bhk->bhv', S, -kk_t)
        S = (S * d_t[:, :, None, :] + sa[..., None] * (kk_t * a_t)[:, :, None, :]
             + v_t[..., None] * k_t[:, :, None, :])
        return S, jnp.einsum('bhvk,bhk->bhv', S, r_t)

    xs = tuple(jnp.moveaxis(t, 1, 0) for t in (rh, dh, kh, vh, kk, ah))
    S_fin, y = lax.scan(step, S0.astype(f32), xs)
    y = jnp.moveaxis(y, 0, 1)
    m = jnp.mean(y, axis=-1, keepdims=True)
    var = jnp.mean(jnp.square(y - m), axis=-1, keepdims=True)
    y = (y - m) * lax.rsqrt(var + RW_LN_EPS)
    y = y.reshape(B, T, D) * lnx_g.astype(f32) + lnx_b.astype(f32)
    bonus = jnp.sum(rh * kh * r_k.astype(f32), axis=-1, keepdims=True) * vh
    y = (y + bonus.reshape(B, T, D)).astype(h.dtype)
    return (y * g) @ wo, h[:, -1], S_fin, v_first


def cross_attn(h, mk, mv, wq, wo):
    B, T, _ = h.shape
    q = (h @ wq).reshape(B, T, XA_HEADS, XA_DH)
    s = jnp.einsum('bqhd,bkhd->bhqk', q, mk.astype(q.dtype)).astype(jnp.float32) * (XA_DH ** -0.5)
    p = jax.nn.softmax(s, axis=-1).astype(q.dtype)
    o = jnp.einsum('bhqk,bkhd->bqhd', p, mv.astype(q.dtype)).reshape(B, T, D_MODEL)
    return o @ wo


def trunk(x, pos, k_pos, diff_k_past, diff_v_past, pool_hist, rw_shift, rw_state, mem_k, mem_v, P):
    new_k, new_v, new_pool, new_shift, new_S = [], [], [], [], []
    v_first = None
    for l in range(DEPTH):
        h = rmsnorm(x, P['norm_mix_g'][l])
        if l % 2 == 0:
            e = l // 2
            kp = None if diff_k_past is None else diff_k_past[e]
            vp = None if diff_v_past is None else diff_v_past[e]
            y, kn, vn, ps = even_mixer(h, pos, k_pos, kp, vp, pool_hist[e], l, P['ev_w_in'][e],
                                       P['ev_pool_w'][e], P['ev_pool_scale'][e], P['ev_lam_q1'][e],
                                       P['ev_lam_k1'][e], P['ev_lam_q2'][e], P['ev_lam_k2'][e],
                                       P['ev_subln_g'][e], P['ev_w_out'][e])
            new_k.append(kn)
            new_v.append(vn)
            new_pool.append(ps)
        else:
            o = l // 2
            vres = None if o == 0 else (P['rw_v0'][o - 1], P['rw_v1'][o - 1], P['rw_v2'][o - 1])
            y, sh, S, v_first = rwkv_mixer(h, rw_shift[o], rw_state[o], v_first, vres, P['rw_mu'][o],
                                           P['rw_wr'][o], P['rw_wk'][o], P['rw_wv'][o], P['rw_wo'][o],
                                           P['rw_w0'][o], P['rw_w1'][o], P['rw_w2'][o], P['rw_a0'][o],
                                           P['rw_a1'][o], P['rw_a2'][o], P['rw_g1'][o], P['rw_g2'][o],
                                           P['rw_k_k'][o], P['rw_k_a'][o], P['rw_r_k'][o],
                                           P['rw_lnx_g'][o], P['rw_lnx_b'][o])
            new_shift.append(sh)
            new_S.append(S)
        x = x + y
        h = rmsnorm(x, P['norm_xa_g'][l])
        x = x + cross_attn(h, mem_k[l], mem_v[l], P['xa_wq'][l], P['xa_wo'][l])
        h = rmsnorm(x, P['norm_ffn_g'][l])
        x = x + swiglu(h, P['ffn_wg'][l], P['ffn_wu'][l], P['ffn_wd'][l])
    y = rmsnorm(x, P['final_norm_g'])
    return y, jnp.stack(new_k), jnp.stack(new_v), jnp.stack(new_pool), jnp.stack(new_shift), jnp.stack(new_S)


def setup_inputs(seed: int = 0) -> dict:
    key = jax.random.key(seed)
    ks = iter(jax.random.split(key, 64))
    nrm = lambda shape, s=1.0: jax.random.normal(next(ks), shape, jnp.float32) * s
    gain = lambda shape: 1.0 + 0.02 * jax.random.normal(next(ks), shape, jnp.float32)
    D = D_MODEL
    return {
        'x_prompt': nrm((BATCH, SEQ, D)),
        'x_sample': nrm((DEC_BATCH, DEC_SEQ, D)),
        'cache_diff_k': nrm((N_EVEN, DEC_BATCH, PAST_LEN, DIFF_HEADS, 2 * DIFF_DH)),
        'cache_diff_v': nrm((N_EVEN, DEC_BATCH, PAST_LEN, DIFF_HEADS, DIFF_VD)),
        'state_pool': nrm((N_EVEN, DEC_BATCH, POOL_HIST, POOL_WIDTH)),
        'state_rw_shift': nrm((N_ODD, DEC_BATCH, D)),
        'state_rw_wkv': nrm((N_ODD, DEC_BATCH, RW_HEADS, RW_N, RW_N), 0.5),
        'cache_mem_k': nrm((DEPTH, DEC_BATCH, N_MEM, XA_HEADS, XA_DH)),
        'cache_mem_v': nrm((DEPTH, DEC_BATCH, N_MEM, XA_HEADS, XA_DH)),
        'mem_prompt': nrm((BATCH, N_MEM, D)),
        'norm_mix_g': gain((DEPTH, D)),
        'norm_xa_g': gain((DEPTH, D)),
        'norm_ffn_g': gain((DEPTH, D)),
        'final_norm_g': gain((D,)),
        'ev_w_in': nrm((N_EVEN, D, EVEN_IN), D ** -0.5),
        'ev_pool_w': nrm((N_EVEN, POOL_GROUPS, POOL_GW, POOL_GW), POOL_GW ** -0.5),
        'ev_pool_scale': gain((N_EVEN, POOL_WIDTH)),
        'ev_lam_q1': nrm((N_EVEN, DIFF_DH), 0.1),
        'ev_lam_k1': nrm((N_EVEN, DIFF_DH), 0.1),
        'ev_lam_q2': nrm((N_EVEN, DIFF_DH), 0.1),
        'ev_lam_k2': nrm((N_EVEN, DIFF_DH), 0.1),
        'ev_subln_g': gain((N_EVEN, DIFF_VD)),
        'ev_w_out': nrm((N_EVEN, MIX_WIDTH, D), MIX_WIDTH ** -0.5),
        'rw_mu': jax.random.uniform(next(ks), (N_ODD, 6, D), jnp.float32),
        'rw_wr': nrm((N_ODD, D, D), D ** -0.5),
        'rw_wk': nrm((N_ODD, D, D), D ** -0.5),
        'rw_wv': nrm((N_ODD, D, D), D ** -0.5),
        'rw_wo': nrm((N_ODD, D, D), D ** -0.5),
        'rw_w0': jnp.linspace(-6.5, -1.5, D, dtype=jnp.float32)[None] + nrm((N_ODD, D), 0.1),
        'rw_w1': nrm((N_ODD, D, RW_DECAY_LORA), D ** -0.5),
        'rw_w2': nrm((N_ODD, RW_DECAY_LORA, D), 0.1 * RW_DECAY_LORA ** -0.5),
        'rw_a0': nrm((N_ODD, D), 0.1),
        'rw_a1': nrm((N_ODD, D, RW_A_LORA), D ** -0.5),
        'rw_a2': nrm((N_ODD, RW_A_LORA, D), 0.1 * RW_A_LORA ** -0.5),
        'rw_v0': nrm((N_VRES, D), 0.1),
        'rw_v1': nrm((N_VRES, D, RW_V_LORA), D ** -0.5),
        'rw_v2': nrm((N_VRES, RW_V_LORA, D), 0.1 * RW_V_LORA ** -0.5),
        'rw_g1': nrm((N_ODD, D, RW_G_LORA), D ** -0.5),
        'rw_g2': nrm((N_ODD, RW_G_LORA, D), RW_G_LORA ** -0.5),
        'rw_k_k': 0.85 + nrm((N_ODD, D), 0.02),
        'rw_k_a': gain((N_ODD, D)),
        'rw_r_k': nrm((N_ODD, RW_HEADS, RW_N), 0.1),
        'rw_lnx_g': gain((N_ODD, D)),
        'rw_lnx_b': nrm((N_ODD, D), 0.02),
        'xa_wq': nrm((DEPTH, D, D), D ** -0.5),
        'xa_wk': nrm((DEPTH, D, D), D ** -0.5),
        'xa_wv': nrm((DEPTH, D, D), D ** -0.5),
        'xa_wo': nrm((DEPTH, D, D), D ** -0.5),
        'ffn_wg': nrm((DEPTH, D, D_FF), D ** -0.5),
        'ffn_wu': nrm((DEPTH, D, D_FF), D ** -0.5),
        'ffn_wd': nrm((DEPTH, D_FF, D), D_FF ** -0.5),
    }


def reference(x_prompt, x_sample, cache_diff_k, cache_diff_v, state_pool, state_rw_shift, state_rw_wkv,
              cache_mem_k, cache_mem_v, mem_prompt, norm_mix_g, norm_xa_g, norm_ffn_g, final_norm_g,
              ev_w_in, ev_pool_w, ev_pool_scale, ev_lam_q1, ev_lam_k1, ev_lam_q2, ev_lam_k2, ev_subln_g,
              ev_w_out, rw_mu, rw_wr, rw_wk, rw_wv, rw_wo, rw_w0, rw_w1, rw_w2, rw_a0, rw_a1, rw_a2,
              rw_v0, rw_v1, rw_v2, rw_g1, rw_g2, rw_k_k, rw_k_a, rw_r_k, rw_lnx_g, rw_lnx_b,
              xa_wq, xa_wk, xa_wv, xa_wo, ffn_wg, ffn_wu, ffn_wd):
    P = dict(norm_mix_g=norm_mix_g, norm_xa_g=norm_xa_g, norm_ffn_g=norm_ffn_g, final_norm_g=final_norm_g,
             ev_w_in=ev_w_in, ev_pool_w=ev_pool_w, ev_pool_scale=ev_pool_scale, ev_lam_q1=ev_lam_q1,
             ev_lam_k1=ev_lam_k1, ev_lam_q2=ev_lam_q2, ev_lam_k2=ev_lam_k2, ev_subln_g=ev_subln_g,
             ev_w_out=ev_w_out, rw_mu=rw_mu, rw_wr=rw_wr, rw_wk=rw_wk, rw_wv=rw_wv, rw_wo=rw_wo,
             rw_w0=rw_w0, rw_w1=rw_w1, rw_w2=rw_w2, rw_a0=rw_a0, rw_a1=rw_a1, rw_a2=rw_a2,
             rw_v0=rw_v0, rw_v1=rw_v1, rw_v2=rw_v2, rw_g1=rw_g1, rw_g2=rw_g2, rw_k_k=rw_k_k,
             rw_k_a=rw_k_a, rw_r_k=rw_r_k, rw_lnx_g=rw_lnx_g, rw_lnx_b=rw_lnx_b, xa_wq=xa_wq,
             xa_wo=xa_wo, ffn_wg=ffn_wg, ffn_wu=ffn_wu, ffn_wd=ffn_wd)
    Bp, Tp = x_prompt.shape[:2]
    pos_p = jnp.arange(Tp, dtype=jnp.int32)
    p_mem_k = jnp.stack([(mem_prompt @ xa_wk[l]).reshape(Bp, -1, XA_HEADS, XA_DH) for l in range(DEPTH)])
    p_mem_v = jnp.stack([(mem_prompt @ xa_wv[l]).reshape(Bp, -1, XA_HEADS, XA_DH) for l in range(DEPTH)])
    zero_pool = jnp.zeros((N_EVEN, Bp, POOL_HIST, POOL_WIDTH), x_prompt.dtype)
    zero_shift = jnp.zeros((N_ODD, Bp, D_MODEL), x_prompt.dtype)
    zero_wkv = jnp.zeros((N_ODD, Bp, RW_HEADS, RW_N, RW_N), jnp.float32)
    y_prompt, p_diff_k, p_diff_v, p_pool, p_rw_shift, p_rw_wkv = trunk(
        x_prompt, pos_p, pos_p, None, None, zero_pool, zero_shift, zero_wkv, p_mem_k, p_mem_v, P)
    past = cache_diff_k.shape[2]
    Ts = x_sample.shape[1]
    pos_s = past + jnp.arange(Ts, dtype=jnp.int32)
    kpos_s = jnp.arange(past + Ts, dtype=jnp.int32)
    y_sample, s_diff_k, s_diff_v, s_pool, s_rw_shift, s_rw_wkv = trunk(
        x_sample, pos_s, kpos_s, cache_diff_k, cache_diff_v, state_pool, state_rw_shift, state_rw_wkv,
        cache_mem_k, cache_mem_v, P)
    return (y_prompt, y_sample, p_diff_k, p_diff_v, p_pool, p_rw_shift, p_rw_wkv, p_mem_k, p_mem_v,
            s_diff_k, s_diff_v, s_pool, s_rw_shift, s_rw_wkv)
```

```python
import math
import numpy as np
from contextlib import ExitStack
import concourse.bass as bass
import concourse.mybir as mybir
from concourse.bass_utils import run_bass_kernel_spmd

F32 = mybir.dt.float32
BF16 = mybir.dt.bfloat16
AF = mybir.ActivationFunctionType
ALU = mybir.AluOpType
AX = mybir.AxisListType
ENGS = ["tensor", "vector", "scalar", "gpsimd", "sync"]

D = 1024
KC = 8
DFF = 2816
FC = 22
DEPTH = 4
NEVEN = 2
NODD = 2
EPS = 1e-6
LNEPS = 64e-5
CDEC = math.exp(-0.5)


class Buf:
    __slots__ = ("w", "r", "q")

    def __init__(self):
        self.w = None
        self.r = []
        self.q = None


class Prog:
    def __init__(self, nc):
        self.nc = nc
        self.es = ExitStack()
        self.ops = {e: [] for e in ENGS}
        self.sems = {}
        self.cnt = {}
        self.waited = {e: {} for e in ENGS}
        self.nsb = 0
        self.ndq = 0
        self.phase_es = None

    def sb(self, shape, dt=F32):
        self.nsb += 1
        st = self.phase_es if self.phase_es is not None else self.es
        return st.enter_context(self.nc.sbuf_tensor(f"sb{self.nsb}", list(shape), dt))

    def ps(self, shape, dt=F32):
        self.nsb += 1
        st = self.phase_es if self.phase_es is not None else self.es
        return st.enter_context(self.nc.psum_tensor(f"ps{self.nsb}", list(shape), dt))

    def begin_phase(self):
        self.barrier()
        self.phase_es = ExitStack()
        self.ndq = 0
        self.phase_id = getattr(self, "phase_id", 0) + 1

    def end_phase(self):
        self.barrier()
        self.phase_es.close()
        self.phase_es = None

    def dq(self):
        return True

    def op(self, eng, fn, reads=(), writes=(), dma=None):
        if dma is None:
            s = "e_" + eng
            inc = 1
        elif isinstance(dma, str):
            s = "d_" + dma
            inc = 16
        else:
            kb = writes[0] if writes else reads[0]
            pid = getattr(self, "phase_id", 0)
            if kb.q is None or kb.q[0] != pid:
                self.ndq += 1
                kb.q = (pid, f"q{self.ndq}")
            s = "d_" + kb.q[1]
            inc = 16
        own = "e_" + eng
        waits = {}
        wd = self.waited[eng]
        same_raw = eng in ("vector", "scalar", "gpsimd")

        def need(sv, same_ok):
            if sv is None:
                return
            sn, val = sv
            if sn == own and not same_ok:
                return
            if wd.get(sn, 0) >= val:
                return
            if waits.get(sn, 0) < val:
                waits[sn] = val

        for b in reads:
            need(b.w, same_raw)
        for b in writes:
            need(b.w, same_raw)
            for x in b.r:
                need(x, False)
        for k, v in waits.items():
            wd[k] = v
        self.cnt[s] = self.cnt.get(s, 0) + inc
        me = (s, self.cnt[s])
        for b in reads:
            b.r.append(me)
            if len(b.r) > 16:
                d = {}
                for sn, v in b.r:
                    if d.get(sn, 0) < v:
                        d[sn] = v
                b.r = list(d.items())
        for b in writes:
            b.w = me
            b.r = []
        self.ops[eng].append((list(waits.items()), fn, s, inc))
        return me

    def barrier(self):
        for eng in ENGS:
            waits = []
            for s, v in self.cnt.items():
                if self.waited[eng].get(s, 0) < v and s != "e_" + eng:
                    waits.append((s, v))
                    self.waited[eng][s] = v
            if waits:
                self.ops[eng].append((waits, None, None, 0))

    def emit(self):
        self.barrier()
        nc = self.nc
        for s in self.cnt:
            if s not in self.sems:
                self.sems[s] = self.es.enter_context(nc.semaphore(s))
        ops = self.ops
        sems = self.sems

        def run(e, lst):
            for waits, fn, s, inc in lst:
                for sn, v in waits:
                    e.wait_ge(sems[sn], v)
                if fn is not None:
                    fn(e).then_inc(sems[s], inc)

        with nc.Block() as block:
            @block.tensor
            def _(e):
                run(e, ops["tensor"])

            @block.vector
            def _(e):
                run(e, ops["vector"])

            @block.scalar
            def _(e):
                run(e, ops["scalar"])

            @block.gpsimd
            def _(e):
                run(e, ops["gpsimd"])

            @block.sync
            def _(e):
                run(e, ops["sync"])
        self.es.close()


def I(name, *a, **k):
    return lambda e: getattr(e, name)(*a, **k)


class Ring:
    def __init__(self, P, shape, dt, n, psum=False):
        self.items = []
        for _ in range(n):
            t = P.ps(shape, dt) if psum else P.sb(shape, dt)
            self.items.append((t, Buf()))
        self.i = 0

    def next(self):
        it = self.items[self.i % len(self.items)]
        self.i += 1
        return it


PV_SPECS = [("norm_mix_g", DEPTH, D), ("norm_xa_g", DEPTH, D), ("norm_ffn_g", DEPTH, D), ("final_norm_g", 1, D),
            ("ev_pool_scale", NEVEN, 512), ("ev_subln_g", NEVEN, 128),
            ("rw_mu", NODD * 6, D), ("rw_w0", NODD, D), ("rw_a0", NODD, D), ("rw_v0", 1, D),
            ("rw_k_k", NODD, D), ("rw_k_a", NODD, D), ("rw_r_k", NODD, D),
            ("rw_lnx_g", NODD, D), ("rw_lnx_b", NODD, D)]


def pv_layout():
    off = {}
    c = 0
    for name, n, ln in PV_SPECS:
        off[name] = (c, ln // 128)
        c += n * (ln // 128)
    return off, c


PV_OFF, PV_COLS = pv_layout()


def pack_pv(inp):
    out = np.zeros((128, PV_COLS), np.float32)
    for name, n, ln in PV_SPECS:
        a = np.asarray(inp[name], np.float32).reshape(n, ln // 128, 128)
        c0, w = PV_OFF[name]
        out[:, c0:c0 + n * w] = a.transpose(2, 0, 1).reshape(128, n * w)
    return out


def wl(w):
    K, F = w.shape
    return np.ascontiguousarray(w.reshape(K // 128, 128, F).transpose(1, 0, 2))


class Group:
    def __init__(self, name, nseq, T):
        self.name = name
        self.nseq = nseq
        self.T = T
        self.NT = nseq * T


import os
_SKIP = set(os.environ.get("BIS", "").split(","))


def build(TP, NSB, TS, PAST, NMEM=256, depth=DEPTH):
    nc = bass.Bass("TRN2", target_bir_lowering=False)
    P = Prog(nc)
    dram_in = {}
    dram_out = {}

    def din(name, shape, dt=F32):
        dram_in[name] = nc.dram_tensor(name, list(shape), dt, kind="ExternalInput").ap()
        return dram_in[name]

    def dout(name, shape, dt=F32):
        dram_out[name] = nc.dram_tensor(name, list(shape), dt, kind="ExternalOutput").ap()
        return dram_out[name]

    def dscr(name, shape, dt=F32):
        return nc.dram_tensor("scr_" + name, list(shape), dt, kind="Internal").ap()

    GP = Group("p", 1, TP)
    GS = Group("s", NSB, TS)
    groups = [GP, GS]
    NPB = PAST // 128

    xin = {"p": din("xT_p", [D, GP.NT]), "s": din("xT_s", [D, GS.NT])}
    pv_d = din("pvec", [128, PV_COLS])
    lam_d = din("lam", [1, NEVEN * 4 * 64])
    lnx_d = din("lnx", [NODD * 2, D])
    w_in_d = din("ev_w_in", [NEVEN, 128, KC, 2048])
    pool_w_d = din("ev_pool_w", [NEVEN, 128, 4, 128])
    w_out_d = din("ev_w_out", [NEVEN, 128, KC, D])
    rw_w_d = {k: din("rw_" + k, [NODD, 128, KC, D]) for k in ("wr", "wk", "wv", "wo")}
    rw_l1_d = {k: din("rw_" + k, [NODD if k != "v1" else 1, 128, KC, n]) for k, n in (("w1", 64), ("a1", 64), ("g1", 160), ("v1", 32))}
    rw_l2_d = {k: din("rw_" + k, [NODD if k != "v2" else 1, n, D]) for k, n in (("w2", 64), ("a2", 64), ("g2", 160), ("v2", 32))}
    xa_w_d = {k: din("xa_" + k, [DEPTH, 128, KC, D]) for k in ("wq", "wk", "wv", "wo")}
    ffn_g_d = din("ffn_wg", [DEPTH, 128, KC, DFF])
    ffn_u_d = din("ffn_wu", [DEPTH, 128, KC, DFF])
    ffn_d_d = din("ffn_wd", [DEPTH, 128, FC, D])
    memT_d = din("memT", [128, KC, NMEM])
    ckT_d = din("cache_kT", [NEVEN, NSB, 4, 128, PAST])
    cv_d = din("cache_v", [NEVEN, NSB, PAST, 4, 128])
    spool_d = din("state_pool", [NEVEN, NSB, 128, 4, 15])
    sshift_d = din("state_shift", [NODD, NSB, 128, KC])
    swkv_d = din("state_wkvT", [NODD, NSB, 128, KC, 64])
    cmkT_d = din("cache_mkT", [DEPTH, NSB, 128, KC, NMEM])
    cmv_d = din("cache_mv", [DEPTH, NSB, NMEM, D])

    yT_o = {"p": dout("yT_p", [D, GP.NT]), "s": dout("yT_s", [D, GS.NT])}
    kT_o = {"p": dout("kT_p", [NEVEN, 512, GP.NT]), "s": dout("kT_s", [NEVEN, 512, GS.NT])}
    v_o = {"p": dout("v_p", [NEVEN, GP.NT, 512]), "s": dout("v_s", [NEVEN, GS.NT, 512])}
    pool_o = {"p": dout("pool_p", [NEVEN, 1, 128, 4, 15]), "s": dout("pool_s", [NEVEN, NSB, 128, 4, 15])}
    shift_o = {"p": dout("shift_p", [NODD, 1, 128, KC]), "s": dout("shift_s", [NODD, NSB, 128, KC])}
    wkv_o = {"p": dout("wkv_p", [NODD, 1, 128, KC, 64]), "s": dout("wkv_s", [NODD, NSB, 128, KC, 64])}
    memk_o = dout("memkT", [DEPTH, 128, KC, NMEM])
    memv_o = dout("memv", [DEPTH, NMEM, D])

    xT = {g.name: dscr("x_" + g.name, [D, g.NT]) for g in groups}
    qT = {g.name: dscr("q_" + g.name, [512, g.NT], BF16) for g in groups}
    kTs = {g.name: dscr("k_" + g.name, [512, g.NT], BF16) for g in groups}
    vtm = {g.name: dscr("v_" + g.name, [g.NT, 512], BF16) for g in groups}
    mixT = {g.name: dscr("mix_" + g.name, [D, g.NT], BF16) for g in groups}
    vfirst = {g.name: dscr("vf_" + g.name, [D, g.NT]) for g in groups}
    mkT_s = dscr("mkT_s", [DEPTH, 128, KC, NMEM], BF16)
    mv_s = dscr("mv_s", [DEPTH, NMEM, D], BF16)

    def fm(ap):
        return ap.rearrange("(c p) t -> p c t", p=128)

    ident_f = P.sb([128, 128], F32)
    ident_b = P.sb([128, 128], BF16)
    ones_b = P.sb([128, 128], BF16)
    ones_f = P.sb([128, 128], F32)
    bones_b = P.sb([128, 128], BF16)
    bones_f = P.sb([128, 128], F32)
    mk3 = P.sb([128, 2, 128], F32)
    mkL = P.sb([128, 128], F32)
    mk3s = P.sb([16, 2, 16], F32)
    mkLs = P.sb([16, 16], F32)
    pv = P.sb([128, PV_COLS], F32)
    invcnt = P.sb([128, 16], F32)
    neglam = P.sb([128, NEVEN], F32)
    gsub = P.sb([128, NEVEN], F32)
    lamrow = P.sb([1, NEVEN * 4 * 64], F32)
    lamt = P.sb([1, 8], F32)
    cB = Buf()

    def gp(fn, **k):
        P.op("gpsimd", fn, **k)

    gp(I("memset", ones_f[:], 1.0), writes=[cB])
    gp(I("memset", ones_b[:], 1.0), writes=[cB])
    gp(I("memset", ident_f[:], 1.0), writes=[cB])
    gp(I("affine_select", out=ident_f[:], in_=ident_f[:], pattern=[[-1, 128]], compare_op=ALU.is_equal,
                                 fill=0.0, base=0, channel_multiplier=1), reads=[cB], writes=[cB])
    gp(I("tensor_copy", out=ident_b[:], in_=ident_f[:]), reads=[cB], writes=[cB])
    gp(I("memset", bones_f[:], 0.0), writes=[cB])
    gp(I("memset", bones_f[0:64, 0:64], 1.0), writes=[cB])
    gp(I("memset", bones_f[64:128, 64:128], 1.0), writes=[cB])
    gp(I("memset", bones_b[:], 0.0), writes=[cB])
    gp(I("memset", bones_b[0:64, 0:64], 1.0), writes=[cB])
    gp(I("memset", bones_b[64:128, 64:128], 1.0), writes=[cB])
    gp(I("memset", mk3[:], 1.0), writes=[cB])
    gp(I("memset", mkL[:], 1.0), writes=[cB])
    gp(I("affine_select", out=mk3[:, 0, :], in_=mk3[:, 0, :], pattern=[[1, 128]], compare_op=ALU.is_gt,
                                 fill=0.0, base=0, channel_multiplier=-1), reads=[cB], writes=[cB])
    gp(I("affine_select", out=mk3[:, 1, :], in_=mk3[:, 1, :], pattern=[[1, 128]], compare_op=ALU.is_ge,
                                 fill=0.0, base=0, channel_multiplier=-1), reads=[cB], writes=[cB])
    gp(I("affine_select", out=mkL[:], in_=mkL[:], pattern=[[-1, 128]], compare_op=ALU.is_gt,
                                 fill=0.0, base=0, channel_multiplier=1), reads=[cB], writes=[cB])
    gp(I("tensor_copy", out=mk3s[:], in_=mk3[0:16, :, 0:16]), reads=[cB], writes=[cB])
    gp(I("tensor_copy", out=mkLs[:], in_=mkL[0:16, 0:16]), reads=[cB], writes=[cB])
    gp(I("memset", mk3[0:64, :, 64:128], 0.0), reads=[cB], writes=[cB])
    gp(I("memset", mk3[64:128, :, 0:64], 0.0), reads=[cB], writes=[cB])
    gp(I("memset", mkL[0:64, 64:128], 0.0), reads=[cB], writes=[cB])
    gp(I("memset", mkL[64:128, 0:64], 0.0), reads=[cB], writes=[cB])
    gp(I("iota", invcnt[:], pattern=[[1, 16]], base=1, channel_multiplier=0,
                        allow_small_or_imprecise_dtypes=True), writes=[cB])
    P.op("vector", I("reciprocal", out=invcnt[:], in_=invcnt[:]), reads=[cB], writes=[cB])
    P.op("sync", I("dma_start", out=pv[:], in_=pv_d), writes=[cB], dma="c0")
    P.op("sync", I("dma_start", out=lamrow[:], in_=lam_d), writes=[cB], dma="c1")
    P.begin_phase()
    lamps = P.ps([128, 512], F32)
    for e_ in range(NEVEN if "lam" not in _SKIP else 0):
        b0 = e_ * 256
        for j in range(2):
            P.op("vector", I("tensor_tensor",
                out=lamrow[0:1, b0 + j * 128:b0 + j * 128 + 64], in0=lamrow[0:1, b0 + j * 128:b0 + j * 128 + 64],
                in1=lamrow[0:1, b0 + j * 128 + 64:b0 + j * 128 + 128], op=ALU.mult), reads=[cB], writes=[cB])
            P.op("vector", I("reduce_sum",
                out=lamt[0:1, e_ * 4 + j:e_ * 4 + j + 1], in_=lamrow[0:1, b0 + j * 128:b0 + j * 128 + 64], axis=AX.X),
                reads=[cB], writes=[cB])
        P.op("scalar", I("activation", out=lamt[0:1, e_ * 4:e_ * 4 + 2], in_=lamt[0:1, e_ * 4:e_ * 4 + 2],
                                                     func=AF.Exp), reads=[cB], writes=[cB])
        lam_init = 0.8 - 0.6 * math.exp(-0.3 * (2 * e_))
        P.op("vector", I("tensor_scalar", out=lamt[0:1, e_ * 4 + 2:e_ * 4 + 3], in0=lamt[0:1, e_ * 4 + 1:e_ * 4 + 2], scalar1=-lam_init,
                         scalar2=1.0, op0=ALU.add, op1=ALU.mult), reads=[cB], writes=[cB])
        P.op("vector", I("tensor_tensor", out=lamt[0:1, e_ * 4 + 2:e_ * 4 + 3], in0=lamt[0:1, e_ * 4 + 2:e_ * 4 + 3],
                         in1=lamt[0:1, e_ * 4:e_ * 4 + 1], op=ALU.subtract), reads=[cB], writes=[cB])
        P.op("tensor", I("matmul", lamps[:, e_:e_ + 1], ones_f[0:1, :], lamt[0:1, e_ * 4 + 2:e_ * 4 + 3],
                                                 start=True, stop=True), reads=[cB], writes=[cB])
        P.op("vector", I("tensor_copy", out=neglam[:, e_:e_ + 1], in_=lamps[:, e_:e_ + 1]),
             reads=[cB], writes=[cB])
        c0 = PV_OFF["ev_subln_g"][0] + e_
        P.op("scalar", I("mul", gsub[:, e_:e_ + 1], pv[:, c0:c0 + 1], 1.0 - lam_init), reads=[cB], writes=[cB])

    def pvc(name, idx=0):
        c0, w = PV_OFF[name]
        return c0 + idx * w

    def load_w(dst, src, buf, eng="gpsimd"):
        P.op(eng, I("dma_start", out=dst, in_=src), writes=[buf], dma=P.dq())

    def norm_tile(xt, xb, n, gcol0, out, ob, R, sq_ring, ps_ring, kc_n=KC, dnorm=D, eps=EPS, ones=None):
        ones = ones_b if ones is None else ones
        sq, sqb = sq_ring.next()
        P.op("scalar", I("activation", out=sq[:, :kc_n, :n], in_=xt[:, :kc_n, :n], func=AF.Square),
             reads=[xb], writes=[sqb])
        ps, psb = ps_ring.next()
        for kc in range(kc_n):
            P.op("tensor", I("matmul", ps[:, :n], ones[:], sq[:, kc, :n], start=(kc == 0),
                                                     stop=(kc == kc_n - 1)), reads=[sqb, cB], writes=[psb])
        rstd, rb = R.next()
        P.op("vector", I("tensor_scalar", out=rstd[:, :n], in0=ps[:, :n], scalar1=1.0 / dnorm, scalar2=eps,
                                                 op0=ALU.mult, op1=ALU.add), reads=[psb], writes=[rb])
        P.op("scalar", I("activation", out=rstd[:, :n], in_=rstd[:, :n], func=AF.Sqrt), reads=[rb], writes=[rb])
        P.op("vector", I("reciprocal", out=rstd[:, :n], in_=rstd[:, :n]), reads=[rb], writes=[rb])
        for kc in range(kc_n):
            eng = "vector"
            P.op(eng, I("scalar_tensor_tensor",
                out=out[:, kc, :n], in0=xt[:, kc, :n], scalar=pv[:, gcol0 + kc:gcol0 + kc + 1], in1=rstd[:, :n],
                op0=ALU.mult, op1=ALU.mult), reads=[xb, rb, cB], writes=[ob])

    def linear(W, wb, h, hb, n, f0, nfo, ps_ring, epi, kc_n=KC):
        for fo in range(nfo):
            ps, psb = ps_ring.next()
            for kc in range(kc_n):
                P.op("tensor", I("matmul",
                    ps[:, :n], W[:, kc, f0 + fo * 128:f0 + (fo + 1) * 128], h[:, kc, :n], start=(kc == 0),
                    stop=(kc == kc_n - 1)), reads=[wb, hb], writes=[psb])
            epi(fo, ps, psb)

    def tiles_of(G, nmax):
        res = []
        for s in range(G.nseq):
            t0 = 0
            while t0 < G.T:
                n = min(nmax, G.T - t0)
                res.append((s, t0, n, s * G.T + t0))
                t0 += n
        return res

    cp = Ring(P, [128, KC, 512], F32, 2)
    for G in (groups if "xcopy" not in _SKIP else []):
        for (s, t0, n, c0) in tiles_of(G, 512):
            t, tb = cp.next()
            P.op("sync", I("dma_start", out=t[:, :, :n], in_=fm(xin[G.name])[:, :, c0:c0 + n]),
                 writes=[tb], dma=P.dq())
            P.op("sync", I("dma_start", out=fm(xT[G.name])[:, :, c0:c0 + n], in_=t[:, :, :n]),
                 reads=[tb], dma=P.dq())
    memT = P.sb([128, KC, NMEM], BF16)
    memB = Buf()
    load_w(memT[:], memT_d, memB)
    wkr = Ring(P, [128, KC, D], BF16, 2)
    psr = Ring(P, [128, 512], F32, 4, psum=True)
    ev32 = Ring(P, [128, 512], F32, 3)
    ev16 = Ring(P, [128, 512], BF16, 3)
    for l in range(depth if "memkv" not in _SKIP else 0):
        wk, wkb = wkr.next()
        load_w(wk[:], xa_w_d["wk"][l], wkb)
        wv, wvb = wkr.next()
        load_w(wv[:], xa_w_d["wv"][l], wvb)

        def epi_k(fo, ps, psb, l=l):
            a, ab = ev32.next()
            b, bb = ev16.next()
            if "e1" not in _SKIP:
                P.op("scalar", I("copy", out=a[:, :NMEM], in_=ps[:, :NMEM]), reads=[psb], writes=[ab])
            if "e2" not in _SKIP:
                P.op("gpsimd", I("tensor_copy", out=b[:, :NMEM], in_=a[:, :NMEM]), reads=[ab], writes=[bb])
            if "e3" not in _SKIP:
                P.op("sync", I("dma_start", out=memk_o[l, :, fo, :], in_=a[:, :NMEM]), reads=[ab], dma=P.dq())
            if "e4" not in _SKIP:
                P.op("sync", I("dma_start", out=mkT_s[l, :, fo, :], in_=b[:, :NMEM]), reads=[bb], dma=P.dq())
        if "mk" not in _SKIP:
            linear(wk, wkb, memT, memB, NMEM, 0, KC, psr, epi_k)
        for kb in range(NMEM // 128 if "mvv" not in _SKIP else 0):
            for hf in range(2):
                ps, psb = psr.next()
                for kc in range(KC):
                    P.op("tensor", I("matmul",
                        ps[:, :], memT[:, kc, kb * 128:(kb + 1) * 128], wv[:, kc, hf * 512:(hf + 1) * 512],
                        start=(kc == 0), stop=(kc == KC - 1)), reads=[wvb, memB], writes=[psb])
                a, ab = ev32.next()
                b, bb = ev16.next()
                P.op("scalar", I("copy", out=a[:], in_=ps[:]), reads=[psb], writes=[ab])
                P.op("gpsimd", I("tensor_copy", out=b[:], in_=a[:]), reads=[ab], writes=[bb])
                P.op("sync", I("dma_start",
                    out=memv_o[l, kb * 128:(kb + 1) * 128, hf * 512:(hf + 1) * 512], in_=a[:]), reads=[ab], dma=P.dq())
                P.op("sync", I("dma_start",
                    out=mv_s[l, kb * 128:(kb + 1) * 128, hf * 512:(hf + 1) * 512], in_=b[:]), reads=[bb], dma=P.dq())
    P.end_phase()

    def phase_even_proj(l):
        e_ = l // 2
        P.begin_phase()
        w_in = P.sb([128, KC, 2048], BF16)
        wb = Buf()
        load_w(w_in[:], w_in_d[e_], wb)
        pw = P.sb([128, 4, 128], BF16)
        pwb = Buf()
        load_w(pw[:], pool_w_d[e_], pwb)
        xr = Ring(P, [128, KC, 512], F32, 2)
        hr = Ring(P, [128, KC, 512], BF16, 2)
        sqr = Ring(P, [128, KC, 512], BF16, 1)
        rr = Ring(P, [128, 512], F32, 2)
        psr = Ring(P, [128, 512], F32, 6, psum=True)
        ev32 = Ring(P, [128, 512], F32, 4)
        ev16 = Ring(P, [128, 512], BF16, 4)
        ubuf = P.sb([128, 4, 15 + 512], F32)
        ub = Buf()
        ta = P.sb([128, 15 + 512], F32)
        tb2 = P.sb([128, 15 + 512], F32)
        tB = Buf()
        for G in groups:
            for (s, t0, n, c0) in tiles_of(G, 512):
                if t0 == 0:
                    if G.name == "p":
                        P.op("gpsimd", I("memset", ubuf[:, :, 0:15], 0.0), writes=[ub])
                    else:
                        P.op("sync", I("dma_start", out=ubuf[:, :, 0:15], in_=spool_d[e_, s]), writes=[ub],
                             dma=P.dq())
                xt, xb = xr.next()
                P.op("sync", I("dma_start", out=xt[:, :, :n], in_=fm(xT[G.name])[:, :, c0:c0 + n]),
                     writes=[xb], dma=P.dq())
                h, hb = hr.next()
                norm_tile(xt, xb, n, pvc("norm_mix_g", l), h, hb, rr, sqr, psr)

                def epi_u(fo, ps, psb):
                    P.op("scalar", I("copy", out=ubuf[:, fo, 15:15 + n], in_=ps[:, :n]), reads=[psb], writes=[ub])
                linear(w_in, wb, h, hb, n, 0, 4, psr, epi_u)

                def epi_q(fo, ps, psb, G=G, c0=c0):
                    b, bb = ev16.next()
                    P.op("vector", I("tensor_copy", out=b[:, :n], in_=ps[:, :n]), reads=[psb], writes=[bb])
                    P.op("sync", I("dma_start", out=qT[G.name][fo * 128:(fo + 1) * 128, c0:c0 + n], in_=b[:, :n]),
                         reads=[bb], dma=P.dq())
                linear(w_in, wb, h, hb, n, 512, 4, psr, epi_q)

                def epi_k(fo, ps, psb, G=G, c0=c0):
                    a, ab = ev32.next()
                    b, bb = ev16.next()
                    P.op("scalar", I("copy", out=a[:, :n], in_=ps[:, :n]), reads=[psb], writes=[ab])
                    P.op("gpsimd", I("tensor_copy", out=b[:, :n], in_=a[:, :n]), reads=[ab], writes=[bb])
                    P.op("sync", I("dma_start", out=kT_o[G.name][e_, fo * 128:(fo + 1) * 128, c0:c0 + n], in_=a[:, :n]),
                         reads=[ab], dma=P.dq())
                    P.op("sync", I("dma_start", out=kTs[G.name][fo * 128:(fo + 1) * 128, c0:c0 + n], in_=b[:, :n]),
                         reads=[bb], dma=P.dq())
                linear(w_in, wb, h, hb, n, 1024, 4, psr, epi_k)
                for j in range((n + 127) // 128):
                    m = min(128, n - j * 128)
                    ps, psb = psr.next()
                    for kc in range(KC):
                        P.op("tensor", I("matmul",
                            ps[:m, :], h[:, kc, j * 128:j * 128 + m], w_in[:, kc, 1536:2048], start=(kc == 0),
                            stop=(kc == KC - 1)), reads=[wb, hb], writes=[psb])
                    a, ab = ev32.next()
                    b, bb = ev16.next()
                    P.op("scalar", I("copy", out=a[:m, :], in_=ps[:m, :]), reads=[psb], writes=[ab])
                    P.op("gpsimd", I("tensor_copy", out=b[:m, :], in_=a[:m, :]), reads=[ab], writes=[bb])
                    r0 = c0 + j * 128
                    P.op("sync", I("dma_start", out=v_o[G.name][e_, r0:r0 + m, :], in_=a[:m, :]),
                         reads=[ab], dma=P.dq())
                    P.op("sync", I("dma_start", out=vtm[G.name][r0:r0 + m, :], in_=b[:m, :]),
                         reads=[bb], dma=P.dq())
                L = 15 + n
                for g in range(4):
                    w = 2 << g
                    src = ubuf[:, g, :]
                    cur = None
                    sh = 1
                    for st in range(g + 1):
                        dst = ta if st % 2 == 0 else tb2
                        s_ap = src if cur is None else cur
                        lo = 2 * sh - 1
                        P.op("vector", I("tensor_tensor",
                            out=dst[:, lo:L], in0=s_ap[:, lo:L], in1=s_ap[:, lo - sh:L - sh], op=ALU.add),
                            reads=[ub, tB], writes=[tB])
                        cur = dst
                        sh *= 2
                    pl, plb = ev32.next()
                    P.op("vector", I("tensor_scalar", out=pl[:, :n], in0=cur[:, 15:15 + n], scalar1=1.0 / w, scalar2=0.0, op0=ALU.mult, op1=ALU.add),
                         reads=[tB], writes=[plb])
                    P.op("vector", I("tensor_tensor", out=pl[:, :n], in0=pl[:, :n], in1=ubuf[:, g, 15:15 + n], op=ALU.subtract),
                         reads=[ub, plb], writes=[plb])
                    if G.name == "p" and t0 == 0:
                        P.op("vector", I("tensor_tensor",
                            out=pl[:, 0:w - 1], in0=cur[:, 15:15 + w - 1], in1=invcnt[:, 0:w - 1], op=ALU.mult),
                            reads=[tB, cB, plb], writes=[plb])
                        P.op("vector", I("tensor_tensor",
                            out=pl[:, 0:w - 1], in0=pl[:, 0:w - 1], in1=ubuf[:, g, 15:15 + w - 1], op=ALU.subtract),
                            reads=[ub, plb], writes=[plb])
                    pb_, pbb = ev16.next()
                    P.op("gpsimd", I("tensor_copy", out=pb_[:, :n], in_=pl[:, :n]), reads=[plb], writes=[pbb])
                    ps, psb = psr.next()
                    P.op("tensor", I("matmul", ps[:, :n], pw[:, g, :], pb_[:, :n], start=True, stop=True),
                         reads=[pwb, pbb], writes=[psb])
                    ob_, obb = ev16.next()
                    sc = pvc("ev_pool_scale", e_) + g
                    P.op("scalar", I("mul", ob_[:, :n], ps[:, :n], pv[:, sc:sc + 1]),
                         reads=[psb, cB], writes=[obb])
                    P.op("sync", I("dma_start",
                        out=mixT[G.name][g * 128:(g + 1) * 128, c0:c0 + n], in_=ob_[:, :n]), reads=[obb], dma=P.dq())
                if t0 + n == G.T:
                    P.op("sync", I("dma_start", out=pool_o[G.name][e_, s], in_=ubuf[:, :, n:n + 15]),
                         reads=[ub], dma=P.dq())
                else:
                    P.op("vector", I("tensor_copy", out=ubuf[:, :, 0:15], in_=ubuf[:, :, n:n + 15]), reads=[ub], writes=[ub])
        P.end_phase()

    def phase_even_attn(l):
        e_ = l // 2
        scale = 64 ** -0.5
        P.begin_phase()
        psS = Ring(P, [128, 512], F32, 3, psum=True)
        psO = [Ring(P, [128, 512], F32, 1, psum=True) for _ in range(2)]
        psD = [Ring(P, [128, 512], F32, 1, psum=True) for _ in range(2)]
        ptr = Ring(P, [128, 512], BF16, 6)
        tmp = Ring(P, [128, 512], F32, 6)
        sqr = Ring(P, [128, 1, 512], BF16, 2)
        o16 = Ring(P, [128, 1, 512], BF16, 2)

        def finish(o_, ob, d_, db, n, dst_rows, G, c0):
            a = []
            for m in range(2):
                r, rb = tmp.next()
                P.op("vector", I("reciprocal", out=r[:, :n], in_=d_[m][:, :n]), reads=[db[m]], writes=[rb])
                t, tb = tmp.next()
                P.op("vector", I("tensor_tensor", out=t[:, :n], in0=o_[m][:, :n], in1=r[:, :n], op=ALU.mult),
                     reads=[ob[m], rb], writes=[tb])
                a.append((t, tb))
            av, avb = tmp.next()
            P.op("vector", I("scalar_tensor_tensor", out=av[:, :n], in0=a[1][0][:, :n], scalar=neglam[:, e_:e_ + 1],
                                                             in1=a[0][0][:, :n], op0=ALU.mult, op1=ALU.add),
                 reads=[a[0][1], a[1][1], cB], writes=[avb])
            out, outb = o16.next()
            sq, sqb = sqr.next()
            P.op("scalar", I("activation", out=sq[:, 0, :n], in_=av[:, :n], func=AF.Square), reads=[avb], writes=[sqb])
            ps, psb = psS.next()
            P.op("tensor", I("matmul", ps[:, :n], ones_b[:], sq[:, 0, :n], start=True, stop=True), reads=[sqb, cB], writes=[psb])
            rstd, rb = tmp.next()
            P.op("vector", I("tensor_scalar", out=rstd[:, :n], in0=ps[:, :n], scalar1=1.0 / 128, scalar2=EPS,
                                                     op0=ALU.mult, op1=ALU.add), reads=[psb], writes=[rb])
            P.op("scalar", I("activation", out=rstd[:, :n], in_=rstd[:, :n], func=AF.Sqrt), reads=[rb], writes=[rb])
            P.op("vector", I("reciprocal", out=rstd[:, :n], in_=rstd[:, :n]), reads=[rb], writes=[rb])
            P.op("vector", I("scalar_tensor_tensor", out=out[:, 0, :n], in0=av[:, :n], scalar=gsub[:, e_:e_ + 1],
                                                             in1=rstd[:, :n], op0=ALU.mult, op1=ALU.mult),
                 reads=[avb, rb, cB], writes=[outb])
            P.op("sync", I("dma_start", out=mixT[G.name][dst_rows:dst_rows + 128, c0:c0 + n], in_=out[:, 0, :n]),
                 reads=[outb], dma=P.dq())

        G = GP
        T = G.T
        kh_r = Ring(P, [128, T], BF16, 2)
        vh_r = Ring(P, [128, max(T // 128, 1), 128], BF16, 2)
        qt_r = Ring(P, [128, 512], BF16, 2)
        for hd in range(4):
            kh, khb = kh_r.next()
            P.op("sync", I("dma_start", out=kh[:, :], in_=kTs["p"][hd * 128:(hd + 1) * 128, :]), writes=[khb], dma=P.dq())
            vh, vhb = vh_r.next()
            P.op("sync", I("dma_start",
                out=vh[:, :, :], in_=vtm["p"][:, hd * 128:(hd + 1) * 128].rearrange("(j p) e -> p j e", p=128)), writes=[vhb], dma=P.dq())
            for (s, t0, n, c0) in tiles_of(G, 512):
                qt, qtb = qt_r.next()
                P.op("sync", I("dma_start", out=qt[:, :n], in_=qT["p"][hd * 128:(hd + 1) * 128, c0:c0 + n]),
                     writes=[qtb], dma=P.dq())
                o_ = [psO[m].next() for m in range(2)]
                d_ = [psD[m].next() for m in range(2)]
                nkb = (t0 + n) // 128

                def pv_block(items):
                    for (pt, ptb, m, jb, q0, diag, first, last) in items:
                        P.op("tensor", I("matmul", o_[m][0][:, q0:n], vh[:, jb, :], pt[:, q0:n], start=first, stop=last),
                             reads=[vhb, ptb], writes=[o_[m][1]])
                        P.op("tensor", I("matmul", d_[m][0][:, q0:n], ones_b[:], pt[:, q0:n], start=first, stop=last),
                             reads=[cB, ptb], writes=[d_[m][1]])

                pending = None
                for j in range(nkb):
                    jj = j - t0 // 128
                    q0 = 0 if jj < 0 else 128 * jj
                    first = (j == 0)
                    last = (j == nkb - 1)
                    items = []
                    for m in range(2):
                        ps, psb = psS.next()
                        P.op("tensor", I("matmul", ps[:, q0:n], kh[64 * m:64 * m + 64, j * 128:(j + 1) * 128], qt[64 * m:64 * m + 64, q0:n],
                                         start=True, stop=True), reads=[khb, qtb], writes=[psb])
                        pt, ptb = ptr.next()
                        P.op("scalar", I("activation", out=pt[:, q0:n], in_=ps[:, q0:n], func=AF.Exp, scale=scale),
                             reads=[psb], writes=[ptb])
                        if jj >= 0:
                            P.op("gpsimd", I("memset", pt[64:128, q0:q0 + 64], 0.0), reads=[ptb], writes=[ptb])
                        items.append((pt, ptb, m, j, q0, jj >= 0, first, last))
                    if pending is not None:
                        pv_block(pending)
                    pending = items
                pv_block(pending)
                finish([o_[0][0], o_[1][0]], [o_[0][1], o_[1][1]], [d_[0][0], d_[1][0]], [d_[0][1], d_[1][1]], n,
                       512 + hd * 128, G, c0)
        G = GS
        TSq = G.T
        kc_r = Ring(P, [128, PAST], BF16, 2)
        vc_r = Ring(P, [128, NPB, 128], BF16, 2)
        kn_r = Ring(P, [128, 16], BF16, 2)
        vn_r = Ring(P, [16, 128], BF16, 2)
        pts_r = Ring(P, [128, 2, NPB, 16], BF16, 2)
        ptn_r = Ring(P, [16, 2, 16], BF16, 2)
        psQ = Ring(P, [128, 512], F32, 1, psum=True)
        for s in range(G.nseq):
            c0 = s * TSq
            for hd in range(4):
                kc, kcb = kc_r.next()
                P.op("gpsimd", I("dma_start", out=kc[:, :], in_=ckT_d[e_, s, hd]), writes=[kcb], dma=P.dq())
                vc, vcb = vc_r.next()
                P.op("gpsimd", I("dma_start",
                    out=vc[:, :, :], in_=cv_d[e_, s, :, hd, :].rearrange("(j p) e -> p j e", p=128)), writes=[vcb], dma=P.dq())
                kn, knb = kn_r.next()
                P.op("sync", I("dma_start", out=kn[:, :], in_=kTs["s"][hd * 128:(hd + 1) * 128, c0:c0 + TSq]),
                     writes=[knb], dma=P.dq())
                vn, vnb = vn_r.next()
                P.op("sync", I("dma_start", out=vn[:, :], in_=vtm["s"][c0:c0 + TSq, hd * 128:(hd + 1) * 128]),
                     writes=[vnb], dma=P.dq())
                qt, qtb = qt_r.next()
                P.op("sync", I("dma_start", out=qt[:, :TSq], in_=qT["s"][hd * 128:(hd + 1) * 128, c0:c0 + TSq]),
                     writes=[qtb], dma=P.dq())
                pts, ptsb = pts_r.next()
                ptn, ptnb = ptn_r.next()
                psn, psnb = psQ.next()
                for m in range(2):
                    ps, psb = psS.next()
                    for j in range(NPB):
                        P.op("tensor", I("matmul",
                            ps[:, j * 16:(j + 1) * 16], kc[64 * m:64 * m + 64, j * 128:(j + 1) * 128], qt[64 * m:64 * m + 64, :TSq],
                            start=True, stop=True), reads=[kcb, qtb], writes=[psb])
                    P.op("scalar", I("activation",
                        out=pts[:, m, :, :], in_=ps[:, :NPB * 16].rearrange("p (j q) -> p j q", q=16), func=AF.Exp, scale=scale),
                        reads=[psb], writes=[ptsb])
                    P.op("tensor", I("matmul", psn[:16, m * 16:(m + 1) * 16], kn[64 * m:64 * m + 64, :], qt[64 * m:64 * m + 64, :TSq],
                                                           start=True, stop=True), reads=[knb, qtb], writes=[psnb])
                P.op("scalar", I("activation", out=ptn[:, :, :], in_=psn[:16, 0:32].rearrange("p (m q) -> p m q", q=16),
                                                      func=AF.Exp, scale=scale), reads=[psnb], writes=[ptnb])
                o_ = [psO[m].next() for m in range(2)]
                d_ = [psD[m].next() for m in range(2)]
                for m in range(2):
                    for j in range(NPB):
                        P.op("tensor", I("matmul", o_[m][0][:, :TSq], vc[:, j, :], pts[:, m, j, :], start=(j == 0), stop=False),
                             reads=[vcb, ptsb], writes=[o_[m][1]])
                    P.op("tensor", I("matmul", o_[m][0][:, :TSq], vn[:, :], ptn[:, m, :], start=False, stop=True),
                         reads=[vnb, ptnb], writes=[o_[m][1]])
                    for j in range(NPB):
                        P.op("tensor", I("matmul", d_[m][0][:, :TSq], ones_b[:], pts[:, m, j, :], start=(j == 0), stop=False),
                             reads=[cB, ptsb], writes=[d_[m][1]])
                    P.op("tensor", I("matmul", d_[m][0][:, :TSq], ones_b[0:16, :], ptn[:, m, :], start=False, stop=True),
                         reads=[cB, ptnb], writes=[d_[m][1]])
                finish([o_[0][0], o_[1][0]], [o_[0][1], o_[1][1]], [d_[0][0], d_[1][0]], [d_[0][1], d_[1][1]], TSq,
                       512 + hd * 128, G, c0)
        P.end_phase()

    def phase_out_xa(l, wout_src):
        P.begin_phase()
        wo_m = P.sb([128, KC, D], BF16)
        wq = P.sb([128, KC, D], BF16)
        wo = P.sb([128, KC, D], BF16)
        wB = [Buf(), Buf(), Buf()]
        load_w(wo_m[:], wout_src, wB[0])
        load_w(wq[:], xa_w_d["wq"][l], wB[1])
        load_w(wo[:], xa_w_d["wo"][l], wB[2])
        mk_r = Ring(P, [128, KC, NMEM], BF16, 2)
        mv_r = Ring(P, [128, NMEM // 128, D], BF16, 2)
        xr = Ring(P, [128, KC, 512], F32, 2)
        mr = Ring(P, [128, KC, 512], BF16, 2)
        hr = Ring(P, [128, KC, 512], BF16, 1)
        qr = Ring(P, [128, KC, 512], BF16, 1)
        otr = Ring(P, [128, KC, 512], BF16, 1)
        sqr = Ring(P, [128, KC, 512], BF16, 1)
        rr = Ring(P, [128, 512], F32, 2)
        ptr = Ring(P, [128, NMEM // 128, 512], BF16, 2)
        psr = Ring(P, [128, 512], F32, 4, psum=True)
        pso = Ring(P, [128, 512], F32, 3, psum=True)
        NKB = NMEM // 128
        for G in groups:
            nmax = 512 if G.name == "p" else G.T
            mk = mv = None
            for (s, t0, n, c0) in tiles_of(G, nmax):
                if G.name == "p":
                    if t0 == 0:
                        mk, mkb = mk_r.next()
                        P.op("sync", I("dma_start", out=mk[:], in_=mkT_s[l]), writes=[mkb], dma=P.dq())
                        mv, mvb = mv_r.next()
                        P.op("sync", I("dma_start", out=mv[:], in_=mv_s[l].rearrange("(j p) d -> p j d", p=128)),
                             writes=[mvb], dma=P.dq())
                else:
                    mk, mkb = mk_r.next()
                    P.op("gpsimd", I("dma_start", out=mk[:], in_=cmkT_d[l, s]), writes=[mkb], dma=P.dq())
                    mv, mvb = mv_r.next()
                    P.op("gpsimd", I("dma_start", out=mv[:], in_=cmv_d[l, s].rearrange("(j p) d -> p j d", p=128)),
                         writes=[mvb], dma=P.dq())
                xt, xb = xr.next()
                P.op("sync", I("dma_start", out=xt[:, :, :n], in_=fm(xT[G.name])[:, :, c0:c0 + n]),
                     writes=[xb], dma=P.dq())
                mt, mb = mr.next()
                P.op("sync", I("dma_start", out=mt[:, :, :n], in_=fm(mixT[G.name])[:, :, c0:c0 + n]),
                     writes=[mb], dma=P.dq())

                def epi_add(fo, ps, psb, xt=xt, xb=xb, n=n):
                    P.op("vector", I("tensor_tensor", out=xt[:, fo, :n], in0=ps[:, :n], in1=xt[:, fo, :n], op=ALU.add),
                         reads=[psb, xb], writes=[xb])
                linear(wo_m, wB[0], mt, mb, n, 0, KC, psr, epi_add)
                h, hb = hr.next()
                norm_tile(xt, xb, n, pvc("norm_xa_g", l), h, hb, rr, sqr, psr)
                q, qb = qr.next()

                def epi_q(fo, ps, psb, q=q, qb=qb, n=n):
                    P.op("scalar", I("copy", out=q[:, fo, :n], in_=ps[:, :n]), reads=[psb], writes=[qb])
                linear(wq, wB[1], h, hb, n, 0, KC, psr, epi_q)
                ot, otb = otr.next()
                for hh in range(4):
                    pt, ptb = ptr.next()
                    for kb in range(NKB):
                        ps, psb = psr.next()
                        for dc in range(2):
                            P.op("tensor", I("matmul",
                                ps[:, :n], mk[:, hh * 2 + dc, kb * 128:(kb + 1) * 128], q[:, hh * 2 + dc, :n], start=(dc == 0), stop=(dc == 1)),
                                reads=[mkb, qb], writes=[psb])
                        P.op("scalar", I("activation", out=pt[:, kb, :n], in_=ps[:, :n], func=AF.Exp, scale=1.0 / 16),
                             reads=[psb], writes=[ptb])
                    dn, dnb = pso.next()
                    for kb in range(NKB):
                        P.op("tensor", I("matmul", dn[:, :n], ones_b[:], pt[:, kb, :n], start=(kb == 0), stop=(kb == NKB - 1)),
                             reads=[cB, ptb], writes=[dnb])
                    rd, rdb = rr.next()
                    P.op("vector", I("reciprocal", out=rd[:, :n], in_=dn[:, :n]), reads=[dnb], writes=[rdb])
                    for dc in range(2):
                        po, pob = pso.next()
                        for kb in range(NKB):
                            P.op("tensor", I("matmul",
                                po[:, :n], mv[:, kb, hh * 256 + dc * 128:hh * 256 + (dc + 1) * 128], pt[:, kb, :n], start=(kb == 0), stop=(kb == NKB - 1)),
                                reads=[mvb, ptb], writes=[pob])
                        P.op("vector", I("tensor_tensor",
                            out=ot[:, hh * 2 + dc, :n], in0=po[:, :n], in1=rd[:, :n], op=ALU.mult), reads=[pob, rdb], writes=[otb])
                linear(wo, wB[2], ot, otb, n, 0, KC, psr, epi_add)
                P.op("sync", I("dma_start", out=fm(xT[G.name])[:, :, c0:c0 + n], in_=xt[:, :, :n]),
                     reads=[xb], dma=P.dq())
        P.end_phase()

    def phase_ffn(l, final):
        P.begin_phase()
        NTF = 256
        wg = P.sb([128, KC, DFF], BF16)
        wu = P.sb([128, KC, DFF], BF16)
        wd = P.sb([128, FC, D], BF16)
        wB = [Buf(), Buf(), Buf()]
        load_w(wg[:], ffn_g_d[l], wB[0])
        load_w(wu[:], ffn_u_d[l], wB[1])
        load_w(wd[:], ffn_d_d[l], wB[2])
        xr = Ring(P, [128, KC, NTF], F32, 2)
        hr = Ring(P, [128, KC, NTF], BF16, 1)
        sqr = Ring(P, [128, KC, NTF], BF16, 1)
        ar = Ring(P, [128, FC, NTF], BF16, 1)
        rr = Ring(P, [128, NTF], F32, 2)
        sg = Ring(P, [128, NTF], F32, 3)
        yr = Ring(P, [128, KC, NTF], F32, 1)
        psr = Ring(P, [128, 512], F32, 6, psum=True)
        for G in groups:
            for (s, t0, n, c0) in tiles_of(Group(G.name, 1, G.NT), NTF):
                xt, xb = xr.next()
                P.op("sync", I("dma_start", out=xt[:, :, :n], in_=fm(xT[G.name])[:, :, c0:c0 + n]),
                     writes=[xb], dma=P.dq())
                h, hb = hr.next()
                norm_tile(xt, xb, n, pvc("norm_ffn_g", l), h, hb, rr, sqr, psr)
                act, ab = ar.next()
                for fo in range(FC):
                    pg, pgb = psr.next()
                    pu, pub = psr.next()
                    for kc in range(KC):
                        P.op("tensor", I("matmul", pg[:, :n], wg[:, kc, fo * 128:(fo + 1) * 128], h[:, kc, :n],
                                                                                 start=(kc == 0), stop=(kc == KC - 1)), reads=[wB[0], hb], writes=[pgb])
                    for kc in range(KC):
                        P.op("tensor", I("matmul", pu[:, :n], wu[:, kc, fo * 128:(fo + 1) * 128], h[:, kc, :n],
                                                                                 start=(kc == 0), stop=(kc == KC - 1)), reads=[wB[1], hb], writes=[pub])
                    s_, sb_ = sg.next()
                    P.op("scalar", I("activation", out=s_[:, :n], in_=pg[:, :n], func=AF.Silu), reads=[pgb], writes=[sb_])
                    P.op("vector", I("tensor_tensor", out=act[:, fo, :n], in0=pu[:, :n], in1=s_[:, :n], op=ALU.mult),
                         reads=[pub, sb_], writes=[ab])

                def epi_add(fo, ps, psb, xt=xt, xb=xb, n=n):
                    P.op("vector", I("tensor_tensor", out=xt[:, fo, :n], in0=ps[:, :n], in1=xt[:, fo, :n], op=ALU.add),
                         reads=[psb, xb], writes=[xb])
                linear(wd, wB[2], act, ab, n, 0, KC, psr, epi_add, kc_n=FC)
                if not final:
                    P.op("sync", I("dma_start", out=fm(xT[G.name])[:, :, c0:c0 + n], in_=xt[:, :, :n]),
                         reads=[xb], dma=P.dq())
                else:
                    y, yb = yr.next()
                    norm_tile(xt, xb, n, pvc("final_norm_g"), y, yb, rr, sqr, psr)
                    P.op("sync", I("dma_start", out=fm(yT_o[G.name])[:, :, c0:c0 + n], in_=y[:, :, :n]),
                         reads=[yb], dma=P.dq())
        P.end_phase()

    def phase_rwkv(l):
        o_ = l // 2
        P.begin_phase()
        W = {}
        WB = {}
        for k in ("wr", "wk", "wv"):
            W[k] = P.sb([128, KC, D], BF16)
            WB[k] = Buf()
            load_w(W[k][:], rw_w_d[k][o_], WB[k])
        for k, nn in (("w1", 64), ("a1", 64), ("g1", 160), ("v1", 32)):
            if k == "v1" and o_ == 0:
                continue
            W[k] = P.sb([128, KC, nn], BF16)
            WB[k] = Buf()
            load_w(W[k][:], rw_l1_d[k][o_ if k != "v1" else o_ - 1], WB[k])
        for k, nn in (("w2", 64), ("a2", 64), ("v2", 32)):
            if k == "v2" and o_ == 0:
                continue
            W[k] = P.sb([nn, 1, D], BF16)
            WB[k] = Buf()
            load_w(W[k][:, 0, :], rw_l2_d[k][o_ if k != "v2" else o_ - 1], WB[k])
        W["g2a"] = P.sb([128, 1, D], BF16)
        W["g2b"] = P.sb([32, 1, D], BF16)
        WB["g2a"] = Buf()
        WB["g2b"] = Buf()
        load_w(W["g2a"][:, 0, :], rw_l2_d["g2"][o_, 0:128, :], WB["g2a"])
        load_w(W["g2b"][:, 0, :], rw_l2_d["g2"][o_, 128:160, :], WB["g2b"])

        NB = 128
        f32t = lambda k=1: P.sb([128, KC, NB + k - 1], F32)
        xh = f32t(2); xhB = Buf()
        hx = f32t(2); hxB = Buf()
        xx = f32t(2); xxB = Buf()
        mixr = Ring(P, [128, KC, NB], BF16, 2)
        sqr = Ring(P, [128, KC, NB + 1], BF16, 1)
        rr = Ring(P, [128, NB + 1], F32, 2)
        psr = Ring(P, [128, 512], F32, 4, psum=True)
        psY2 = [P.ps([128, 4, 128], F32), P.ps([128, 4, 128], F32)]; psYB = Buf()
        psSt = P.ps([128, KC, 64], F32); psStB = Buf()
        r_ = f32t(); k_ = f32t(); v_ = f32t(); sg_ = f32t(); a_ = f32t(); g_ = f32t()
        rB, kB, vB, sgB, aB, gB = [Buf() for _ in range(6)]
        lh = {k: P.sb([128, 2, NB], BF16) for k in ("w", "a", "g", "v")}
        lhB = {k: Buf() for k in lh}
        kk = f32t(); kkB = Buf()
        km = f32t(); kmB = Buf()
        bb_ = f32t(); bbB = Buf()
        t1 = f32t(); t1B = Buf()
        t2 = f32t(); t2B = Buf()
        vf = f32t(); vfB = Buf()
        csA = xh; csB_ = xx
        Eout = f32t(); EoutB = Buf()
        PC = P.sb([128, KC, 2], F32); PCB = Buf()
        bon = f32t(); bonB = Buf()
        AR = P.sb([128, KC, 2, NB], BF16); ARB = Buf()
        Bt = P.sb([128, KC, NB], BF16); Kt = P.sb([128, KC, NB], BF16); BKB = Buf()
        R32 = f32t(); R32B = Buf()
        tmf = Ring(P, [128, KC, NB], F32, 2)
        Z = [P.sb([128, 16, 128], BF16) for _ in range(2)]
        ZB = [[Buf() for _ in range(16)] for _ in range(2)]
        Vtm = P.sb([128, KC, 128], BF16); VtmB = Buf()
        BPtm = P.sb([128, KC, 128], BF16); BPB = Buf()
        KPtm = P.sb([128, KC, 128], BF16); KPB = Buf()
        XAs = Ring(P, [128, 2, NB], BF16, 6)
        XBs = Ring(P, [128, 2, NB], BF16, 6)
        MLr = Ring(P, [128, 2, NB], BF16, 12)
        Ls = Ring(P, [128, NB], BF16, 6)
        v16 = lambda t: t[:, :, 0:NB].rearrange("p k (a v) -> p (k a) v", v=64)
        v42 = lambda t: t[:, :, 0:NB].rearrange("p k (c v) -> p k c v", v=64)
        PhiT = v42(sg_); PhiB = sgB
        Psi = v42(t1); PsiB = t1B
        Om = P.sb([128, KC, NB], BF16); OmB = Buf()
        S16 = P.sb([128, KC, 64], BF16); S16B = Buf()
        Y0 = bb_; Y0B = bbB
        S = [P.sb([128, KC, 64], F32) for _ in range(2)]
        SB_ = [Buf(), Buf()]
        Ytm = r_; YB = rB
        st1 = P.sb([128, 16], F32); st2 = P.sb([128, 16], F32); stB = Buf()
        yc = k_; ycB = kB
        ysq = a_; ysqB = aB
        yo = Ring(P, [128, KC, NB], BF16, 2)
        yt32 = kk; yt32B = kkB

        mu0 = pvc("rw_mu", o_ * 6)
        lgc = pvc("rw_lnx_g", o_)
        lbc = pvc("rw_lnx_b", o_)

        def small_lin(Wt, wb, src, sb, nrows, n, ps, psb, f0, start=True, stop=True):
            P.op("tensor", I("matmul", ps[:, :n], Wt[:nrows, 0, f0:f0 + 128], src[:nrows, :n], start=start, stop=stop),
                 reads=[wb, sb], writes=[psb])

        cur = 0
        for G in groups:
            C = 64 if G.name == "p" else G.T
            for (s, t0, n, c0) in tiles_of(G, NB):
                nch = n // C
                nlog = int(math.log2(C))
                first = (t0 == 0)
                if first:
                    P.op("sync", I("dma_start", out=xh[:, :, 1:n + 1], in_=fm(xT[G.name])[:, :, c0:c0 + n]),
                         writes=[xhB], dma=P.dq())
                    P.op("gpsimd", I("memset", xh[:, :, 0:1], 1.0), writes=[xhB])
                else:
                    P.op("sync", I("dma_start", out=xh[:, :, 0:n + 1], in_=fm(xT[G.name])[:, :, c0 - 1:c0 + n]),
                         writes=[xhB], dma=P.dq())
                norm_tile(xh, xhB, n + 1, pvc("norm_mix_g", l), hx, hxB, rr, sqr, psr)
                if first:
                    if G.name == "p":
                        P.op("gpsimd", I("memset", hx[:, :, 0:1], 0.0), reads=[hxB], writes=[hxB])
                    else:
                        P.op("sync", I("dma_start", out=hx[:, :, 0:1], in_=sshift_d[o_, s].unsqueeze(2), allow_slow_non_contiguous=True), reads=[hxB], writes=[hxB], dma=P.dq())
                if t0 + n == G.T:
                    P.op("sync", I("dma_start", out=shift_o[G.name][o_, s].unsqueeze(2), in_=hx[:, :, n:n + 1], allow_slow_non_contiguous=True), reads=[hxB], dma=P.dq())
                P.op("vector", I("tensor_tensor", out=xx[:, :, :n], in0=hx[:, :, 0:n], in1=hx[:, :, 1:n + 1], op=ALU.subtract),
                     reads=[hxB], writes=[xxB])

                def mix(i, n=n):
                    m, mb = mixr.next()
                    for kc in range(KC):
                        eng = "vector"
                        P.op(eng, I("scalar_tensor_tensor",
                            out=m[:, kc, :n], in0=xx[:, kc, :n], scalar=pv[:, mu0 + i * KC + kc:mu0 + i * KC + kc + 1],
                            in1=hx[:, kc, 1:n + 1], op0=ALU.mult, op1=ALU.add), reads=[xxB, hxB, cB], writes=[mb])
                    return m, mb

                def epi_copy(dst, dB, n=n):
                    def f(fo, ps, psb):
                        P.op("scalar", I("copy", out=dst[:, fo, :n], in_=ps[:, :n]), reads=[psb], writes=[dB])
                    return f

                def lora_hidden(m, mb, key, nn, func, n=n):
                    for ci, (r0, rn) in enumerate([(0, min(nn, 128))] + ([(128, nn - 128)] if nn > 128 else [])):
                        ps, psb = psr.next()
                        for kc in range(KC):
                            P.op("tensor", I("matmul", ps[:rn, :n], W[key][:, kc, r0:r0 + rn], m[:, kc, :n],
                                                                                   start=(kc == 0), stop=(kc == KC - 1)), reads=[WB[key], mb], writes=[psb])
                        P.op("scalar", I("activation", out=lh[key[0]][:rn, ci, :n], in_=ps[:rn, :n], func=func),
                             reads=[psb], writes=[lhB[key[0]]])

                m, mb = mix(0)
                linear(W["wr"], WB["wr"], m, mb, n, 0, KC, psr, epi_copy(r_, rB))
                m, mb = mix(1)
                lora_hidden(m, mb, "w1", 64, AF.Tanh)
                w0c = pvc("rw_w0", o_)
                for fo in range(KC):
                    ps, psb = psr.next()
                    small_lin(W["w2"], WB["w2"], lh["w"][:, 0, :], lhB["w"], 64, n, ps, psb, fo * 128)
                    P.op("scalar", I("activation", out=sg_[:, fo, :n], in_=ps[:, :n], func=AF.Sigmoid,
                                                                        bias=pv[:, w0c + fo:w0c + fo + 1]), reads=[psb, cB], writes=[sgB])
                m, mb = mix(2)
                linear(W["wk"], WB["wk"], m, mb, n, 0, KC, psr, epi_copy(k_, kB))
                m, mb = mix(3)
                linear(W["wv"], WB["wv"], m, mb, n, 0, KC, psr, epi_copy(v_, vB))
                if o_ == 0:
                    P.op("sync", I("dma_start", out=fm(vfirst[G.name])[:, :, c0:c0 + n], in_=v_[:, :, :n]), reads=[vB], dma=P.dq())
                else:
                    P.op("sync", I("dma_start", out=vf[:, :, :n], in_=fm(vfirst[G.name])[:, :, c0:c0 + n]), writes=[vfB], dma=P.dq())
                    lora_hidden(m, mb, "v1", 32, AF.Copy)
                    v0c = pvc("rw_v0", 0)
                    for fo in range(KC):
                        ps, psb = psr.next()
                        small_lin(W["v2"], WB["v2"], lh["v"][:, 0, :], lhB["v"], 32, n, ps, psb, fo * 128)
                        P.op("scalar", I("activation", out=t1[:, fo, :n], in_=ps[:, :n], func=AF.Sigmoid,
                                                                            bias=pv[:, v0c + fo:v0c + fo + 1]), reads=[psb, cB], writes=[t1B])
                    P.op("vector", I("tensor_tensor", out=vf[:, :, :n], in0=vf[:, :, :n], in1=v_[:, :, :n], op=ALU.subtract),
                         reads=[vfB, vB], writes=[vfB])
                    P.op("vector", I("tensor_tensor", out=vf[:, :, :n], in0=vf[:, :, :n], in1=t1[:, :, :n], op=ALU.mult),
                         reads=[vfB, t1B], writes=[vfB])
                    P.op("vector", I("tensor_tensor", out=v_[:, :, :n], in0=v_[:, :, :n], in1=vf[:, :, :n], op=ALU.add),
                         reads=[vfB, vB], writes=[vB])
                m, mb = mix(4)
                lora_hidden(m, mb, "a1", 64, AF.Copy)
                a0c = pvc("rw_a0", o_)
                for fo in range(KC):
                    ps, psb = psr.next()
                    small_lin(W["a2"], WB["a2"], lh["a"][:, 0, :], lhB["a"], 64, n, ps, psb, fo * 128)
                    P.op("scalar", I("activation", out=a_[:, fo, :n], in_=ps[:, :n], func=AF.Sigmoid,
                                                                        bias=pv[:, a0c + fo:a0c + fo + 1]), reads=[psb, cB], writes=[aB])
                m, mb = mix(5)
                lora_hidden(m, mb, "g1", 160, AF.Sigmoid)
                for fo in range(KC):
                    ps, psb = psr.next()
                    small_lin(W["g2a"], WB["g2a"], lh["g"][:, 0, :], lhB["g"], 128, n, ps, psb, fo * 128, True, False)
                    small_lin(W["g2b"], WB["g2b"], lh["g"][:, 1, :], lhB["g"], 32, n, ps, psb, fo * 128, False, True)
                    P.op("scalar", I("copy", out=g_[:, fo, :n], in_=ps[:, :n]), reads=[psb], writes=[gB])
                if "r1" in _SKIP:
                    continue
                kkc = pvc("rw_k_k", o_)
                kac = pvc("rw_k_a", o_)
                rkc = pvc("rw_r_k", o_)
                sqb_, sqbB = sqr.next()
                for kc in range(KC):
                    P.op("scalar", I("mul", kk[:, kc, :n], k_[:, kc, :n], pv[:, kkc + kc:kkc + kc + 1]),
                         reads=[kB, cB], writes=[kkB])
                P.op("scalar", I("activation", out=sqb_[:, :, :n], in_=kk[:, :, :n], func=AF.Square), reads=[kkB], writes=[sqbB])
                for kc in range(KC):
                    ps, psb = psr.next()
                    P.op("tensor", I("matmul", ps[:, :n], bones_b[:], sqb_[:, kc, :n], start=True, stop=True),
                         reads=[cB, sqbB], writes=[psb])
                    P.op("scalar", I("activation", out=t1[:, kc, :n], in_=ps[:, :n], func=AF.Sqrt), reads=[psb], writes=[t1B])
                P.op("vector", I("tensor_scalar", out=t1[:, :, :n], in0=t1[:, :, :n], scalar1=1e-12, scalar2=1.0, op0=ALU.max, op1=ALU.mult),
                     reads=[t1B], writes=[t1B])
                P.op("vector", I("reciprocal", out=t1[:, :, :n], in_=t1[:, :, :n]), reads=[t1B], writes=[t1B])
                P.op("vector", I("tensor_tensor", out=kk[:, :, :n], in0=kk[:, :, :n], in1=t1[:, :, :n], op=ALU.mult),
                     reads=[kkB, t1B], writes=[kkB])
                P.op("vector", I("tensor_scalar", out=t2[:, :, :n], in0=a_[:, :, :n], scalar1=-1.0, scalar2=1.0, op0=ALU.add, op1=ALU.mult),
                     reads=[aB], writes=[t2B])
                for kc in range(KC):
                    P.op("scalar", I("mul", t2[:, kc, :n], t2[:, kc, :n], pv[:, kac + kc:kac + kc + 1]), reads=[t2B, cB], writes=[t2B])
                P.op("vector", I("scalar_tensor_tensor", out=km[:, :, :n], in0=t2[:, :, :n], scalar=1.0, in1=k_[:, :, :n],
                                                                 op0=ALU.add, op1=ALU.mult), reads=[t2B, kB], writes=[kmB])
                P.op("gpsimd", I("tensor_tensor", out=bb_[:, :, :n], in0=kk[:, :, :n], in1=a_[:, :, :n], op=ALU.mult),
                     reads=[kkB, aB], writes=[bbB])
                for kc in range(KC):
                    P.op("vector", I("scalar_tensor_tensor", out=t2[:, kc, :n], in0=r_[:, kc, :n], scalar=pv[:, rkc + kc:rkc + kc + 1],
                                                                            in1=km[:, kc, :n], op0=ALU.mult, op1=ALU.mult), reads=[rB, kmB, cB, t2B], writes=[t2B])
                sqb2, sqb2B = sqr.next()
                P.op("scalar", I("copy", out=sqb2[:, :, :n], in_=t2[:, :, :n]), reads=[t2B], writes=[sqb2B])
                for kc in range(KC):
                    ps, psb = psr.next()
                    P.op("tensor", I("matmul", ps[:, :n], bones_b[:], sqb2[:, kc, :n], start=True, stop=True),
                         reads=[cB, sqb2B], writes=[psb])
                    P.op("vector", I("tensor_tensor", out=bon[:, kc, :n], in0=ps[:, :n], in1=v_[:, kc, :n], op=ALU.mult),
                         reads=[psb, vB], writes=[bonB])
                    P.op("scalar", I("activation", out=bon[:, kc, :n], in_=bon[:, kc, :n], func=AF.Identity, bias=pv[:, lbc + kc:lbc + kc + 1]),
                         reads=[bonB, cB], writes=[bonB])
                if "r2" in _SKIP:
                    continue
                def v4(t):
                    return t[:, :, :n].rearrange("p k (c t) -> p k c t", t=C)
                src, srcB = sg_, sgB
                sh = 1
                i = 0
                while sh < C:
                    dst, dstB = (csA, xhB) if i % 2 == 0 else (csB_, xxB)
                    P.op("vector", I("tensor_tensor", out=v4(dst)[:, :, :, sh:C], in0=v4(src)[:, :, :, sh:C], in1=v4(src)[:, :, :, 0:C - sh], op=ALU.add),
                         reads=[srcB], writes=[dstB])
                    P.op("scalar", I("copy", out=v4(dst)[:, :, :, 0:sh], in_=v4(src)[:, :, :, 0:sh]), reads=[srcB], writes=[dstB])
                    src, srcB = dst, dstB
                    sh *= 2
                    i += 1
                cs, csBuf = src, srcB
                Ein, EinB = hx, hxB
                Eprev, EprevB = t2, t2B
                Eend, EendB = vf, vfB
                P.op("scalar", I("activation", out=Ein[:, :, :n], in_=cs[:, :, :n], func=AF.Exp, scale=-CDEC), reads=[csBuf], writes=[EinB])
                P.op("scalar", I("activation", out=Eout[:, :, :n], in_=cs[:, :, :n], func=AF.Exp, scale=CDEC), reads=[csBuf], writes=[EoutB])
                P.op("vector", I("tensor_tensor", out=t1[:, :, :n], in0=cs[:, :, :n], in1=sg_[:, :, :n], op=ALU.subtract),
                     reads=[csBuf, sgB], writes=[t1B])
                P.op("scalar", I("activation", out=Eprev[:, :, :n], in_=t1[:, :, :n], func=AF.Exp, scale=-CDEC), reads=[t1B], writes=[EprevB])
                P.op("vector", I("tensor_tensor", out=v4(t1), in0=v4(cs), in1=v4(cs)[:, :, :, C - 1:C].to_broadcast([128, KC, nch, C]),
                                 op=ALU.subtract), reads=[csBuf], writes=[t1B])
                P.op("scalar", I("activation", out=Eend[:, :, :n], in_=t1[:, :, :n], func=AF.Exp, scale=CDEC), reads=[t1B], writes=[EendB])
                P.op("scalar", I("activation", out=PC[:, :, :nch], in_=v4(cs)[:, :, :, C - 1], func=AF.Exp, scale=-CDEC), reads=[csBuf], writes=[PCB])
                if "r3" in _SKIP:
                    continue
                P.op("vector", I("tensor_tensor", out=R32[:, :, :n], in0=r_[:, :, :n], in1=Ein[:, :, :n], op=ALU.mult), reads=[rB, EinB], writes=[R32B])
                P.op("gpsimd", I("tensor_copy", out=AR[:, :, 1, :n], in_=R32[:, :, :n]), reads=[R32B], writes=[ARB])
                ta_, taB = tmf.next()
                P.op("vector", I("scalar_tensor_tensor", out=ta_[:, :, :n], in0=kk[:, :, :n], scalar=-1.0, in1=Eprev[:, :, :n],
                                                                 op0=ALU.mult, op1=ALU.mult), reads=[kkB, EprevB], writes=[taB])
                P.op("gpsimd", I("tensor_copy", out=AR[:, :, 0, :n], in_=ta_[:, :, :n]), reads=[taB], writes=[ARB])
                P.op("vector", I("tensor_tensor", out=Bt[:, :, :n], in0=bb_[:, :, :n], in1=Eout[:, :, :n], op=ALU.mult), reads=[bbB, EoutB], writes=[BKB])
                P.op("gpsimd", I("tensor_tensor", out=Kt[:, :, :n], in0=km[:, :, :n], in1=Eout[:, :, :n], op=ALU.mult), reads=[kmB, EoutB], writes=[BKB])
                tb_, tbB = tmf.next()
                P.op("vector", I("tensor_tensor", out=tb_[:, :, :n], in0=bb_[:, :, :n], in1=Eend[:, :, :n], op=ALU.mult), reads=[bbB, EendB], writes=[tbB])

                def transp(src, sB, dst_fn, dB_fn, n=n):
                    for fc in range(KC):
                        ps, psb = psr.next()
                        P.op("tensor", I("transpose", out=ps[:n, 0:128], in_=src[:, fc, :n], identity=ident_f[:]),
                             reads=[sB, cB], writes=[psb])
                        dst_fn(fc, ps, psb)
                zc = cur

                def dst_A(fc, ps, psb, n=n):
                    P.op("vector", I("tensor_copy", out=Z[zc][:n, 2 * fc:2 * fc + 2, 0:64], in_=ps[:n, 0:128].rearrange("p (h k) -> p h k", k=64)),
                         reads=[psb], writes=[ZB[zc][2 * fc], ZB[zc][2 * fc + 1]])
                transp(ta_, taB, dst_A, None)

                def dst_simple(dst, dB, n=n):
                    def f(fc, ps, psb):
                        P.op("scalar", I("copy", out=dst[:n, fc, :], in_=ps[:n, 0:128]), reads=[psb], writes=[dB])
                    return f
                transp(tb_, tbB, dst_simple(BPtm, BPB), None)
                tc_, tcB = tmf.next()
                P.op("vector", I("tensor_tensor", out=tc_[:, :, :n], in0=km[:, :, :n], in1=Eend[:, :, :n], op=ALU.mult), reads=[kmB, EendB], writes=[tcB])
                transp(tc_, tcB, dst_simple(KPtm, KPB), None)
                transp(v_, vB, dst_simple(Vtm, VtmB), None)

                if "r4" in _SKIP:
                    continue
                mk3_ = mk3 if G.name == "p" else mk3s
                mkL_ = mkL if G.name == "p" else mkLs
                HG = 4

                def stage_a(hd):
                    hb0 = 64 * (hd % 2)
                    fc = hd // 2
                    hs = slice(hb0, hb0 + 64)
                    psA, psAb = psr.next()
                    P.op("tensor", I("matmul", psA[:n, 0:2 * n].rearrange("p (a t) -> p a t", a=2), Bt[hs, fc, :n], AR[hs, fc, :, :n], start=True, stop=True),
                         reads=[BKB, ARB], writes=[psAb])
                    xa_, xaB = XAs.next()
                    P.op("vector", I("tensor_tensor", out=xa_[:n, :, :n], in0=psA[:n, 0:2 * n].rearrange("p (a t) -> p a t", a=2),
                                     in1=mk3_[:n, :, :n], op=ALU.mult), reads=[psAb, cB], writes=[xaB])
                    psB_, psBb = psr.next()
                    P.op("tensor", I("matmul", psB_[:n, 0:2 * n].rearrange("p (a t) -> p a t", a=2), Kt[hs, fc, :n], AR[hs, fc, :, :n], start=True, stop=True),
                         reads=[BKB, ARB], writes=[psBb])
                    xb_, xbB = XBs.next()
                    P.op("vector", I("tensor_tensor", out=xb_[:n, :, :n], in0=psB_[:n, 0:2 * n].rearrange("p (a t) -> p a t", a=2),
                                     in1=mk3_[:n, :, :n], op=ALU.mult), reads=[psBb, cB], writes=[xbB])
                    psC, psCb = psr.next()
                    P.op("tensor", I("matmul", psC[:n, 0:n], AR[hs, fc, 0, :n], Bt[hs, fc, :n], start=True, stop=True),
                         reads=[BKB, ARB], writes=[psCb])
                    L0, L0B = Ls.next()
                    P.op("vector", I("tensor_tensor", out=L0[:n, :n], in0=psC[:n, 0:n], in1=mkL_[:n, :n], op=ALU.mult),
                         reads=[psCb, cB], writes=[L0B])
                    psG, psGb = psr.next()
                    P.op("tensor", I("matmul", psG[:n, 0:64], xb_[:n, 0, :n], Vtm[:n, fc, hs], start=True, stop=True),
                         reads=[xbB, VtmB], writes=[psGb])
                    P.op("scalar", I("copy", out=Z[zc][:n, hd, 64:128], in_=psG[:n, 0:64]), reads=[psGb], writes=[ZB[zc][hd]])
                    return dict(hd=hd, fc=fc, hs=hs, xa_=xa_, xaB=xaB, xb_=xb_, xbB=xbB, zi=zc,
                                Mj=xa_[:n, 0, :n], MjB=xaB, Lj=L0[:n, :n], LjB=L0B)

                def stage_d(st):
                    hd, fc, hs, xa_, xaB, xb_, xbB = st["hd"], st["fc"], st["hs"], st["xa_"], st["xaB"], st["xb_"], st["xbB"]
                    ZF = Z[st["zi"]]; ZFB = ZB[st["zi"]][hd]
                    psDs = []
                    for c in range(nch):
                        cs_ = slice(c * C, (c + 1) * C)
                        psD, psDb = psr.next()
                        psDs.append((psD, psDb))
                        P.op("tensor", I("matmul", psD[hs, c * 128:c * 128 + 64], ZF[cs_, hd, 0:64], BPtm[cs_, fc, hs], start=True, stop=True),
                             reads=[ZFB, BPB], writes=[psDb])
                        P.op("tensor", I("matmul", psD[hs, c * 128 + 64:c * 128 + 128], BPtm[cs_, fc, hs], ZF[cs_, hd, 64:128], start=True, stop=False),
                             reads=[ZFB, BPB], writes=[psDb])
                        P.op("tensor", I("matmul", psD[hs, c * 128 + 64:c * 128 + 128], KPtm[cs_, fc, hs], Vtm[cs_, fc, hs], start=False, stop=True),
                             reads=[KPB, VtmB], writes=[psDb])
                    for c in range(nch):
                        psD, psDb = psDs[c]
                        P.op("vector", I("scalar_tensor_tensor", out=PhiT[hs, fc, c, :], in0=ident_f[hs, hs], scalar=PC[hs, fc, c:c + 1],
                                         in1=psD[hs, c * 128:c * 128 + 64], op0=ALU.mult, op1=ALU.add), reads=[psDb, PCB, cB], writes=[PhiB])
                        P.op("vector", I("tensor_copy", out=Psi[hs, fc, c, :], in_=psD[hs, c * 128 + 64:c * 128 + 128]),
                             reads=[psDb], writes=[PsiB])
                    psO_, psOb = psr.next()
                    P.op("tensor", I("matmul", psO_[hs, 0:n], ZF[:n, hd, 0:64], xa_[:n, 1, :n], start=True, stop=True),
                         reads=[ZFB, xaB], writes=[psOb])
                    psY0, psY0b = psr.next()
                    P.op("tensor", I("matmul", psY0[hs, 0:n], ZF[:n, hd, 64:128], xa_[:n, 1, :n], start=True, stop=False),
                         reads=[ZFB, xaB], writes=[psY0b])
                    P.op("tensor", I("matmul", psY0[hs, 0:n], Vtm[:n, fc, hs], xb_[:n, 1, :n], start=False, stop=True),
                         reads=[xbB, VtmB], writes=[psY0b])
                    P.op("vector", I("tensor_tensor", out=Om[hs, fc, :n], in0=psO_[hs, 0:n], in1=R32[hs, fc, :n], op=ALU.add),
                         reads=[psOb, R32B], writes=[OmB])
                    P.op("scalar", I("copy", out=Y0[hs, fc, :n], in_=psY0[hs, 0:n]), reads=[psY0b], writes=[Y0B])

                for g0 in range(0, 16, HG):
                    sts = [stage_a(hd) for hd in range(g0, g0 + HG)]
                    for j in range(nlog):
                        pz = []
                        for st in sts:
                            psZ, psZb = psr.next()
                            P.op("tensor", I("matmul", psZ[:n, 0:128], st["Mj"], Z[st["zi"]][:n, st["hd"], :], start=True, stop=True),
                                 reads=[st["MjB"], ZB[st["zi"]][st["hd"]]], writes=[psZb])
                            pz.append((psZ, psZb))
                        for st, (psZ, psZb) in zip(sts, pz):
                            zi, hd = st["zi"], st["hd"]
                            P.op("vector", I("tensor_tensor", out=Z[1 - zi][:n, hd, :], in0=psZ[:n, 0:128], in1=Z[zi][:n, hd, :], op=ALU.add),
                                 reads=[psZb, ZB[zi][hd]], writes=[ZB[1 - zi][hd]])
                            st["zi"] = 1 - zi
                        if j < nlog - 1:
                            pq = []
                            for st in sts:
                                psS_, psSb = psr.next()
                                P.op("tensor", I("matmul", psS_[:n, 0:n], st["Lj"], st["Mj"], start=True, stop=True),
                                     reads=[st["MjB"], st["LjB"]], writes=[psSb])
                                if j < nlog - 2:
                                    P.op("tensor", I("matmul", psS_[:n, n:2 * n], st["Mj"], st["Lj"], start=True, stop=True),
                                         reads=[st["MjB"], st["LjB"]], writes=[psSb])
                                pq.append((psS_, psSb))
                            w_ = 2 if j < nlog - 2 else 1
                            for st, (psS_, psSb) in zip(sts, pq):
                                ml, mlB = MLr.next()
                                P.op("scalar", I("copy", out=ml[:n, 0:w_, :n], in_=psS_[:n, 0:w_ * n].rearrange("p (a t) -> p a t", a=w_)),
                                     reads=[psSb], writes=[mlB])
                                st["Mj"] = ml[:n, 0, :n]; st["MjB"] = mlB
                                st["Lj"] = ml[:n, 1, :n]; st["LjB"] = mlB
                    for st in sts:
                        stage_d(st)
                if "r5" in _SKIP:
                    continue
                cur = zc
                if first:
                    if G.name == "p":
                        P.op("gpsimd", I("memset", S[0][:], 0.0), writes=[SB_[0]])
                    else:
                        P.op("sync", I("dma_start", out=S[0][:], in_=swkv_d[o_, s]), writes=[SB_[0]], dma=P.dq())
                    si = 0
                for c in range(nch):
                    cs_ = slice(c * C, (c + 1) * C)
                    P.op("scalar", I("copy", out=S16[:], in_=S[si][:]), reads=[SB_[si]], writes=[S16B])
                    for hd in range(16 if "cy" not in _SKIP else 0):
                        hb0 = 64 * (hd % 2); fc = hd // 2; hs = slice(hb0, hb0 + 64)
                        P.op("tensor", I("matmul", psY2[fc // 4][hs, fc % 4, cs_], S16[hs, fc, :], Om[hs, fc, cs_], start=True, stop=True),
                             reads=[OmB, S16B], writes=[psYB])
                    for hh_ in range(2 if "cy" not in _SKIP else 0):
                        hsl = slice(hh_ * 4, hh_ * 4 + 4)
                        P.op("vector", I("tensor_tensor", out=Ytm[:, hsl, cs_], in0=psY2[hh_][:, :, cs_], in1=Y0[:, hsl, cs_], op=ALU.add),
                             reads=[psYB, Y0B], writes=[YB])
                    if "cs" in _SKIP:
                        continue
                    for hd in range(16):
                        hb0 = 64 * (hd % 2); fc = hd // 2; hs = slice(hb0, hb0 + 64)
                        P.op("tensor", I("matmul", psSt[hs, fc, :], PhiT[hs, fc, c, :], S[si][hs, fc, :], start=True, stop=True),
                             reads=[PhiB, SB_[si]], writes=[psStB])
                    P.op("vector", I("tensor_tensor", out=S[1 - si][:, :, :], in0=psSt[:, :, :], in1=Psi[:, :, c, :], op=ALU.add),
                         reads=[psStB, PsiB], writes=[SB_[1 - si]])
                    si = 1 - si
                if t0 + n == G.T:
                    P.op("sync", I("dma_start", out=wkv_o[G.name][o_, s], in_=S[si][:]), reads=[SB_[si]], dma=P.dq())
                if "r6" in _SKIP:
                    continue
                yo_, yoB = yo.next()
                for fc in range(KC):
                    ps, psb = psr.next()
                    P.op("tensor", I("matmul", ps[:, :n], bones_f[:], Ytm[:, fc, :n], start=True, stop=True), reads=[cB, YB], writes=[psb])
                    P.op("vector", I("scalar_tensor_tensor", out=yc[:, fc, :n], in0=ps[:, :n], scalar=-1.0 / 64, in1=Ytm[:, fc, :n],
                                     op0=ALU.mult, op1=ALU.add), reads=[psb, YB], writes=[ycB])
                P.op("scalar", I("activation", out=ysq[:, :, :n], in_=yc[:, :, :n], func=AF.Square), reads=[ycB], writes=[ysqB])
                for fc in range(KC):
                    ps, psb = psr.next()
                    P.op("tensor", I("matmul", ps[:, :n], bones_f[:], ysq[:, fc, :n], start=True, stop=True), reads=[cB, ysqB], writes=[psb])
                    P.op("vector", I("tensor_scalar", out=yt32[:, fc, :n], in0=ps[:, :n], scalar1=1.0 / 64, scalar2=LNEPS, op0=ALU.mult, op1=ALU.add),
                         reads=[psb], writes=[yt32B])
                P.op("scalar", I("activation", out=yt32[:, :, :n], in_=yt32[:, :, :n], func=AF.Sqrt), reads=[yt32B], writes=[yt32B])
                P.op("vector", I("reciprocal", out=yt32[:, :, :n], in_=yt32[:, :, :n]), reads=[yt32B], writes=[yt32B])
                P.op("vector", I("tensor_tensor", out=yc[:, :, :n], in0=yc[:, :, :n], in1=yt32[:, :, :n], op=ALU.mult), reads=[ycB, yt32B], writes=[ycB])
                for fc in range(KC):
                    P.op("vector", I("scalar_tensor_tensor", out=yt32[:, fc, :n], in0=yc[:, fc, :n], scalar=pv[:, lgc + fc:lgc + fc + 1],
                                     in1=bon[:, fc, :n], op0=ALU.mult, op1=ALU.add), reads=[ycB, bonB, cB], writes=[yt32B])
                P.op("gpsimd", I("tensor_tensor", out=yo_[:, :, :n], in0=yt32[:, :, :n], in1=g_[:, :, :n], op=ALU.mult),
                     reads=[yt32B, gB], writes=[yoB])
                P.op("sync", I("dma_start", out=fm(mixT[G.name])[:, :, c0:c0 + n], in_=yo_[:, :, :n]), reads=[yoB], dma=P.dq())
        P.end_phase()

    plist = []
    for l in range(depth):
        if l % 2 == 0:
            plist.append((phase_even_proj, (l,)))
            plist.append((phase_even_attn, (l,)))
            plist.append((phase_out_xa, (l, w_out_d[l // 2])))
        else:
            plist.append((phase_rwkv, (l,)))
            plist.append((phase_out_xa, (l, rw_w_d["wo"][l // 2])))
        plist.append((phase_ffn, (l, l == depth - 1)))
    for f, a in plist[:_MAXPH]:
        f(*a)
    P.emit()
    return nc


_CACHE = {}
_DEPTH = DEPTH
_MAXPH = 1000
_NCORE = 8


def prep_common(inp):
    c = {}
    c["pvec"] = pack_pv(inp)
    lam = np.stack([np.concatenate([inp["ev_lam_q1"][e], inp["ev_lam_k1"][e], inp["ev_lam_q2"][e], inp["ev_lam_k2"][e]])
                    for e in range(NEVEN)]).reshape(1, -1)
    c["lam"] = np.ascontiguousarray(lam, np.float32)
    c["lnx"] = np.ascontiguousarray(np.stack([inp["rw_lnx_g"][0], inp["rw_lnx_b"][0], inp["rw_lnx_g"][1], inp["rw_lnx_b"][1]]), np.float32)
    c["ev_w_in"] = np.stack([wl(inp["ev_w_in"][e]) for e in range(NEVEN)])
    c["ev_pool_w"] = np.ascontiguousarray(np.asarray(inp["ev_pool_w"]).transpose(0, 2, 1, 3))
    c["ev_w_out"] = np.stack([wl(inp["ev_w_out"][e]) for e in range(NEVEN)])
    for k in ("wr", "wk", "wv", "wo"):
        c["rw_" + k] = np.stack([wl(inp["rw_" + k][o]) for o in range(NODD)])
    for k in ("w1", "a1", "g1", "v1"):
        a = inp["rw_" + k]
        c["rw_" + k] = np.stack([wl(a[o]) for o in range(a.shape[0])])
    for k in ("w2", "a2", "g2", "v2"):
        c["rw_" + k] = np.ascontiguousarray(inp["rw_" + k])
    for k in ("wq", "wk", "wv", "wo"):
        c["xa_" + k] = np.stack([wl(inp["xa_" + k][l]) for l in range(DEPTH)])
    c["ffn_wg"] = np.stack([wl(inp["ffn_wg"][l]) for l in range(DEPTH)])
    c["ffn_wu"] = np.stack([wl(inp["ffn_wu"][l]) for l in range(DEPTH)])
    c["ffn_wd"] = np.stack([wl(inp["ffn_wd"][l]) for l in range(DEPTH)])
    return c


def kernel(**inp):
    inp = {k: np.asarray(v) for k, v in inp.items()}
    B, TP, _ = inp["x_prompt"].shape
    SB, TS, _ = inp["x_sample"].shape
    PAST = inp["cache_diff_k"].shape[2]
    NMEM = inp["mem_prompt"].shape[1]
    NCORE = _NCORE
    NSB = SB // NCORE
    key = (TP, NSB, TS, PAST, NMEM, _DEPTH)
    if key not in _CACHE:
        _CACHE[key] = build(TP, NSB, TS, PAST, NMEM, _DEPTH)
    nc = _CACHE[key]
    common = prep_common(inp)
    in_maps = []
    for c in range(NCORE):
        b = c % B
        sb = slice(c * NSB, (c + 1) * NSB)
        m = dict(common)
        m["xT_p"] = np.ascontiguousarray(inp["x_prompt"][b].T)
        m["xT_s"] = np.ascontiguousarray(inp["x_sample"][sb].reshape(NSB * TS, D).T)
        m["memT"] = wl(np.ascontiguousarray(inp["mem_prompt"][b].T))
        m["cache_kT"] = np.ascontiguousarray(inp["cache_diff_k"][:, sb].transpose(0, 1, 3, 4, 2))
        m["cache_v"] = np.ascontiguousarray(inp["cache_diff_v"][:, sb])
        sp = inp["state_pool"][:, sb]
        m["state_pool"] = np.ascontiguousarray(sp.reshape(NEVEN, NSB, 15, 4, 128).transpose(0, 1, 4, 3, 2))
        m["state_shift"] = np.ascontiguousarray(inp["state_rw_shift"][:, sb].reshape(NODD, NSB, KC, 128).transpose(0, 1, 3, 2))
        sw = inp["state_rw_wkv"][:, sb]
        m["state_wkvT"] = np.ascontiguousarray(sw.reshape(NODD, NSB, 8, 2, 64, 64).transpose(0, 1, 3, 5, 2, 4).reshape(NODD, NSB, 128, 8, 64))
        mk = inp["cache_mem_k"][:, sb].reshape(DEPTH, NSB, NMEM, KC, 128)
        m["cache_mkT"] = np.ascontiguousarray(mk.transpose(0, 1, 4, 3, 2))
        m["cache_mv"] = np.ascontiguousarray(inp["cache_mem_v"][:, sb].reshape(DEPTH, NSB, NMEM, D))
        in_maps.append(m)
    res = run_bass_kernel_spmd(nc, in_maps, core_ids=list(range(NCORE))).results

    def unfm(a):
        return np.swapaxes(a, 0, 1).reshape((-1,) + a.shape[2:])

    y_p = np.stack([res[b]["yT_p"].T for b in range(B)])
    y_s = np.concatenate([res[c]["yT_s"].T.reshape(NSB, TS, D) for c in range(NCORE)])
    p_k = np.stack([np.stack([res[b]["kT_p"][e].T.reshape(TP, 4, 128) for b in range(B)]) for e in range(NEVEN)])
    p_v = np.stack([np.stack([res[b]["v_p"][e].reshape(TP, 4, 128) for b in range(B)]) for e in range(NEVEN)])
    s_k = np.stack([np.concatenate([res[c]["kT_s"][e].T.reshape(NSB, TS, 4, 128) for c in range(NCORE)]) for e in range(NEVEN)])
    s_v = np.stack([np.concatenate([res[c]["v_s"][e].reshape(NSB, TS, 4, 128) for c in range(NCORE)]) for e in range(NEVEN)])

    def pool_back(a):
        return a.transpose(0, 1, 4, 3, 2).reshape(a.shape[0], a.shape[1], 15, 512)

    p_pool = pool_back(np.concatenate([res[b]["pool_p"] for b in range(B)], axis=1))
    s_pool = pool_back(np.concatenate([res[c]["pool_s"] for c in range(NCORE)], axis=1))

    def shift_back(a):
        return a.transpose(0, 1, 3, 2).reshape(a.shape[0], a.shape[1], D)

    p_sh = shift_back(np.concatenate([res[b]["shift_p"] for b in range(B)], axis=1))
    s_sh = shift_back(np.concatenate([res[c]["shift_s"] for c in range(NCORE)], axis=1))

    def wkv_back(a):
        o, n = a.shape[:2]
        return a.reshape(o, n, 2, 64, 8, 64).transpose(0, 1, 4, 2, 5, 3).reshape(o, n, 16, 64, 64)

    p_wkv = wkv_back(np.concatenate([res[b]["wkv_p"] for b in range(B)], axis=1))
    s_wkv = wkv_back(np.concatenate([res[c]["wkv_s"] for c in range(NCORE)], axis=1))
    p_mk = np.stack([np.stack([unfm(res[b]["memkT"][l]).T.reshape(NMEM, 4, 256) for b in range(B)]) for l in range(DEPTH)])
    p_mv = np.stack([np.stack([res[b]["memv"][l].reshape(NMEM, 4, 256) for b in range(B)]) for l in range(DEPTH)])
    outs = (y_p, y_s, p_k, p_v, p_pool, p_sh, p_wkv, p_mk, p_mv, s_k, s_v, s_pool, s_sh, s_wkv)
    return tuple(np.ascontiguousarray(o, dtype=np.float32) for o in outs)
```

```python
import math
import numpy as np
from contextlib import ExitStack
import concourse.bass as bass
import concourse.mybir as mybir
from concourse.bass_utils import run_bass_kernel_spmd

F32 = mybir.dt.float32
BF16 = mybir.dt.bfloat16
AF = mybir.ActivationFunctionType
ALU = mybir.AluOpType
AX = mybir.AxisListType
ENGS = ["tensor", "vector", "scalar", "gpsimd", "sync"]

D = 1024
KC = 8
DFF = 2816
FC = 22
DEPTH = 4
NEVEN = 2
NODD = 2
EPS = 1e-6
LNEPS = 64e-5
CDEC = math.exp(-0.5)


class Buf:
    __slots__ = ("w", "r", "q")

    def __init__(self):
        self.w = None
        self.r = []
        self.q = None


class Prog:
    def __init__(self, nc):
        self.nc = nc
        self.es = ExitStack()
        self.ops = {e: [] for e in ENGS}
        self.sems = {}
        self.cnt = {}
        self.waited = {e: {} for e in ENGS}
        self.nsb = 0
        self.ndq = 0
        self.phase_es = None

    def sb(self, shape, dt=F32):
        self.nsb += 1
        st = self.phase_es if self.phase_es is not None else self.es
        return st.enter_context(self.nc.sbuf_tensor(f"sb{self.nsb}", list(shape), dt))

    def ps(self, shape, dt=F32):
        self.nsb += 1
        st = self.phase_es if self.phase_es is not None else self.es
        return st.enter_context(self.nc.psum_tensor(f"ps{self.nsb}", list(shape), dt))

    def begin_phase(self):
        self.barrier()
        self.phase_es = ExitStack()
        self.ndq = 0
        self.phase_id = getattr(self, "phase_id", 0) + 1

    def end_phase(self):
        self.barrier()
        self.phase_es.close()
        self.phase_es = None

    def dq(self):
        return True

    def op(self, eng, fn, reads=(), writes=(), dma=None):
        if dma is None:
            s = "e_" + eng
            inc = 1
        elif isinstance(dma, str):
            s = "d_" + dma
            inc = 16
        else:
            kb = writes[0] if writes else reads[0]
            pid = getattr(self, "phase_id", 0)
            if kb.q is None or kb.q[0] != pid:
                self.ndq += 1
                kb.q = (pid, f"q{self.ndq}")
            s = "d_" + kb.q[1]
            inc = 16
        own = "e_" + eng
        waits = {}
        wd = self.waited[eng]
        same_raw = eng in ("vector", "scalar", "gpsimd")

        def need(sv, same_ok):
            if sv is None:
                return
            sn, val = sv
            if sn == own and not same_ok:
                return
            if wd.get(sn, 0) >= val:
                return
            if waits.get(sn, 0) < val:
                waits[sn] = val

        for b in reads:
            need(b.w, same_raw)
        for b in writes:
            need(b.w, same_raw)
            for x in b.r:
                need(x, False)
        for k, v in waits.items():
            wd[k] = v
        self.cnt[s] = self.cnt.get(s, 0) + inc
        me = (s, self.cnt[s])
        for b in reads:
            b.r.append(me)
            if len(b.r) > 16:
                d = {}
                for sn, v in b.r:
                    if d.get(sn, 0) < v:
                        d[sn] = v
                b.r = list(d.items())
        for b in writes:
            b.w = me
            b.r = []
        self.ops[eng].append((list(waits.items()), fn, s, inc))
        return me

    def barrier(self):
        for eng in ENGS:
            waits = []
            for s, v in self.cnt.items():
                if self.waited[eng].get(s, 0) < v and s != "e_" + eng:
                    waits.append((s, v))
                    self.waited[eng][s] = v
            if waits:
                self.ops[eng].append((waits, None, None, 0))

    def emit(self):
        self.barrier()
        nc = self.nc
        for s in self.cnt:
            if s not in self.sems:
                self.sems[s] = self.es.enter_context(nc.semaphore(s))
        ops = self.ops
        sems = self.sems

        def run(e, lst):
            for waits, fn, s, inc in lst:
                for sn, v in waits:
                    e.wait_ge(sems[sn], v)
                if fn is not None:
                    fn(e).then_inc(sems[s], inc)

        with nc.Block() as block:
            @block.tensor
            def _(e):
                run(e, ops["tensor"])

            @block.vector
            def _(e):
                run(e, ops["vector"])

            @block.scalar
            def _(e):
                run(e, ops["scalar"])

            @block.gpsimd
            def _(e):
                run(e, ops["gpsimd"])

            @block.sync
            def _(e):
                run(e, ops["sync"])
        self.es.close()


def I(name, *a, **k):
    return lambda e: getattr(e, name)(*a, **k)


class Ring:
    def __init__(self, P, shape, dt, n, psum=False):
        self.items = []
        for _ in range(n):
            t = P.ps(shape, dt) if psum else P.sb(shape, dt)
            self.items.append((t, Buf()))
        self.i = 0

    def next(self):
        it = self.items[self.i % len(self.items)]
        self.i += 1
        return it


PV_SPECS = [("norm_mix_g", DEPTH, D), ("norm_xa_g", DEPTH, D), ("norm_ffn_g", DEPTH, D), ("final_norm_g", 1, D),
            ("ev_pool_scale", NEVEN, 512), ("ev_subln_g", NEVEN, 128),
            ("rw_mu", NODD * 6, D), ("rw_w0", NODD, D), ("rw_a0", NODD, D), ("rw_v0", 1, D),
            ("rw_k_k", NODD, D), ("rw_k_a", NODD, D), ("rw_r_k", NODD, D),
            ("rw_lnx_g", NODD, D), ("rw_lnx_b", NODD, D)]


def pv_layout():
    off = {}
    c = 0
    for name, n, ln in PV_SPECS:
        off[name] = (c, ln // 128)
        c += n * (ln // 128)
    return off, c


PV_OFF, PV_COLS = pv_layout()


def pack_pv(inp):
    out = np.zeros((128, PV_COLS), np.float32)
    for name, n, ln in PV_SPECS:
        a = np.asarray(inp[name], np.float32).reshape(n, ln // 128, 128)
        c0, w = PV_OFF[name]
        out[:, c0:c0 + n * w] = a.transpose(2, 0, 1).reshape(128, n * w)
    return out


def wl(w):
    K, F = w.shape
    return np.ascontiguousarray(w.reshape(K // 128, 128, F).transpose(1, 0, 2))


class Group:
    def __init__(self, name, nseq, T):
        self.name = name
        self.nseq = nseq
        self.T = T
        self.NT = nseq * T


import os
_SKIP = set(os.environ.get("BIS", "").split(","))


def build(TP, NSB, TS, PAST, NMEM=256, depth=DEPTH):
    nc = bass.Bass("TRN2", target_bir_lowering=False)
    P = Prog(nc)
    dram_in = {}
    dram_out = {}

    def din(name, shape, dt=F32):
        dram_in[name] = nc.dram_tensor(name, list(shape), dt, kind="ExternalInput").ap()
        return dram_in[name]

    def dout(name, shape, dt=F32):
        dram_out[name] = nc.dram_tensor(name, list(shape), dt, kind="ExternalOutput").ap()
        return dram_out[name]

    def dscr(name, shape, dt=F32):
        return nc.dram_tensor("scr_" + name, list(shape), dt, kind="Internal").ap()

    GP = Group("p", 1, TP)
    GS = Group("s", NSB, TS)
    groups = [GP, GS]
    NPB = PAST // 128

    xin = {"p": din("xT_p", [D, GP.NT]), "s": din("xT_s", [D, GS.NT])}
    pv_d = din("pvec", [128, PV_COLS])
    lam_d = din("lam", [1, NEVEN * 4 * 64])
    lnx_d = din("lnx", [NODD * 2, D])
    w_in_d = din("ev_w_in", [NEVEN, 128, KC, 2048])
    pool_w_d = din("ev_pool_w", [NEVEN, 128, 4, 128])
    w_out_d = din("ev_w_out", [NEVEN, 128, KC, D])
    rw_w_d = {k: din("rw_" + k, [NODD, 128, KC, D]) for k in ("wr", "wk", "wv", "wo")}
    rw_l1_d = {k: din("rw_" + k, [NODD if k != "v1" else 1, 128, KC, n]) for k, n in (("w1", 64), ("a1", 64), ("g1", 160), ("v1", 32))}
    rw_l2_d = {k: din("rw_" + k, [NODD if k != "v2" else 1, n, D]) for k, n in (("w2", 64), ("a2", 64), ("g2", 160), ("v2", 32))}
    xa_w_d = {k: din("xa_" + k, [DEPTH, 128, KC, D]) for k in ("wq", "wk", "wv", "wo")}
    ffn_g_d = din("ffn_wg", [DEPTH, 128, KC, DFF])
    ffn_u_d = din("ffn_wu", [DEPTH, 128, KC, DFF])
    ffn_d_d = din("ffn_wd", [DEPTH, 128, FC, D])
    memT_d = din("memT", [128, KC, NMEM])
    ckT_d = din("cache_kT", [NEVEN, NSB, 4, 128, PAST])
    cv_d = din("cache_v", [NEVEN, NSB, PAST, 4, 128])
    spool_d = din("state_pool", [NEVEN, NSB, 128, 4, 15])
    sshift_d = din("state_shift", [NODD, NSB, 128, KC])
    swkv_d = din("state_wkvT", [NODD, NSB, 128, KC, 64])
    cmkT_d = din("cache_mkT", [DEPTH, NSB, 128, KC, NMEM])
    cmv_d = din("cache_mv", [DEPTH, NSB, NMEM, D])

    yT_o = {"p": dout("yT_p", [D, GP.NT]), "s": dout("yT_s", [D, GS.NT])}
    kT_o = {"p": dout("kT_p", [NEVEN, 512, GP.NT]), "s": dout("kT_s", [NEVEN, 512, GS.NT])}
    v_o = {"p": dout("v_p", [NEVEN, GP.NT, 512]), "s": dout("v_s", [NEVEN, GS.NT, 512])}
    pool_o = {"p": dout("pool_p", [NEVEN, 1, 128, 4, 15]), "s": dout("pool_s", [NEVEN, NSB, 128, 4, 15])}
    shift_o = {"p": dout("shift_p", [NODD, 1, 128, KC]), "s": dout("shift_s", [NODD, NSB, 128, KC])}
    wkv_o = {"p": dout("wkv_p", [NODD, 1, 128, KC, 64]), "s": dout("wkv_s", [NODD, NSB, 128, KC, 64])}
    memk_o = dout("memkT", [DEPTH, 128, KC, NMEM])
    memv_o = dout("memv", [DEPTH, NMEM, D])

    xT = {g.name: dscr("x_" + g.name, [D, g.NT]) for g in groups}
    qT = {g.name: dscr("q_" + g.name, [512, g.NT], BF16) for g in groups}
    kTs = {g.name: dscr("k_" + g.name, [512, g.NT], BF16) for g in groups}
    vtm = {g.name: dscr("v_" + g.name, [g.NT, 512], BF16) for g in groups}
    mixT = {g.name: dscr("mix_" + g.name, [D, g.NT], BF16) for g in groups}
    vfirst = {g.name: dscr("vf_" + g.name, [D, g.NT]) for g in groups}
    mkT_s = dscr("mkT_s", [DEPTH, 128, KC, NMEM], BF16)
    mv_s = dscr("mv_s", [DEPTH, NMEM, D], BF16)

    def fm(ap):
        return ap.rearrange("(c p) t -> p c t", p=128)

    ident_f = P.sb([128, 128], F32)
    ident_b = P.sb([128, 128], BF16)
    ones_b = P.sb([128, 128], BF16)
    ones_f = P.sb([128, 128], F32)
    bones_b = P.sb([128, 128], BF16)
    bones_f = P.sb([128, 128], F32)
    mk3 = P.sb([128, 2, 128], F32)
    mkL = P.sb([128, 128], F32)
    mk3s = P.sb([16, 2, 16], F32)
    mkLs = P.sb([16, 16], F32)
    pv = P.sb([128, PV_COLS], F32)
    invcnt = P.sb([128, 16], F32)
    neglam = P.sb([128, NEVEN], F32)
    gsub = P.sb([128, NEVEN], F32)
    lamrow = P.sb([1, NEVEN * 4 * 64], F32)
    lamt = P.sb([1, 8], F32)
    cB = Buf()

    def gp(fn, **k):
        P.op("gpsimd", fn, **k)

    gp(I("memset", ones_f[:], 1.0), writes=[cB])
    gp(I("memset", ones_b[:], 1.0), writes=[cB])
    gp(I("memset", ident_f[:], 1.0), writes=[cB])
    gp(I("affine_select", out=ident_f[:], in_=ident_f[:], pattern=[[-1, 128]], compare_op=ALU.is_equal,
                                 fill=0.0, base=0, channel_multiplier=1), reads=[cB], writes=[cB])
    gp(I("tensor_copy", out=ident_b[:], in_=ident_f[:]), reads=[cB], writes=[cB])
    gp(I("memset", bones_f[:], 0.0), writes=[cB])
    gp(I("memset", bones_f[0:64, 0:64], 1.0), writes=[cB])
    gp(I("memset", bones_f[64:128, 64:128], 1.0), writes=[cB])
    gp(I("memset", bones_b[:], 0.0), writes=[cB])
    gp(I("memset", bones_b[0:64, 0:64], 1.0), writes=[cB])
    gp(I("memset", bones_b[64:128, 64:128], 1.0), writes=[cB])
    gp(I("memset", mk3[:], 1.0), writes=[cB])
    gp(I("memset", mkL[:], 1.0), writes=[cB])
    gp(I("affine_select", out=mk3[:, 0, :], in_=mk3[:, 0, :], pattern=[[1, 128]], compare_op=ALU.is_gt,
                                 fill=0.0, base=0, channel_multiplier=-1), reads=[cB], writes=[cB])
    gp(I("affine_select", out=mk3[:, 1, :], in_=mk3[:, 1, :], pattern=[[1, 128]], compare_op=ALU.is_ge,
                                 fill=0.0, base=0, channel_multiplier=-1), reads=[cB], writes=[cB])
    gp(I("affine_select", out=mkL[:], in_=mkL[:], pattern=[[-1, 128]], compare_op=ALU.is_gt,
                                 fill=0.0, base=0, channel_multiplier=1), reads=[cB], writes=[cB])
    gp(I("tensor_copy", out=mk3s[:], in_=mk3[0:16, :, 0:16]), reads=[cB], writes=[cB])
    gp(I("tensor_copy", out=mkLs[:], in_=mkL[0:16, 0:16]), reads=[cB], writes=[cB])
    gp(I("memset", mk3[0:64, :, 64:128], 0.0), reads=[cB], writes=[cB])
    gp(I("memset", mk3[64:128, :, 0:64], 0.0), reads=[cB], writes=[cB])
    gp(I("memset", mkL[0:64, 64:128], 0.0), reads=[cB], writes=[cB])
    gp(I("memset", mkL[64:128, 0:64], 0.0), reads=[cB], writes=[cB])
    gp(I("iota", invcnt[:], pattern=[[1, 16]], base=1, channel_multiplier=0,
                        allow_small_or_imprecise_dtypes=True), writes=[cB])
    P.op("vector", I("reciprocal", out=invcnt[:], in_=invcnt[:]), reads=[cB], writes=[cB])
    P.op("sync", I("dma_start", out=pv[:], in_=pv_d), writes=[cB], dma="c0")
    P.op("sync", I("dma_start", out=lamrow[:], in_=lam_d), writes=[cB], dma="c1")
    P.begin_phase()
    lamps = P.ps([128, 512], F32)
    for e_ in range(NEVEN if "lam" not in _SKIP else 0):
        b0 = e_ * 256
        for j in range(2):
            P.op("vector", I("tensor_tensor",
                out=lamrow[0:1, b0 + j * 128:b0 + j * 128 + 64], in0=lamrow[0:1, b0 + j * 128:b0 + j * 128 + 64],
                in1=lamrow[0:1, b0 + j * 128 + 64:b0 + j * 128 + 128], op=ALU.mult), reads=[cB], writes=[cB])
            P.op("vector", I("reduce_sum",
                out=lamt[0:1, e_ * 4 + j:e_ * 4 + j + 1], in_=lamrow[0:1, b0 + j * 128:b0 + j * 128 + 64], axis=AX.X),
                reads=[cB], writes=[cB])
        P.op("scalar", I("activation", out=lamt[0:1, e_ * 4:e_ * 4 + 2], in_=lamt[0:1, e_ * 4:e_ * 4 + 2],
                                                     func=AF.Exp), reads=[cB], writes=[cB])
        lam_init = 0.8 - 0.6 * math.exp(-0.3 * (2 * e_))
        P.op("vector", I("tensor_scalar", out=lamt[0:1, e_ * 4 + 2:e_ * 4 + 3], in0=lamt[0:1, e_ * 4 + 1:e_ * 4 + 2], scalar1=-lam_init,
                         scalar2=1.0, op0=ALU.add, op1=ALU.mult), reads=[cB], writes=[cB])
        P.op("vector", I("tensor_tensor", out=lamt[0:1, e_ * 4 + 2:e_ * 4 + 3], in0=lamt[0:1, e_ * 4 + 2:e_ * 4 + 3],
                         in1=lamt[0:1, e_ * 4:e_ * 4 + 1], op=ALU.subtract), reads=[cB], writes=[cB])
        P.op("tensor", I("matmul", lamps[:, e_:e_ + 1], ones_f[0:1, :], lamt[0:1, e_ * 4 + 2:e_ * 4 + 3],
                                                 start=True, stop=True), reads=[cB], writes=[cB])
        P.op("vector", I("tensor_copy", out=neglam[:, e_:e_ + 1], in_=lamps[:, e_:e_ + 1]),
             reads=[cB], writes=[cB])
        c0 = PV_OFF["ev_subln_g"][0] + e_
        P.op("scalar", I("mul", gsub[:, e_:e_ + 1], pv[:, c0:c0 + 1], 1.0 - lam_init), reads=[cB], writes=[cB])

    def pvc(name, idx=0):
        c0, w = PV_OFF[name]
        return c0 + idx * w

    def load_w(dst, src, buf, eng="gpsimd"):
        P.op(eng, I("dma_start", out=dst, in_=src), writes=[buf], dma=P.dq())

    def norm_tile(xt, xb, n, gcol0, out, ob, R, sq_ring, ps_ring, kc_n=KC, dnorm=D, eps=EPS, ones=None):
        ones = ones_b if ones is None else ones
        sq, sqb = sq_ring.next()
        P.op("scalar", I("activation", out=sq[:, :kc_n, :n], in_=xt[:, :kc_n, :n], func=AF.Square),
             reads=[xb], writes=[sqb])
        ps, psb = ps_ring.next()
        for kc in range(kc_n):
            P.op("tensor", I("matmul", ps[:, :n], ones[:], sq[:, kc, :n], start=(kc == 0),
                                                     stop=(kc == kc_n - 1)), reads=[sqb, cB], writes=[psb])
        rstd, rb = R.next()
        P.op("vector", I("tensor_scalar", out=rstd[:, :n], in0=ps[:, :n], scalar1=1.0 / dnorm, scalar2=eps,
                                                 op0=ALU.mult, op1=ALU.add), reads=[psb], writes=[rb])
        P.op("scalar", I("activation", out=rstd[:, :n], in_=rstd[:, :n], func=AF.Sqrt), reads=[rb], writes=[rb])
        P.op("vector", I("reciprocal", out=rstd[:, :n], in_=rstd[:, :n]), reads=[rb], writes=[rb])
        for kc in range(kc_n):
            eng = "vector"
            P.op(eng, I("scalar_tensor_tensor",
                out=out[:, kc, :n], in0=xt[:, kc, :n], scalar=pv[:, gcol0 + kc:gcol0 + kc + 1], in1=rstd[:, :n],
                op0=ALU.mult, op1=ALU.mult), reads=[xb, rb, cB], writes=[ob])

    def linear(W, wb, h, hb, n, f0, nfo, ps_ring, epi, kc_n=KC):
        for fo in range(nfo):
            ps, psb = ps_ring.next()
            for kc in range(kc_n):
                P.op("tensor", I("matmul",
                    ps[:, :n], W[:, kc, f0 + fo * 128:f0 + (fo + 1) * 128], h[:, kc, :n], start=(kc == 0),
                    stop=(kc == kc_n - 1)), reads=[wb, hb], writes=[psb])
            epi(fo, ps, psb)

    def tiles_of(G, nmax):
        res = []
        for s in range(G.nseq):
            t0 = 0
            while t0 < G.T:
                n = min(nmax, G.T - t0)
                res.append((s, t0, n, s * G.T + t0))
                t0 += n
        return res

    cp = Ring(P, [128, KC, 512], F32, 2)
    for G in (groups if "xcopy" not in _SKIP else []):
        for (s, t0, n, c0) in tiles_of(G, 512):
            t, tb = cp.next()
            P.op("sync", I("dma_start", out=t[:, :, :n], in_=fm(xin[G.name])[:, :, c0:c0 + n]),
                 writes=[tb], dma=P.dq())
            P.op("sync", I("dma_start", out=fm(xT[G.name])[:, :, c0:c0 + n], in_=t[:, :, :n]),
                 reads=[tb], dma=P.dq())
    memT = P.sb([128, KC, NMEM], BF16)
    memB = Buf()
    load_w(memT[:], memT_d, memB)
    wkr = Ring(P, [128, KC, D], BF16, 2)
    psr = Ring(P, [128, 512], F32, 4, psum=True)
    ev32 = Ring(P, [128, 512], F32, 3)
    ev16 = Ring(P, [128, 512], BF16, 3)
    for l in range(depth if "memkv" not in _SKIP else 0):
        wk, wkb = wkr.next()
        load_w(wk[:], xa_w_d["wk"][l], wkb)
        wv, wvb = wkr.next()
        load_w(wv[:], xa_w_d["wv"][l], wvb)

        def epi_k(fo, ps, psb, l=l):
            a, ab = ev32.next()
            b, bb = ev16.next()
            if "e1" not in _SKIP:
                P.op("scalar", I("copy", out=a[:, :NMEM], in_=ps[:, :NMEM]), reads=[psb], writes=[ab])
            if "e2" not in _SKIP:
                P.op("gpsimd", I("tensor_copy", out=b[:, :NMEM], in_=a[:, :NMEM]), reads=[ab], writes=[bb])
            if "e3" not in _SKIP:
                P.op("sync", I("dma_start", out=memk_o[l, :, fo, :], in_=a[:, :NMEM]), reads=[ab], dma=P.dq())
            if "e4" not in _SKIP:
                P.op("sync", I("dma_start", out=mkT_s[l, :, fo, :], in_=b[:, :NMEM]), reads=[bb], dma=P.dq())
        if "mk" not in _SKIP:
            linear(wk, wkb, memT, memB, NMEM, 0, KC, psr, epi_k)
        for kb in range(NMEM // 128 if "mvv" not in _SKIP else 0):
            for hf in range(2):
                ps, psb = psr.next()
                for kc in range(KC):
                    P.op("tensor", I("matmul",
                        ps[:, :], memT[:, kc, kb * 128:(kb + 1) * 128], wv[:, kc, hf * 512:(hf + 1) * 512],
                        start=(kc == 0), stop=(kc == KC - 1)), reads=[wvb, memB], writes=[psb])
                a, ab = ev32.next()
                b, bb = ev16.next()
                P.op("scalar", I("copy", out=a[:], in_=ps[:]), reads=[psb], writes=[ab])
                P.op("gpsimd", I("tensor_copy", out=b[:], in_=a[:]), reads=[ab], writes=[bb])
                P.op("sync", I("dma_start",
                    out=memv_o[l, kb * 128:(kb + 1) * 128, hf * 512:(hf + 1) * 512], in_=a[:]), reads=[ab], dma=P.dq())
                P.op("sync", I("dma_start",
                    out=mv_s[l, kb * 128:(kb + 1) * 128, hf * 512:(hf + 1) * 512], in_=b[:]), reads=[bb], dma=P.dq())
    P.end_phase()

    def phase_even_proj(l):
        e_ = l // 2
        P.begin_phase()
        w_in = P.sb([128, KC, 2048], BF16)
        wb = Buf()
        load_w(w_in[:], w_in_d[e_], wb)
        pw = P.sb([128, 4, 128], BF16)
        pwb = Buf()
        load_w(pw[:], pool_w_d[e_], pwb)
        xr = Ring(P, [128, KC, 512], F32, 2)
        hr = Ring(P, [128, KC, 512], BF16, 2)
        sqr = Ring(P, [128, KC, 512], BF16, 1)
        rr = Ring(P, [128, 512], F32, 2)
        psr = Ring(P, [128, 512], F32, 6, psum=True)
        ev32 = Ring(P, [128, 512], F32, 4)
        ev16 = Ring(P, [128, 512], BF16, 4)
        ubuf = P.sb([128, 4, 15 + 512], F32)
        ub = Buf()
        ta = P.sb([128, 15 + 512], F32)
        tb2 = P.sb([128, 15 + 512], F32)
        tB = Buf()
        for G in groups:
            for (s, t0, n, c0) in tiles_of(G, 512):
                if t0 == 0:
                    if G.name == "p":
                        P.op("gpsimd", I("memset", ubuf[:, :, 0:15], 0.0), writes=[ub])
                    else:
                        P.op("sync", I("dma_start", out=ubuf[:, :, 0:15], in_=spool_d[e_, s]), writes=[ub],
                             dma=P.dq())
                xt, xb = xr.next()
                P.op("sync", I("dma_start", out=xt[:, :, :n], in_=fm(xT[G.name])[:, :, c0:c0 + n]),
                     writes=[xb], dma=P.dq())
                h, hb = hr.next()
                norm_tile(xt, xb, n, pvc("norm_mix_g", l), h, hb, rr, sqr, psr)

                def epi_u(fo, ps, psb):
                    P.op("scalar", I("copy", out=ubuf[:, fo, 15:15 + n], in_=ps[:, :n]), reads=[psb], writes=[ub])
                linear(w_in, wb, h, hb, n, 0, 4, psr, epi_u)

                def epi_q(fo, ps, psb, G=G, c0=c0):
                    b, bb = ev16.next()
                    P.op("vector", I("tensor_copy", out=b[:, :n], in_=ps[:, :n]), reads=[psb], writes=[bb])
                    P.op("sync", I("dma_start", out=qT[G.name][fo * 128:(fo + 1) * 128, c0:c0 + n], in_=b[:, :n]),
                         reads=[bb], dma=P.dq())
                linear(w_in, wb, h, hb, n, 512, 4, psr, epi_q)

                def epi_k(fo, ps, psb, G=G, c0=c0):
                    a, ab = ev32.next()
                    b, bb = ev16.next()
                    P.op("scalar", I("copy", out=a[:, :n], in_=ps[:, :n]), reads=[psb], writes=[ab])
                    P.op("gpsimd", I("tensor_copy", out=b[:, :n], in_=a[:, :n]), reads=[ab], writes=[bb])
                    P.op("sync", I("dma_start", out=kT_o[G.name][e_, fo * 128:(fo + 1) * 128, c0:c0 + n], in_=a[:, :n]),
                         reads=[ab], dma=P.dq())
                    P.op("sync", I("dma_start", out=kTs[G.name][fo * 128:(fo + 1) * 128, c0:c0 + n], in_=b[:, :n]),
                         reads=[bb], dma=P.dq())
                linear(w_in, wb, h, hb, n, 1024, 4, psr, epi_k)
                for j in range((n + 127) // 128):
                    m = min(128, n - j * 128)
                    ps, psb = psr.next()
                    for kc in range(KC):
                        P.op("tensor", I("matmul",
                            ps[:m, :], h[:, kc, j * 128:j * 128 + m], w_in[:, kc, 1536:2048], start=(kc == 0),
                            stop=(kc == KC - 1)), reads=[wb, hb], writes=[psb])
                    a, ab = ev32.next()
                    b, bb = ev16.next()
                    P.op("scalar", I("copy", out=a[:m, :], in_=ps[:m, :]), reads=[psb], writes=[ab])
                    P.op("gpsimd", I("tensor_copy", out=b[:m, :], in_=a[:m, :]), reads=[ab], writes=[bb])
                    r0 = c0 + j * 128
                    P.op("sync", I("dma_start", out=v_o[G.name][e_, r0:r0 + m, :], in_=a[:m, :]),
                         reads=[ab], dma=P.dq())
                    P.op("sync", I("dma_start", out=vtm[G.name][r0:r0 + m, :], in_=b[:m, :]),
                         reads=[bb], dma=P.dq())
                L = 15 + n
                for g in range(4):
                    w = 2 << g
                    src = ubuf[:, g, :]
                    cur = None
                    sh = 1
                    for st in range(g + 1):
                        dst = ta if st % 2 == 0 else tb2
                        s_ap = src if cur is None else cur
                        lo = 2 * sh - 1
                        P.op("vector", I("tensor_tensor",
                            out=dst[:, lo:L], in0=s_ap[:, lo:L], in1=s_ap[:, lo - sh:L - sh], op=ALU.add),
                            reads=[ub, tB], writes=[tB])
                        cur = dst
                        sh *= 2
                    pl, plb = ev32.next()
                    P.op("vector", I("tensor_scalar", out=pl[:, :n], in0=cur[:, 15:15 + n], scalar1=1.0 / w, scalar2=0.0, op0=ALU.mult, op1=ALU.add),
                         reads=[tB], writes=[plb])
                    P.op("vector", I("tensor_tensor", out=pl[:, :n], in0=pl[:, :n], in1=ubuf[:, g, 15:15 + n], op=ALU.subtract),
                         reads=[ub, plb], writes=[plb])
                    if G.name == "p" and t0 == 0:
                        P.op("vector", I("tensor_tensor",
                            out=pl[:, 0:w - 1], in0=cur[:, 15:15 + w - 1], in1=invcnt[:, 0:w - 1], op=ALU.mult),
                            reads=[tB, cB, plb], writes=[plb])
                        P.op("vector", I("tensor_tensor",
                            out=pl[:, 0:w - 1], in0=pl[:, 0:w - 1], in1=ubuf[:, g, 15:15 + w - 1], op=ALU.subtract),
                            reads=[ub, plb], writes=[plb])
                    pb_, pbb = ev16.next()
                    P.op("gpsimd", I("tensor_copy", out=pb_[:, :n], in_=pl[:, :n]), reads=[plb], writes=[pbb])
                    ps, psb = psr.next()
                    P.op("tensor", I("matmul", ps[:, :n], pw[:, g, :], pb_[:, :n], start=True, stop=True),
                         reads=[pwb, pbb], writes=[psb])
                    ob_, obb = ev16.next()
                    sc = pvc("ev_pool_scale", e_) + g
                    P.op("scalar", I("mul", ob_[:, :n], ps[:, :n], pv[:, sc:sc + 1]),
                         reads=[psb, cB], writes=[obb])
                    P.op("sync", I("dma_start",
                        out=mixT[G.name][g * 128:(g + 1) * 128, c0:c0 + n], in_=ob_[:, :n]), reads=[obb], dma=P.dq())
                if t0 + n == G.T:
                    P.op("sync", I("dma_start", out=pool_o[G.name][e_, s], in_=ubuf[:, :, n:n + 15]),
                         reads=[ub], dma=P.dq())
                else:
                    P.op("vector", I("tensor_copy", out=ubuf[:, :, 0:15], in_=ubuf[:, :, n:n + 15]), reads=[ub], writes=[ub])
        P.end_phase()

    def phase_even_attn(l):
        e_ = l // 2
        scale = 64 ** -0.5
        P.begin_phase()
        psS = Ring(P, [128, 512], F32, 3, psum=True)
        psO = [Ring(P, [128, 512], F32, 1, psum=True) for _ in range(2)]
        psD = [Ring(P, [128, 512], F32, 1, psum=True) for _ in range(2)]
        ptr = Ring(P, [128, 512], BF16, 6)
        tmp = Ring(P, [128, 512], F32, 6)
        sqr = Ring(P, [128, 1, 512], BF16, 2)
        o16 = Ring(P, [128, 1, 512], BF16, 2)

        def finish(o_, ob, d_, db, n, dst_rows, G, c0):
            a = []
            for m in range(2):
                r, rb = tmp.next()
                P.op("vector", I("reciprocal", out=r[:, :n], in_=d_[m][:, :n]), reads=[db[m]], writes=[rb])
                t, tb = tmp.next()
                P.op("vector", I("tensor_tensor", out=t[:, :n], in0=o_[m][:, :n], in1=r[:, :n], op=ALU.mult),
                     reads=[ob[m], rb], writes=[tb])
                a.append((t, tb))
            av, avb = tmp.next()
            P.op("vector", I("scalar_tensor_tensor", out=av[:, :n], in0=a[1][0][:, :n], scalar=neglam[:, e_:e_ + 1],
                                                             in1=a[0][0][:, :n], op0=ALU.mult, op1=ALU.add),
                 reads=[a[0][1], a[1][1], cB], writes=[avb])
            out, outb = o16.next()
            sq, sqb = sqr.next()
            P.op("scalar", I("activation", out=sq[:, 0, :n], in_=av[:, :n], func=AF.Square), reads=[avb], writes=[sqb])
            ps, psb = psS.next()
            P.op("tensor", I("matmul", ps[:, :n], ones_b[:], sq[:, 0, :n], start=True, stop=True), reads=[sqb, cB], writes=[psb])
            rstd, rb = tmp.next()
            P.op("vector", I("tensor_scalar", out=rstd[:, :n], in0=ps[:, :n], scalar1=1.0 / 128, scalar2=EPS,
                                                     op0=ALU.mult, op1=ALU.add), reads=[psb], writes=[rb])
            P.op("scalar", I("activation", out=rstd[:, :n], in_=rstd[:, :n], func=AF.Sqrt), reads=[rb], writes=[rb])
            P.op("vector", I("reciprocal", out=rstd[:, :n], in_=rstd[:, :n]), reads=[rb], writes=[rb])
            P.op("vector", I("scalar_tensor_tensor", out=out[:, 0, :n], in0=av[:, :n], scalar=gsub[:, e_:e_ + 1],
                                                             in1=rstd[:, :n], op0=ALU.mult, op1=ALU.mult),
                 reads=[avb, rb, cB], writes=[outb])
            P.op("sync", I("dma_start", out=mixT[G.name][dst_rows:dst_rows + 128, c0:c0 + n], in_=out[:, 0, :n]),
                 reads=[outb], dma=P.dq())

        G = GP
        T = G.T
        kh_r = Ring(P, [128, T], BF16, 2)
        vh_r = Ring(P, [128, max(T // 128, 1), 128], BF16, 2)
        qt_r = Ring(P, [128, 512], BF16, 2)
        accr = [Ring(P, [128, 512], F32, 2) for _ in range(2)]
        for hd in range(4):
            kh, khb = kh_r.next()
            P.op("sync", I("dma_start", out=kh[:, :], in_=kTs["p"][hd * 128:(hd + 1) * 128, :]), writes=[khb], dma=P.dq())
            vh, vhb = vh_r.next()
            P.op("sync", I("dma_start",
                out=vh[:, :, :], in_=vtm["p"][:, hd * 128:(hd + 1) * 128].rearrange("(j p) e -> p j e", p=128)), writes=[vhb], dma=P.dq())
            for (s, t0, n, c0) in tiles_of(G, 512):
                qt, qtb = qt_r.next()
                P.op("sync", I("dma_start", out=qt[:, :n], in_=qT["p"][hd * 128:(hd + 1) * 128, c0:c0 + n]),
                     writes=[qtb], dma=P.dq())
                o_ = [psO[m].next() for m in range(2)]
                d_ = [psD[m].next() for m in range(2)]
                acc = [accr[m].next() for m in range(2)]
                nkb = (t0 + n) // 128

                def pv_block(items):
                    for (pt, ptb, m, jb, q0, diag, first, last) in items:
                        P.op("tensor", I("matmul", o_[m][0][:, q0:n], vh[:, jb, :], pt[:, q0:n], start=first, stop=last),
                             reads=[vhb, ptb], writes=[o_[m][1]])
                        aeng = "gpsimd" if m == 0 else "vector"
                        if first:
                            P.op(aeng, I("tensor_copy", out=acc[m][0][:, q0:n], in_=pt[:, q0:n]), reads=[ptb], writes=[acc[m][1]])
                        else:
                            P.op(aeng, I("tensor_tensor", out=acc[m][0][:, q0:n], in0=acc[m][0][:, q0:n], in1=pt[:, q0:n], op=ALU.add),
                                 reads=[ptb, acc[m][1]], writes=[acc[m][1]])
                        if last:
                            ab16, ab16b = ptr.next()
                            P.op(aeng, I("tensor_copy", out=ab16[:, :n], in_=acc[m][0][:, :n]), reads=[acc[m][1]], writes=[ab16b])
                            P.op("tensor", I("matmul", d_[m][0][:, :n], ones_b[:], ab16[:, :n], start=True, stop=True),
                                 reads=[cB, ab16b], writes=[d_[m][1]])

                pending = None
                for j in range(nkb):
                    jj = j - t0 // 128
                    q0 = 0 if jj < 0 else 128 * jj
                    first = (j == 0)
                    last = (j == nkb - 1)
                    items = []
                    for m in range(2):
                        ps, psb = psS.next()
                        P.op("tensor", I("matmul", ps[:, q0:n], kh[64 * m:64 * m + 64, j * 128:(j + 1) * 128], qt[64 * m:64 * m + 64, q0:n],
                                         start=True, stop=True), reads=[khb, qtb], writes=[psb])
                        pt, ptb = ptr.next()
                        P.op("scalar", I("activation", out=pt[:, q0:n], in_=ps[:, q0:n], func=AF.Exp, scale=scale),
                             reads=[psb], writes=[ptb])
                        if jj >= 0:
                            P.op("gpsimd", I("memset", pt[64:128, q0:q0 + 64], 0.0), reads=[ptb], writes=[ptb])
                        items.append((pt, ptb, m, j, q0, jj >= 0, first, last))
                    if pending is not None:
                        pv_block(pending)
                    pending = items
                pv_block(pending)
                finish([o_[0][0], o_[1][0]], [o_[0][1], o_[1][1]], [d_[0][0], d_[1][0]], [d_[0][1], d_[1][1]], n,
                       512 + hd * 128, G, c0)
        G = GS
        TSq = G.T
        kc_r = Ring(P, [128, PAST], BF16, 2)
        vc_r = Ring(P, [128, NPB, 128], BF16, 2)
        kn_r = Ring(P, [128, 16], BF16, 2)
        vn_r = Ring(P, [16, 128], BF16, 2)
        pts_r = Ring(P, [128, 2, NPB, 16], BF16, 2)
        ptn_r = Ring(P, [16, 2, 16], BF16, 2)
        psQ = Ring(P, [128, 512], F32, 1, psum=True)
        for s in range(G.nseq):
            c0 = s * TSq
            for hd in range(4):
                kc, kcb = kc_r.next()
                P.op("gpsimd", I("dma_start", out=kc[:, :], in_=ckT_d[e_, s, hd]), writes=[kcb], dma=P.dq())
                vc, vcb = vc_r.next()
                P.op("gpsimd", I("dma_start",
                    out=vc[:, :, :], in_=cv_d[e_, s, :, hd, :].rearrange("(j p) e -> p j e", p=128)), writes=[vcb], dma=P.dq())
                kn, knb = kn_r.next()
                P.op("sync", I("dma_start", out=kn[:, :], in_=kTs["s"][hd * 128:(hd + 1) * 128, c0:c0 + TSq]),
                     writes=[knb], dma=P.dq())
                vn, vnb = vn_r.next()
                P.op("sync", I("dma_start", out=vn[:, :], in_=vtm["s"][c0:c0 + TSq, hd * 128:(hd + 1) * 128]),
                     writes=[vnb], dma=P.dq())
                qt, qtb = qt_r.next()
                P.op("sync", I("dma_start", out=qt[:, :TSq], in_=qT["s"][hd * 128:(hd + 1) * 128, c0:c0 + TSq]),
                     writes=[qtb], dma=P.dq())
                pts, ptsb = pts_r.next()
                ptn, ptnb = ptn_r.next()
                psn, psnb = psQ.next()
                for m in range(2):
                    ps, psb = psS.next()
                    for j in range(NPB):
                        P.op("tensor", I("matmul",
                            ps[:, j * 16:(j + 1) * 16], kc[64 * m:64 * m + 64, j * 128:(j + 1) * 128], qt[64 * m:64 * m + 64, :TSq],
                            start=True, stop=True), reads=[kcb, qtb], writes=[psb])
                    P.op("scalar", I("activation",
                        out=pts[:, m, :, :], in_=ps[:, :NPB * 16].rearrange("p (j q) -> p j q", q=16), func=AF.Exp, scale=scale),
                        reads=[psb], writes=[ptsb])
                    P.op("tensor", I("matmul", psn[:16, m * 16:(m + 1) * 16], kn[64 * m:64 * m + 64, :], qt[64 * m:64 * m + 64, :TSq],
                                                           start=True, stop=True), reads=[knb, qtb], writes=[psnb])
                P.op("scalar", I("activation", out=ptn[:, :, :], in_=psn[:16, 0:32].rearrange("p (m q) -> p m q", q=16),
                                                      func=AF.Exp, scale=scale), reads=[psnb], writes=[ptnb])
                o_ = [psO[m].next() for m in range(2)]
                d_ = [psD[m].next() for m in range(2)]
                for m in range(2):
                    for j in range(NPB):
                        P.op("tensor", I("matmul", o_[m][0][:, :TSq], vc[:, j, :], pts[:, m, j, :], start=(j == 0), stop=False),
                             reads=[vcb, ptsb], writes=[o_[m][1]])
                    P.op("tensor", I("matmul", o_[m][0][:, :TSq], vn[:, :], ptn[:, m, :], start=False, stop=True),
                         reads=[vnb, ptnb], writes=[o_[m][1]])
                    for j in range(NPB):
                        P.op("tensor", I("matmul", d_[m][0][:, :TSq], ones_b[:], pts[:, m, j, :], start=(j == 0), stop=False),
                             reads=[cB, ptsb], writes=[d_[m][1]])
                    P.op("tensor", I("matmul", d_[m][0][:, :TSq], ones_b[0:16, :], ptn[:, m, :], start=False, stop=True),
                         reads=[cB, ptnb], writes=[d_[m][1]])
                finish([o_[0][0], o_[1][0]], [o_[0][1], o_[1][1]], [d_[0][0], d_[1][0]], [d_[0][1], d_[1][1]], TSq,
                       512 + hd * 128, G, c0)
        P.end_phase()

    def phase_out_xa(l, wout_src):
        P.begin_phase()
        wo_m = P.sb([128, KC, D], BF16)
        wq = P.sb([128, KC, D], BF16)
        wo = P.sb([128, KC, D], BF16)
        wB = [Buf(), Buf(), Buf()]
        load_w(wo_m[:], wout_src, wB[0])
        load_w(wq[:], xa_w_d["wq"][l], wB[1])
        load_w(wo[:], xa_w_d["wo"][l], wB[2])
        mk_r = Ring(P, [128, KC, NMEM], BF16, 2)
        mv_r = Ring(P, [128, NMEM // 128, D], BF16, 2)
        xr = Ring(P, [128, KC, 512], F32, 2)
        mr = Ring(P, [128, KC, 512], BF16, 2)
        hr = Ring(P, [128, KC, 512], BF16, 1)
        qr = Ring(P, [128, KC, 512], BF16, 1)
        otr = Ring(P, [128, KC, 512], BF16, 1)
        sqr = Ring(P, [128, KC, 512], BF16, 1)
        rr = Ring(P, [128, 512], F32, 2)
        ptr = Ring(P, [128, NMEM // 128, 512], BF16, 2)
        psr = Ring(P, [128, 512], F32, 4, psum=True)
        pso = Ring(P, [128, 512], F32, 3, psum=True)
        NKB = NMEM // 128
        for G in groups:
            nmax = 512 if G.name == "p" else G.T
            mk = mv = None
            for (s, t0, n, c0) in tiles_of(G, nmax):
                if G.name == "p":
                    if t0 == 0:
                        mk, mkb = mk_r.next()
                        P.op("sync", I("dma_start", out=mk[:], in_=mkT_s[l]), writes=[mkb], dma=P.dq())
                        mv, mvb = mv_r.next()
                        P.op("sync", I("dma_start", out=mv[:], in_=mv_s[l].rearrange("(j p) d -> p j d", p=128)),
                             writes=[mvb], dma=P.dq())
                else:
                    mk, mkb = mk_r.next()
                    P.op("gpsimd", I("dma_start", out=mk[:], in_=cmkT_d[l, s]), writes=[mkb], dma=P.dq())
                    mv, mvb = mv_r.next()
                    P.op("gpsimd", I("dma_start", out=mv[:], in_=cmv_d[l, s].rearrange("(j p) d -> p j d", p=128)),
                         writes=[mvb], dma=P.dq())
                xt, xb = xr.next()
                P.op("sync", I("dma_start", out=xt[:, :, :n], in_=fm(xT[G.name])[:, :, c0:c0 + n]),
                     writes=[xb], dma=P.dq())
                mt, mb = mr.next()
                P.op("sync", I("dma_start", out=mt[:, :, :n], in_=fm(mixT[G.name])[:, :, c0:c0 + n]),
                     writes=[mb], dma=P.dq())

                def epi_add(fo, ps, psb, xt=xt, xb=xb, n=n):
                    P.op("vector", I("tensor_tensor", out=xt[:, fo, :n], in0=ps[:, :n], in1=xt[:, fo, :n], op=ALU.add),
                         reads=[psb, xb], writes=[xb])
                linear(wo_m, wB[0], mt, mb, n, 0, KC, psr, epi_add)
                h, hb = hr.next()
                norm_tile(xt, xb, n, pvc("norm_xa_g", l), h, hb, rr, sqr, psr)
                q, qb = qr.next()

                def epi_q(fo, ps, psb, q=q, qb=qb, n=n):
                    P.op("scalar", I("copy", out=q[:, fo, :n], in_=ps[:, :n]), reads=[psb], writes=[qb])
                linear(wq, wB[1], h, hb, n, 0, KC, psr, epi_q)
                ot, otb = otr.next()
                for hh in range(4):
                    pt, ptb = ptr.next()
                    for kb in range(NKB):
                        ps, psb = psr.next()
                        for dc in range(2):
                            P.op("tensor", I("matmul",
                                ps[:, :n], mk[:, hh * 2 + dc, kb * 128:(kb + 1) * 128], q[:, hh * 2 + dc, :n], start=(dc == 0), stop=(dc == 1)),
                                reads=[mkb, qb], writes=[psb])
                        P.op("scalar", I("activation", out=pt[:, kb, :n], in_=ps[:, :n], func=AF.Exp, scale=1.0 / 16),
                             reads=[psb], writes=[ptb])
                    dn, dnb = pso.next()
                    for kb in range(NKB):
                        P.op("tensor", I("matmul", dn[:, :n], ones_b[:], pt[:, kb, :n], start=(kb == 0), stop=(kb == NKB - 1)),
                             reads=[cB, ptb], writes=[dnb])
                    rd, rdb = rr.next()
                    P.op("vector", I("reciprocal", out=rd[:, :n], in_=dn[:, :n]), reads=[dnb], writes=[rdb])
                    for dc in range(2):
                        po, pob = pso.next()
                        for kb in range(NKB):
                            P.op("tensor", I("matmul",
                                po[:, :n], mv[:, kb, hh * 256 + dc * 128:hh * 256 + (dc + 1) * 128], pt[:, kb, :n], start=(kb == 0), stop=(kb == NKB - 1)),
                                reads=[mvb, ptb], writes=[pob])
                        P.op("vector", I("tensor_tensor",
                            out=ot[:, hh * 2 + dc, :n], in0=po[:, :n], in1=rd[:, :n], op=ALU.mult), reads=[pob, rdb], writes=[otb])
                linear(wo, wB[2], ot, otb, n, 0, KC, psr, epi_add)
                P.op("sync", I("dma_start", out=fm(xT[G.name])[:, :, c0:c0 + n], in_=xt[:, :, :n]),
                     reads=[xb], dma=P.dq())
        P.end_phase()

    def phase_ffn(l, final):
        P.begin_phase()
        NTF = 256
        wg = P.sb([128, KC, DFF], BF16)
        wu = P.sb([128, KC, DFF], BF16)
        wd = P.sb([128, FC, D], BF16)
        wB = [Buf(), Buf(), Buf()]
        load_w(wg[:], ffn_g_d[l], wB[0])
        load_w(wu[:], ffn_u_d[l], wB[1])
        load_w(wd[:], ffn_d_d[l], wB[2])
        xr = Ring(P, [128, KC, NTF], F32, 2)
        hr = Ring(P, [128, KC, NTF], BF16, 1)
        sqr = Ring(P, [128, KC, NTF], BF16, 1)
        ar = Ring(P, [128, FC, NTF], BF16, 1)
        rr = Ring(P, [128, NTF], F32, 2)
        sg = Ring(P, [128, NTF], F32, 3)
        yr = Ring(P, [128, KC, NTF], F32, 1)
        psr = Ring(P, [128, 512], F32, 6, psum=True)
        for G in groups:
            for (s, t0, n, c0) in tiles_of(Group(G.name, 1, G.NT), NTF):
                xt, xb = xr.next()
                P.op("sync", I("dma_start", out=xt[:, :, :n], in_=fm(xT[G.name])[:, :, c0:c0 + n]),
                     writes=[xb], dma=P.dq())
                h, hb = hr.next()
                norm_tile(xt, xb, n, pvc("norm_ffn_g", l), h, hb, rr, sqr, psr)
                act, ab = ar.next()
                for fo in range(FC):
                    pg, pgb = psr.next()
                    pu, pub = psr.next()
                    for kc in range(KC):
                        P.op("tensor", I("matmul", pg[:, :n], wg[:, kc, fo * 128:(fo + 1) * 128], h[:, kc, :n],
                                                                                 start=(kc == 0), stop=(kc == KC - 1)), reads=[wB[0], hb], writes=[pgb])
                    for kc in range(KC):
                        P.op("tensor", I("matmul", pu[:, :n], wu[:, kc, fo * 128:(fo + 1) * 128], h[:, kc, :n],
                                                                                 start=(kc == 0), stop=(kc == KC - 1)), reads=[wB[1], hb], writes=[pub])
                    s_, sb_ = sg.next()
                    P.op("scalar", I("activation", out=s_[:, :n], in_=pg[:, :n], func=AF.Silu), reads=[pgb], writes=[sb_])
                    P.op("vector", I("tensor_tensor", out=act[:, fo, :n], in0=pu[:, :n], in1=s_[:, :n], op=ALU.mult),
                         reads=[pub, sb_], writes=[ab])

                def epi_add(fo, ps, psb, xt=xt, xb=xb, n=n):
                    P.op("vector", I("tensor_tensor", out=xt[:, fo, :n], in0=ps[:, :n], in1=xt[:, fo, :n], op=ALU.add),
                         reads=[psb, xb], writes=[xb])
                linear(wd, wB[2], act, ab, n, 0, KC, psr, epi_add, kc_n=FC)
                if not final:
                    P.op("sync", I("dma_start", out=fm(xT[G.name])[:, :, c0:c0 + n], in_=xt[:, :, :n]),
                         reads=[xb], dma=P.dq())
                else:
                    y, yb = yr.next()
                    norm_tile(xt, xb, n, pvc("final_norm_g"), y, yb, rr, sqr, psr)
                    P.op("sync", I("dma_start", out=fm(yT_o[G.name])[:, :, c0:c0 + n], in_=y[:, :, :n]),
                         reads=[yb], dma=P.dq())
        P.end_phase()

    def phase_rwkv(l):
        o_ = l // 2
        P.begin_phase()
        W = {}
        WB = {}
        for k in ("wr", "wk", "wv"):
            W[k] = P.sb([128, KC, D], BF16)
            WB[k] = Buf()
            load_w(W[k][:], rw_w_d[k][o_], WB[k])
        for k, nn in (("w1", 64), ("a1", 64), ("g1", 160), ("v1", 32)):
            if k == "v1" and o_ == 0:
                continue
            W[k] = P.sb([128, KC, nn], BF16)
            WB[k] = Buf()
            load_w(W[k][:], rw_l1_d[k][o_ if k != "v1" else o_ - 1], WB[k])
        for k, nn in (("w2", 64), ("a2", 64), ("v2", 32)):
            if k == "v2" and o_ == 0:
                continue
            W[k] = P.sb([nn, 1, D], BF16)
            WB[k] = Buf()
            load_w(W[k][:, 0, :], rw_l2_d[k][o_ if k != "v2" else o_ - 1], WB[k])
        W["g2a"] = P.sb([128, 1, D], BF16)
        W["g2b"] = P.sb([32, 1, D], BF16)
        WB["g2a"] = Buf()
        WB["g2b"] = Buf()
        load_w(W["g2a"][:, 0, :], rw_l2_d["g2"][o_, 0:128, :], WB["g2a"])
        load_w(W["g2b"][:, 0, :], rw_l2_d["g2"][o_, 128:160, :], WB["g2b"])

        NB = 128
        f32t = lambda k=1: P.sb([128, KC, NB + k - 1], F32)
        xh = f32t(2); xhB = Buf()
        hx = f32t(2); hxB = Buf()
        xx = f32t(2); xxB = Buf()
        mixr = Ring(P, [128, KC, NB], BF16, 2)
        sqr = Ring(P, [128, KC, NB + 1], BF16, 1)
        rr = Ring(P, [128, NB + 1], F32, 2)
        psr = Ring(P, [128, 512], F32, 5, psum=True)
        psY2 = [P.ps([128, 4, 128], F32), P.ps([128, 4, 128], F32)]; psYB = Buf()
        psSt = P.ps([128, KC, 64], F32); psStB = Buf()
        r_ = f32t(); k_ = f32t(); v_ = f32t(); sg_ = f32t(); a_ = f32t(); g_ = f32t()
        rB, kB, vB, sgB, aB, gB = [Buf() for _ in range(6)]
        lh = {k: P.sb([128, 2, NB], BF16) for k in ("w", "a", "g", "v")}
        lhB = {k: Buf() for k in lh}
        kk = f32t(); kkB = Buf()
        km = f32t(); kmB = Buf()
        bb_ = f32t(); bbB = Buf()
        t1 = f32t(); t1B = Buf()
        t2 = f32t(); t2B = Buf()
        vf = f32t(); vfB = Buf()
        csA = xh; csB_ = xx
        Eout = f32t(); EoutB = Buf()
        PC = P.sb([128, KC, 2], F32); PCB = Buf()
        bon = f32t(); bonB = Buf()
        AR = P.sb([128, KC, 2, NB], BF16); ARB = Buf()
        Bt = P.sb([128, KC, NB], BF16); Kt = P.sb([128, KC, NB], BF16); BKB = Buf()
        R32 = f32t(); R32B = Buf()
        tmf = Ring(P, [128, KC, NB], F32, 2)
        Z = [P.sb([128, 16, 128], BF16) for _ in range(2)]
        ZB = [[Buf() for _ in range(16)] for _ in range(2)]
        Vtm = P.sb([128, KC, 128], BF16); VtmB = Buf()
        BPtm = P.sb([128, KC, 128], BF16); BPB = Buf()
        KPtm = P.sb([128, KC, 128], BF16); KPB = Buf()
        XAs = Ring(P, [128, 2, NB], BF16, 8)
        XBs = Ring(P, [128, 2, NB], BF16, 8)
        MLr = Ring(P, [128, 2, NB], BF16, 10)
        Ls = Ring(P, [128, NB], BF16, 8)
        v16 = lambda t: t[:, :, 0:NB].rearrange("p k (a v) -> p (k a) v", v=64)
        v42 = lambda t: t[:, :, 0:NB].rearrange("p k (c v) -> p k c v", v=64)
        PhiT = v42(sg_); PhiB = sgB
        Psi = v42(t1); PsiB = t1B
        Om = P.sb([128, KC, NB], BF16); OmB = Buf()
        S16 = P.sb([128, KC, 64], BF16); S16B = Buf()
        Y0 = bb_; Y0B = bbB
        S = [P.sb([128, KC, 64], F32) for _ in range(2)]
        SB_ = [Buf(), Buf()]
        Ytm = r_; YB = rB
        st1 = P.sb([128, 16], F32); st2 = P.sb([128, 16], F32); stB = Buf()
        yc = k_; ycB = kB
        ysq = a_; ysqB = aB
        yo = Ring(P, [128, KC, NB], BF16, 2)
        yt32 = kk; yt32B = kkB

        mu0 = pvc("rw_mu", o_ * 6)
        lgc = pvc("rw_lnx_g", o_)
        lbc = pvc("rw_lnx_b", o_)

        def small_lin(Wt, wb, src, sb, nrows, n, ps, psb, f0, start=True, stop=True):
            P.op("tensor", I("matmul", ps[:, :n], Wt[:nrows, 0, f0:f0 + 128], src[:nrows, :n], start=start, stop=stop),
                 reads=[wb, sb], writes=[psb])

        cur = 0
        for G in groups:
            C = 64 if G.name == "p" else G.T
            for (s, t0, n, c0) in tiles_of(G, NB):
                nch = n // C
                nlog = int(math.log2(C))
                first = (t0 == 0)
                if first:
                    P.op("sync", I("dma_start", out=xh[:, :, 1:n + 1], in_=fm(xT[G.name])[:, :, c0:c0 + n]),
                         writes=[xhB], dma=P.dq())
                    P.op("gpsimd", I("memset", xh[:, :, 0:1], 1.0), writes=[xhB])
                else:
                    P.op("sync", I("dma_start", out=xh[:, :, 0:n + 1], in_=fm(xT[G.name])[:, :, c0 - 1:c0 + n]),
                         writes=[xhB], dma=P.dq())
                norm_tile(xh, xhB, n + 1, pvc("norm_mix_g", l), hx, hxB, rr, sqr, psr)
                if first:
                    if G.name == "p":
                        P.op("gpsimd", I("memset", hx[:, :, 0:1], 0.0), reads=[hxB], writes=[hxB])
                    else:
                        P.op("sync", I("dma_start", out=hx[:, :, 0:1], in_=sshift_d[o_, s].unsqueeze(2), allow_slow_non_contiguous=True), reads=[hxB], writes=[hxB], dma=P.dq())
                if t0 + n == G.T:
                    P.op("sync", I("dma_start", out=shift_o[G.name][o_, s].unsqueeze(2), in_=hx[:, :, n:n + 1], allow_slow_non_contiguous=True), reads=[hxB], dma=P.dq())
                P.op("vector", I("tensor_tensor", out=xx[:, :, :n], in0=hx[:, :, 0:n], in1=hx[:, :, 1:n + 1], op=ALU.subtract),
                     reads=[hxB], writes=[xxB])

                def bc(c0_):
                    return pv[:, c0_:c0_ + KC].unsqueeze(2).to_broadcast([128, KC, n])

                def mix(i, n=n):
                    m, mb = mixr.next()
                    tt, ttB = (t1, t1B) if i % 2 == 0 else (t2, t2B)
                    P.op("gpsimd", I("tensor_tensor", out=tt[:, :, :n], in0=xx[:, :, :n], in1=bc(mu0 + i * KC), op=ALU.mult),
                         reads=[xxB, cB], writes=[ttB])
                    P.op("vector", I("tensor_tensor", out=m[:, :, :n], in0=tt[:, :, :n], in1=hx[:, :, 1:n + 1], op=ALU.add),
                         reads=[ttB, hxB], writes=[mb])
                    return m, mb

                def epi_copy(dst, dB, n=n):
                    def f(fo, ps, psb):
                        P.op("scalar", I("copy", out=dst[:, fo, :n], in_=ps[:, :n]), reads=[psb], writes=[dB])
                    return f

                def lora_hidden(m, mb, key, nn, func, n=n):
                    for ci, (r0, rn) in enumerate([(0, min(nn, 128))] + ([(128, nn - 128)] if nn > 128 else [])):
                        ps, psb = psr.next()
                        for kc in range(KC):
                            P.op("tensor", I("matmul", ps[:rn, :n], W[key][:, kc, r0:r0 + rn], m[:, kc, :n],
                                                                                   start=(kc == 0), stop=(kc == KC - 1)), reads=[WB[key], mb], writes=[psb])
                        P.op("scalar", I("activation", out=lh[key[0]][:rn, ci, :n], in_=ps[:rn, :n], func=func),
                             reads=[psb], writes=[lhB[key[0]]])

                m, mb = mix(0)
                linear(W["wr"], WB["wr"], m, mb, n, 0, KC, psr, epi_copy(r_, rB))
                m, mb = mix(1)
                lora_hidden(m, mb, "w1", 64, AF.Tanh)
                w0c = pvc("rw_w0", o_)
                for fo in range(KC):
                    ps, psb = psr.next()
                    small_lin(W["w2"], WB["w2"], lh["w"][:, 0, :], lhB["w"], 64, n, ps, psb, fo * 128)
                    P.op("scalar", I("activation", out=sg_[:, fo, :n], in_=ps[:, :n], func=AF.Sigmoid,
                                                                        bias=pv[:, w0c + fo:w0c + fo + 1]), reads=[psb, cB], writes=[sgB])
                m, mb = mix(2)
                linear(W["wk"], WB["wk"], m, mb, n, 0, KC, psr, epi_copy(k_, kB))
                m, mb = mix(3)
                linear(W["wv"], WB["wv"], m, mb, n, 0, KC, psr, epi_copy(v_, vB))
                if o_ == 0:
                    P.op("sync", I("dma_start", out=fm(vfirst[G.name])[:, :, c0:c0 + n], in_=v_[:, :, :n]), reads=[vB], dma=P.dq())
                else:
                    P.op("sync", I("dma_start", out=vf[:, :, :n], in_=fm(vfirst[G.name])[:, :, c0:c0 + n]), writes=[vfB], dma=P.dq())
                    lora_hidden(m, mb, "v1", 32, AF.Copy)
                    v0c = pvc("rw_v0", 0)
                    for fo in range(KC):
                        ps, psb = psr.next()
                        small_lin(W["v2"], WB["v2"], lh["v"][:, 0, :], lhB["v"], 32, n, ps, psb, fo * 128)
                        P.op("scalar", I("activation", out=t1[:, fo, :n], in_=ps[:, :n], func=AF.Sigmoid,
                                                                            bias=pv[:, v0c + fo:v0c + fo + 1]), reads=[psb, cB], writes=[t1B])
                    P.op("vector", I("tensor_tensor", out=vf[:, :, :n], in0=vf[:, :, :n], in1=v_[:, :, :n], op=ALU.subtract),
                         reads=[vfB, vB], writes=[vfB])
                    P.op("vector", I("tensor_tensor", out=vf[:, :, :n], in0=vf[:, :, :n], in1=t1[:, :, :n], op=ALU.mult),
                         reads=[vfB, t1B], writes=[vfB])
                    P.op("vector", I("tensor_tensor", out=v_[:, :, :n], in0=v_[:, :, :n], in1=vf[:, :, :n], op=ALU.add),
                         reads=[vfB, vB], writes=[vB])
                m, mb = mix(4)
                lora_hidden(m, mb, "a1", 64, AF.Copy)
                a0c = pvc("rw_a0", o_)
                for fo in range(KC):
                    ps, psb = psr.next()
                    small_lin(W["a2"], WB["a2"], lh["a"][:, 0, :], lhB["a"], 64, n, ps, psb, fo * 128)
                    P.op("scalar", I("activation", out=a_[:, fo, :n], in_=ps[:, :n], func=AF.Sigmoid,
                                                                        bias=pv[:, a0c + fo:a0c + fo + 1]), reads=[psb, cB], writes=[aB])
                m, mb = mix(5)
                lora_hidden(m, mb, "g1", 160, AF.Sigmoid)
                for fo in range(KC):
                    ps, psb = psr.next()
                    small_lin(W["g2a"], WB["g2a"], lh["g"][:, 0, :], lhB["g"], 128, n, ps, psb, fo * 128, True, False)
                    small_lin(W["g2b"], WB["g2b"], lh["g"][:, 1, :], lhB["g"], 32, n, ps, psb, fo * 128, False, True)
                    P.op("scalar", I("copy", out=g_[:, fo, :n], in_=ps[:, :n]), reads=[psb], writes=[gB])
                if "r1" in _SKIP:
                    continue
                kkc = pvc("rw_k_k", o_)
                kac = pvc("rw_k_a", o_)
                rkc = pvc("rw_r_k", o_)
                sqb_, sqbB = sqr.next()
                P.op("gpsimd", I("tensor_tensor", out=kk[:, :, :n], in0=k_[:, :, :n], in1=bc(kkc), op=ALU.mult), reads=[kB, cB], writes=[kkB])
                P.op("scalar", I("activation", out=sqb_[:, :, :n], in_=kk[:, :, :n], func=AF.Square), reads=[kkB], writes=[sqbB])
                for g4 in range(2):
                    ps, psb = psr.next()
                    for k4 in range(4):
                        kc = g4 * 4 + k4
                        P.op("tensor", I("matmul", ps[:, k4 * 128:k4 * 128 + n], bones_b[:], sqb_[:, kc, :n], start=True, stop=True),
                             reads=[cB, sqbB], writes=[psb])
                    P.op("scalar", I("activation", out=t1[:, g4 * 4:g4 * 4 + 4, :n], in_=ps[:, :].rearrange("p (k t) -> p k t", t=128)[:, :, :n],
                                     func=AF.Sqrt), reads=[psb], writes=[t1B])
                P.op("vector", I("tensor_scalar", out=t1[:, :, :n], in0=t1[:, :, :n], scalar1=1e-12, scalar2=1.0, op0=ALU.max, op1=ALU.mult),
                     reads=[t1B], writes=[t1B])
                P.op("vector", I("reciprocal", out=t1[:, :, :n], in_=t1[:, :, :n]), reads=[t1B], writes=[t1B])
                P.op("vector", I("tensor_tensor", out=kk[:, :, :n], in0=kk[:, :, :n], in1=t1[:, :, :n], op=ALU.mult),
                     reads=[kkB, t1B], writes=[kkB])
                P.op("vector", I("tensor_scalar", out=t2[:, :, :n], in0=a_[:, :, :n], scalar1=-1.0, scalar2=1.0, op0=ALU.add, op1=ALU.mult),
                     reads=[aB], writes=[t2B])
                P.op("gpsimd", I("tensor_tensor", out=t2[:, :, :n], in0=t2[:, :, :n], in1=bc(kac), op=ALU.mult), reads=[t2B, cB], writes=[t2B])
                P.op("vector", I("scalar_tensor_tensor", out=km[:, :, :n], in0=t2[:, :, :n], scalar=1.0, in1=k_[:, :, :n],
                                 op0=ALU.add, op1=ALU.mult), reads=[t2B, kB], writes=[kmB])
                P.op("gpsimd", I("tensor_tensor", out=bb_[:, :, :n], in0=kk[:, :, :n], in1=a_[:, :, :n], op=ALU.mult),
                     reads=[kkB, aB], writes=[bbB])
                P.op("gpsimd", I("tensor_tensor", out=t2[:, :, :n], in0=r_[:, :, :n], in1=bc(rkc), op=ALU.mult), reads=[rB, cB, t2B], writes=[t2B])
                sqb2, sqb2B = sqr.next()
                P.op("vector", I("tensor_tensor", out=sqb2[:, :, :n], in0=t2[:, :, :n], in1=km[:, :, :n], op=ALU.mult), reads=[t2B, kmB], writes=[sqb2B])
                for g4 in range(2):
                    ps, psb = psr.next()
                    for k4 in range(4):
                        kc = g4 * 4 + k4
                        P.op("tensor", I("matmul", ps[:, k4 * 128:k4 * 128 + n], bones_b[:], sqb2[:, kc, :n], start=True, stop=True),
                             reads=[cB, sqb2B], writes=[psb])
                    P.op("vector", I("tensor_tensor", out=bon[:, g4 * 4:g4 * 4 + 4, :n], in0=ps[:, :].rearrange("p (k t) -> p k t", t=128)[:, :, :n],
                                     in1=v_[:, g4 * 4:g4 * 4 + 4, :n], op=ALU.mult), reads=[psb, vB], writes=[bonB])
                P.op("gpsimd", I("tensor_tensor", out=bon[:, :, :n], in0=bon[:, :, :n], in1=bc(lbc), op=ALU.add), reads=[bonB, cB], writes=[bonB])
                if "r2" in _SKIP:
                    continue
                def v4(t):
                    return t[:, :, :n].rearrange("p k (c t) -> p k c t", t=C)
                src, srcB = sg_, sgB
                sh = 1
                i = 0
                while sh < C:
                    dst, dstB = (csA, xhB) if i % 2 == 0 else (csB_, xxB)
                    P.op("vector", I("tensor_tensor", out=v4(dst)[:, :, :, sh:C], in0=v4(src)[:, :, :, sh:C], in1=v4(src)[:, :, :, 0:C - sh], op=ALU.add),
                         reads=[srcB], writes=[dstB])
                    P.op("scalar", I("copy", out=v4(dst)[:, :, :, 0:sh], in_=v4(src)[:, :, :, 0:sh]), reads=[srcB], writes=[dstB])
                    src, srcB = dst, dstB
                    sh *= 2
                    i += 1
                cs, csBuf = src, srcB
                Ein, EinB = hx, hxB
                Eprev, EprevB = t2, t2B
                Eend, EendB = vf, vfB
                P.op("scalar", I("activation", out=Ein[:, :, :n], in_=cs[:, :, :n], func=AF.Exp, scale=-CDEC), reads=[csBuf], writes=[EinB])
                P.op("scalar", I("activation", out=Eout[:, :, :n], in_=cs[:, :, :n], func=AF.Exp, scale=CDEC), reads=[csBuf], writes=[EoutB])
                P.op("vector", I("tensor_tensor", out=t1[:, :, :n], in0=cs[:, :, :n], in1=sg_[:, :, :n], op=ALU.subtract),
                     reads=[csBuf, sgB], writes=[t1B])
                P.op("scalar", I("activation", out=Eprev[:, :, :n], in_=t1[:, :, :n], func=AF.Exp, scale=-CDEC), reads=[t1B], writes=[EprevB])
                P.op("vector", I("tensor_tensor", out=v4(t1), in0=v4(cs), in1=v4(cs)[:, :, :, C - 1:C].to_broadcast([128, KC, nch, C]),
                                 op=ALU.subtract), reads=[csBuf], writes=[t1B])
                P.op("scalar", I("activation", out=Eend[:, :, :n], in_=t1[:, :, :n], func=AF.Exp, scale=CDEC), reads=[t1B], writes=[EendB])
                P.op("scalar", I("activation", out=PC[:, :, :nch], in_=v4(cs)[:, :, :, C - 1], func=AF.Exp, scale=-CDEC), reads=[csBuf], writes=[PCB])
                if "r3" in _SKIP:
                    continue
                P.op("vector", I("tensor_tensor", out=R32[:, :, :n], in0=r_[:, :, :n], in1=Ein[:, :, :n], op=ALU.mult), reads=[rB, EinB], writes=[R32B])
                P.op("gpsimd", I("tensor_copy", out=AR[:, :, 1, :n], in_=R32[:, :, :n]), reads=[R32B], writes=[ARB])
                ta_, taB = tmf.next()
                P.op("vector", I("scalar_tensor_tensor", out=ta_[:, :, :n], in0=kk[:, :, :n], scalar=-1.0, in1=Eprev[:, :, :n],
                                                                 op0=ALU.mult, op1=ALU.mult), reads=[kkB, EprevB], writes=[taB])
                P.op("gpsimd", I("tensor_copy", out=AR[:, :, 0, :n], in_=ta_[:, :, :n]), reads=[taB], writes=[ARB])
                P.op("vector", I("tensor_tensor", out=Bt[:, :, :n], in0=bb_[:, :, :n], in1=Eout[:, :, :n], op=ALU.mult), reads=[bbB, EoutB], writes=[BKB])
                P.op("gpsimd", I("tensor_tensor", out=Kt[:, :, :n], in0=km[:, :, :n], in1=Eout[:, :, :n], op=ALU.mult), reads=[kmB, EoutB], writes=[BKB])
                tb_, tbB = tmf.next()
                P.op("vector", I("tensor_tensor", out=tb_[:, :, :n], in0=bb_[:, :, :n], in1=Eend[:, :, :n], op=ALU.mult), reads=[bbB, EendB], writes=[tbB])

                def transp(src, sB, dst_fn, dB_fn, n=n):
                    for fc in range(KC):
                        ps, psb = psr.next()
                        P.op("tensor", I("transpose", out=ps[:n, 0:128], in_=src[:, fc, :n], identity=ident_f[:]),
                             reads=[sB, cB], writes=[psb])
                        dst_fn(fc, ps, psb)
                zc = cur

                def dst_A(fc, ps, psb, n=n):
                    P.op("vector", I("tensor_copy", out=Z[zc][:n, 2 * fc:2 * fc + 2, 0:64], in_=ps[:n, 0:128].rearrange("p (h k) -> p h k", k=64)),
                         reads=[psb], writes=[ZB[zc][2 * fc], ZB[zc][2 * fc + 1]])
                transp(ta_, taB, dst_A, None)

                def dst_simple(dst, dB, n=n):
                    def f(fc, ps, psb):
                        P.op("scalar", I("copy", out=dst[:n, fc, :], in_=ps[:n, 0:128]), reads=[psb], writes=[dB])
                    return f
                transp(tb_, tbB, dst_simple(BPtm, BPB), None)
                tc_, tcB = tmf.next()
                P.op("vector", I("tensor_tensor", out=tc_[:, :, :n], in0=km[:, :, :n], in1=Eend[:, :, :n], op=ALU.mult), reads=[kmB, EendB], writes=[tcB])
                transp(tc_, tcB, dst_simple(KPtm, KPB), None)
                transp(v_, vB, dst_simple(Vtm, VtmB), None)

                if "r4" in _SKIP:
                    continue
                mk3_ = mk3 if G.name == "p" else mk3s
                mkL_ = mkL if G.name == "p" else mkLs
                HG = 4

                def stage_a(hd):
                    hb0 = 64 * (hd % 2)
                    fc = hd // 2
                    hs = slice(hb0, hb0 + 64)
                    psA, psAb = psr.next()
                    P.op("tensor", I("matmul", psA[:n, 0:2 * n].rearrange("p (a t) -> p a t", a=2), Bt[hs, fc, :n], AR[hs, fc, :, :n], start=True, stop=True),
                         reads=[BKB, ARB], writes=[psAb])
                    xa_, xaB = XAs.next()
                    P.op("vector", I("tensor_tensor", out=xa_[:n, :, :n], in0=psA[:n, 0:2 * n].rearrange("p (a t) -> p a t", a=2),
                                     in1=mk3_[:n, :, :n], op=ALU.mult), reads=[psAb, cB], writes=[xaB])
                    psB_, psBb = psr.next()
                    P.op("tensor", I("matmul", psB_[:n, 0:2 * n].rearrange("p (a t) -> p a t", a=2), Kt[hs, fc, :n], AR[hs, fc, :, :n], start=True, stop=True),
                         reads=[BKB, ARB], writes=[psBb])
                    xb_, xbB = XBs.next()
                    P.op("vector", I("tensor_tensor", out=xb_[:n, :, :n], in0=psB_[:n, 0:2 * n].rearrange("p (a t) -> p a t", a=2),
                                     in1=mk3_[:n, :, :n], op=ALU.mult), reads=[psBb, cB], writes=[xbB])
                    psC, psCb = psr.next()
                    P.op("tensor", I("matmul", psC[:n, 0:n], AR[hs, fc, 0, :n], Bt[hs, fc, :n], start=True, stop=True),
                         reads=[BKB, ARB], writes=[psCb])
                    L0, L0B = Ls.next()
                    P.op("vector", I("tensor_tensor", out=L0[:n, :n], in0=psC[:n, 0:n], in1=mkL_[:n, :n], op=ALU.mult),
                         reads=[psCb, cB], writes=[L0B])
                    psG, psGb = psr.next()
                    P.op("tensor", I("matmul", psG[:n, 0:64], xb_[:n, 0, :n], Vtm[:n, fc, hs], start=True, stop=True),
                         reads=[xbB, VtmB], writes=[psGb])
                    P.op("scalar", I("copy", out=Z[zc][:n, hd, 64:128], in_=psG[:n, 0:64]), reads=[psGb], writes=[ZB[zc][hd]])
                    return dict(hd=hd, fc=fc, hs=hs, xa_=xa_, xaB=xaB, xb_=xb_, xbB=xbB, zi=zc,
                                Mj=xa_[:n, 0, :n], MjB=xaB, Lj=L0[:n, :n], LjB=L0B)

                def stage_d(st):
                    hd, fc, hs, xa_, xaB, xb_, xbB = st["hd"], st["fc"], st["hs"], st["xa_"], st["xaB"], st["xb_"], st["xbB"]
                    ZF = Z[st["zi"]]; ZFB = ZB[st["zi"]][hd]
                    psDs = []
                    for c in range(nch):
                        cs_ = slice(c * C, (c + 1) * C)
                        psD, psDb = psr.next()
                        psDs.append((psD, psDb))
                        P.op("tensor", I("matmul", psD[hs, c * 128:c * 128 + 64], ZF[cs_, hd, 0:64], BPtm[cs_, fc, hs], start=True, stop=True),
                             reads=[ZFB, BPB], writes=[psDb])
                        P.op("tensor", I("matmul", psD[hs, c * 128 + 64:c * 128 + 128], BPtm[cs_, fc, hs], ZF[cs_, hd, 64:128], start=True, stop=False),
                             reads=[ZFB, BPB], writes=[psDb])
                        P.op("tensor", I("matmul", psD[hs, c * 128 + 64:c * 128 + 128], KPtm[cs_, fc, hs], Vtm[cs_, fc, hs], start=False, stop=True),
                             reads=[KPB, VtmB], writes=[psDb])
                    for c in range(nch):
                        psD, psDb = psDs[c]
                        P.op("vector", I("scalar_tensor_tensor", out=PhiT[hs, fc, c, :], in0=ident_f[hs, hs], scalar=PC[hs, fc, c:c + 1],
                                         in1=psD[hs, c * 128:c * 128 + 64], op0=ALU.mult, op1=ALU.add), reads=[psDb, PCB, cB], writes=[PhiB])
                        P.op("vector", I("tensor_copy", out=Psi[hs, fc, c, :], in_=psD[hs, c * 128 + 64:c * 128 + 128]),
                             reads=[psDb], writes=[PsiB])
                    psO_, psOb = psr.next()
                    P.op("tensor", I("matmul", psO_[hs, 0:n], ZF[:n, hd, 0:64], xa_[:n, 1, :n], start=True, stop=True),
                         reads=[ZFB, xaB], writes=[psOb])
                    psY0, psY0b = psr.next()
                    P.op("tensor", I("matmul", psY0[hs, 0:n], ZF[:n, hd, 64:128], xa_[:n, 1, :n], start=True, stop=False),
                         reads=[ZFB, xaB], writes=[psY0b])
                    P.op("tensor", I("matmul", psY0[hs, 0:n], Vtm[:n, fc, hs], xb_[:n, 1, :n], start=False, stop=True),
                         reads=[xbB, VtmB], writes=[psY0b])
                    P.op("vector", I("tensor_tensor", out=Om[hs, fc, :n], in0=psO_[hs, 0:n], in1=R32[hs, fc, :n], op=ALU.add),
                         reads=[psOb, R32B], writes=[OmB])
                    P.op("scalar", I("copy", out=Y0[hs, fc, :n], in_=psY0[hs, 0:n]), reads=[psY0b], writes=[Y0B])

                nxt = [stage_a(hd) for hd in range(0, HG)]
                for g0 in range(0, 16, HG):
                    sts = nxt
                    for j in range(nlog):
                        pz = []
                        for st in sts:
                            psZ, psZb = psr.next()
                            P.op("tensor", I("matmul", psZ[:n, 0:128], st["Mj"], Z[st["zi"]][:n, st["hd"], :], start=True, stop=True),
                                 reads=[st["MjB"], ZB[st["zi"]][st["hd"]]], writes=[psZb])
                            pz.append((psZ, psZb))
                        for st, (psZ, psZb) in zip(sts, pz):
                            zi, hd = st["zi"], st["hd"]
                            P.op("vector", I("tensor_tensor", out=Z[1 - zi][:n, hd, :], in0=psZ[:n, 0:128], in1=Z[zi][:n, hd, :], op=ALU.add),
                                 reads=[psZb, ZB[zi][hd]], writes=[ZB[1 - zi][hd]])
                            st["zi"] = 1 - zi
                        if j < nlog - 1:
                            pq = []
                            for st in sts:
                                psS_, psSb = psr.next()
                                P.op("tensor", I("matmul", psS_[:n, 0:n], st["Lj"], st["Mj"], start=True, stop=True),
                                     reads=[st["MjB"], st["LjB"]], writes=[psSb])
                                if j < nlog - 2:
                                    P.op("tensor", I("matmul", psS_[:n, n:2 * n], st["Mj"], st["Lj"], start=True, stop=True),
                                         reads=[st["MjB"], st["LjB"]], writes=[psSb])
                                pq.append((psS_, psSb))
                            w_ = 2 if j < nlog - 2 else 1
                            for st, (psS_, psSb) in zip(sts, pq):
                                ml, mlB = MLr.next()
                                P.op("scalar", I("copy", out=ml[:n, 0:w_, :n], in_=psS_[:n, 0:w_ * n].rearrange("p (a t) -> p a t", a=w_)),
                                     reads=[psSb], writes=[mlB])
                                st["Mj"] = ml[:n, 0, :n]; st["MjB"] = mlB
                                st["Lj"] = ml[:n, 1, :n]; st["LjB"] = mlB
                    if g0 + HG < 16:
                        nxt = [stage_a(hd) for hd in range(g0 + HG, g0 + 2 * HG)]
                    for st in sts:
                        stage_d(st)
                if "r5" in _SKIP:
                    continue
                cur = zc
                if first:
                    if G.name == "p":
                        P.op("gpsimd", I("memset", S[0][:], 0.0), writes=[SB_[0]])
                    else:
                        P.op("sync", I("dma_start", out=S[0][:], in_=swkv_d[o_, s]), writes=[SB_[0]], dma=P.dq())
                    si = 0
                for c in range(nch):
                    cs_ = slice(c * C, (c + 1) * C)
                    P.op("scalar", I("copy", out=S16[:], in_=S[si][:]), reads=[SB_[si]], writes=[S16B])
                    for hd in range(16 if "cy" not in _SKIP else 0):
                        hb0 = 64 * (hd % 2); fc = hd // 2; hs = slice(hb0, hb0 + 64)
                        P.op("tensor", I("matmul", psY2[fc // 4][hs, fc % 4, cs_], S16[hs, fc, :], Om[hs, fc, cs_], start=True, stop=True),
                             reads=[OmB, S16B], writes=[psYB])
                    for hh_ in range(2 if "cy" not in _SKIP else 0):
                        hsl = slice(hh_ * 4, hh_ * 4 + 4)
                        P.op("vector", I("tensor_tensor", out=Ytm[:, hsl, cs_], in0=psY2[hh_][:, :, cs_], in1=Y0[:, hsl, cs_], op=ALU.add),
                             reads=[psYB, Y0B], writes=[YB])
                    if "cs" in _SKIP:
                        continue
                    for hd in range(16):
                        hb0 = 64 * (hd % 2); fc = hd // 2; hs = slice(hb0, hb0 + 64)
                        P.op("tensor", I("matmul", psSt[hs, fc, :], PhiT[hs, fc, c, :], S[si][hs, fc, :], start=True, stop=True),
                             reads=[PhiB, SB_[si]], writes=[psStB])
                    P.op("vector", I("tensor_tensor", out=S[1 - si][:, :, :], in0=psSt[:, :, :], in1=Psi[:, :, c, :], op=ALU.add),
                         reads=[psStB, PsiB], writes=[SB_[1 - si]])
                    si = 1 - si
                if t0 + n == G.T:
                    P.op("sync", I("dma_start", out=wkv_o[G.name][o_, s], in_=S[si][:]), reads=[SB_[si]], dma=P.dq())
                if "r6" in _SKIP:
                    continue
                yo_, yoB = yo.next()
                for fc in range(KC):
                    ps, psb = psr.next()
                    P.op("tensor", I("matmul", ps[:, :n], bones_f[:], Ytm[:, fc, :n], start=True, stop=True), reads=[cB, YB], writes=[psb])
                    P.op("vector", I("scalar_tensor_tensor", out=yc[:, fc, :n], in0=ps[:, :n], scalar=-1.0 / 64, in1=Ytm[:, fc, :n],
                                     op0=ALU.mult, op1=ALU.add), reads=[psb, YB], writes=[ycB])
                P.op("scalar", I("activation", out=ysq[:, :, :n], in_=yc[:, :, :n], func=AF.Square), reads=[ycB], writes=[ysqB])
                for fc in range(KC):
                    ps, psb = psr.next()
                    P.op("tensor", I("matmul", ps[:, :n], bones_f[:], ysq[:, fc, :n], start=True, stop=True), reads=[cB, ysqB], writes=[psb])
                    P.op("vector", I("tensor_scalar", out=yt32[:, fc, :n], in0=ps[:, :n], scalar1=1.0 / 64, scalar2=LNEPS, op0=ALU.mult, op1=ALU.add),
                         reads=[psb], writes=[yt32B])
                P.op("scalar", I("activation", out=yt32[:, :, :n], in_=yt32[:, :, :n], func=AF.Sqrt), reads=[yt32B], writes=[yt32B])
                P.op("vector", I("reciprocal", out=yt32[:, :, :n], in_=yt32[:, :, :n]), reads=[yt32B], writes=[yt32B])
                P.op("vector", I("tensor_tensor", out=yc[:, :, :n], in0=yc[:, :, :n], in1=yt32[:, :, :n], op=ALU.mult), reads=[ycB, yt32B], writes=[ycB])
                for fc in range(KC):
                    P.op("vector", I("scalar_tensor_tensor", out=yt32[:, fc, :n], in0=yc[:, fc, :n], scalar=pv[:, lgc + fc:lgc + fc + 1],
                                     in1=bon[:, fc, :n], op0=ALU.mult, op1=ALU.add), reads=[ycB, bonB, cB], writes=[yt32B])
                P.op("gpsimd", I("tensor_tensor", out=yo_[:, :, :n], in0=yt32[:, :, :n], in1=g_[:, :, :n], op=ALU.mult),
                     reads=[yt32B, gB], writes=[yoB])
                P.op("sync", I("dma_start", out=fm(mixT[G.name])[:, :, c0:c0 + n], in_=yo_[:, :, :n]), reads=[yoB], dma=P.dq())
        P.end_phase()

    plist = []
    for l in range(depth):
        if l % 2 == 0:
            plist.append((phase_even_proj, (l,)))
            plist.append((phase_even_attn, (l,)))
            plist.append((phase_out_xa, (l, w_out_d[l // 2])))
        else:
            plist.append((phase_rwkv, (l,)))
            plist.append((phase_out_xa, (l, rw_w_d["wo"][l // 2])))
        plist.append((phase_ffn, (l, l == depth - 1)))
    for f, a in plist[:_MAXPH]:
        f(*a)
    P.emit()
    return nc


_CACHE = {}
_DEPTH = DEPTH
_MAXPH = 1000
_NCORE = 8


def prep_common(inp):
    c = {}
    c["pvec"] = pack_pv(inp)
    lam = np.stack([np.concatenate([inp["ev_lam_q1"][e], inp["ev_lam_k1"][e], inp["ev_lam_q2"][e], inp["ev_lam_k2"][e]])
                    for e in range(NEVEN)]).reshape(1, -1)
    c["lam"] = np.ascontiguousarray(lam, np.float32)
    c["lnx"] = np.ascontiguousarray(np.stack([inp["rw_lnx_g"][0], inp["rw_lnx_b"][0], inp["rw_lnx_g"][1], inp["rw_lnx_b"][1]]), np.float32)
    c["ev_w_in"] = np.stack([wl(inp["ev_w_in"][e]) for e in range(NEVEN)])
    c["ev_pool_w"] = np.ascontiguousarray(np.asarray(inp["ev_pool_w"]).transpose(0, 2, 1, 3))
    c["ev_w_out"] = np.stack([wl(inp["ev_w_out"][e]) for e in range(NEVEN)])
    for k in ("wr", "wk", "wv", "wo"):
        c["rw_" + k] = np.stack([wl(inp["rw_" + k][o]) for o in range(NODD)])
    for k in ("w1", "a1", "g1", "v1"):
        a = inp["rw_" + k]
        c["rw_" + k] = np.stack([wl(a[o]) for o in range(a.shape[0])])
    for k in ("w2", "a2", "g2", "v2"):
        c["rw_" + k] = np.ascontiguousarray(inp["rw_" + k])
    for k in ("wq", "wk", "wv", "wo"):
        c["xa_" + k] = np.stack([wl(inp["xa_" + k][l]) for l in range(DEPTH)])
    c["ffn_wg"] = np.stack([wl(inp["ffn_wg"][l]) for l in range(DEPTH)])
    c["ffn_wu"] = np.stack([wl(inp["ffn_wu"][l]) for l in range(DEPTH)])
    c["ffn_wd"] = np.stack([wl(inp["ffn_wd"][l]) for l in range(DEPTH)])
    return c


def kernel(**inp):
    inp = {k: np.asarray(v) for k, v in inp.items()}
    B, TP, _ = inp["x_prompt"].shape
    SB, TS, _ = inp["x_sample"].shape
    PAST = inp["cache_diff_k"].shape[2]
    NMEM = inp["mem_prompt"].shape[1]
    NCORE = _NCORE
    NSB = SB // NCORE
    key = (TP, NSB, TS, PAST, NMEM, _DEPTH)
    if key not in _CACHE:
        _CACHE[key] = build(TP, NSB, TS, PAST, NMEM, _DEPTH)
    nc = _CACHE[key]
    common = prep_common(inp)
    in_maps = []
    for c in range(NCORE):
        b = c % B
        sb = slice(c * NSB, (c + 1) * NSB)
        m = dict(common)
        m["xT_p"] = np.ascontiguousarray(inp["x_prompt"][b].T)
        m["xT_s"] = np.ascontiguousarray(inp["x_sample"][sb].reshape(NSB * TS, D).T)
        m["memT"] = wl(np.ascontiguousarray(inp["mem_prompt"][b].T))
        m["cache_kT"] = np.ascontiguousarray(inp["cache_diff_k"][:, sb].transpose(0, 1, 3, 4, 2))
        m["cache_v"] = np.ascontiguousarray(inp["cache_diff_v"][:, sb])
        sp = inp["state_pool"][:, sb]
        m["state_pool"] = np.ascontiguousarray(sp.reshape(NEVEN, NSB, 15, 4, 128).transpose(0, 1, 4, 3, 2))
        m["state_shift"] = np.ascontiguousarray(inp["state_rw_shift"][:, sb].reshape(NODD, NSB, KC, 128).transpose(0, 1, 3, 2))
        sw = inp["state_rw_wkv"][:, sb]
        m["state_wkvT"] = np.ascontiguousarray(sw.reshape(NODD, NSB, 8, 2, 64, 64).transpose(0, 1, 3, 5, 2, 4).reshape(NODD, NSB, 128, 8, 64))
        mk = inp["cache_mem_k"][:, sb].reshape(DEPTH, NSB, NMEM, KC, 128)
        m["cache_mkT"] = np.ascontiguousarray(mk.transpose(0, 1, 4, 3, 2))
        m["cache_mv"] = np.ascontiguousarray(inp["cache_mem_v"][:, sb].reshape(DEPTH, NSB, NMEM, D))
        in_maps.append(m)
    res = run_bass_kernel_spmd(nc, in_maps, core_ids=list(range(NCORE))).results

    def unfm(a):
        return np.swapaxes(a, 0, 1).reshape((-1,) + a.shape[2:])

    y_p = np.stack([res[b]["yT_p"].T for b in range(B)])
    y_s = np.concatenate([res[c]["yT_s"].T.reshape(NSB, TS, D) for c in range(NCORE)])
    p_k = np.stack([np.stack([res[b]["kT_p"][e].T.reshape(TP, 4, 128) for b in range(B)]) for e in range(NEVEN)])
    p_v = np.stack([np.stack([res[b]["v_p"][e].reshape(TP, 4, 128) for b in range(B)]) for e in range(NEVEN)])
    s_k = np.stack([np.concatenate([res[c]["kT_s"][e].T.reshape(NSB, TS, 4, 128) for c in range(NCORE)]) for e in range(NEVEN)])
    s_v = np.stack([np.concatenate([res[c]["v_s"][e].reshape(NSB, TS, 4, 128) for c in range(NCORE)]) for e in range(NEVEN)])

    def pool_back(a):
        return a.transpose(0, 1, 4, 3, 2).reshape(a.shape[0], a.shape[1], 15, 512)

    p_pool = pool_back(np.concatenate([res[b]["pool_p"] for b in range(B)], axis=1))
    s_pool = pool_back(np.concatenate([res[c]["pool_s"] for c in range(NCORE)], axis=1))

    def shift_back(a):
        return a.transpose(0, 1, 3, 2).reshape(a.shape[0], a.shape[1], D)

    p_sh = shift_back(np.concatenate([res[b]["shift_p"] for b in range(B)], axis=1))
    s_sh = shift_back(np.concatenate([res[c]["shift_s"] for c in range(NCORE)], axis=1))

    def wkv_back(a):
        o, n = a.shape[:2]
        return a.reshape(o, n, 2, 64, 8, 64).transpose(0, 1, 4, 2, 5, 3).reshape(o, n, 16, 64, 64)

    p_wkv = wkv_back(np.concatenate([res[b]["wkv_p"] for b in range(B)], axis=1))
    s_wkv = wkv_back(np.concatenate([res[c]["wkv_s"] for c in range(NCORE)], axis=1))
    p_mk = np.stack([np.stack([unfm(res[b]["memkT"][l]).T.reshape(NMEM, 4, 256) for b in range(B)]) for l in range(DEPTH)])
    p_mv = np.stack([np.stack([res[b]["memv"][l].reshape(NMEM, 4, 256) for b in range(B)]) for l in range(DEPTH)])
    outs = (y_p, y_s, p_k, p_v, p_pool, p_sh, p_wkv, p_mk, p_mv, s_k, s_v, s_pool, s_sh, s_wkv)
    return tuple(np.ascontiguousarray(o, dtype=np.float32) for o in outs)
```

```python
import math
import numpy as np
from contextlib import ExitStack
import concourse.bass as bass
import concourse.mybir as mybir
from concourse.bass_utils import run_bass_kernel_spmd

F32 = mybir.dt.float32
BF16 = mybir.dt.bfloat16
AF = mybir.ActivationFunctionType
ALU = mybir.AluOpType
AX = mybir.AxisListType
ENGS = ["tensor", "vector", "scalar", "gpsimd", "sync"]

D = 1024
KC = 8
DFF = 2816
FC = 22
DEPTH = 4
NEVEN = 2
NODD = 2
EPS = 1e-6
LNEPS = 64e-5
CDEC = math.exp(-0.5)


class Buf:
    __slots__ = ("w", "r", "q")

    def __init__(self):
        self.w = None
        self.r = []
        self.q = None


class Prog:
    def __init__(self, nc):
        self.nc = nc
        self.es = ExitStack()
        self.ops = {e: [] for e in ENGS}
        self.sems = {}
        self.cnt = {}
        self.waited = {e: {} for e in ENGS}
        self.nsb = 0
        self.ndq = 0
        self.phase_es = None

    def sb(self, shape, dt=F32):
        self.nsb += 1
        st = self.phase_es if self.phase_es is not None else self.es
        return st.enter_context(self.nc.sbuf_tensor(f"sb{self.nsb}", list(shape), dt))

    def ps(self, shape, dt=F32):
        self.nsb += 1
        st = self.phase_es if self.phase_es is not None else self.es
        return st.enter_context(self.nc.psum_tensor(f"ps{self.nsb}", list(shape), dt))

    def begin_phase(self):
        self.barrier()
        self.phase_es = ExitStack()
        self.ndq = 0
        self.phase_id = getattr(self, "phase_id", 0) + 1

    def end_phase(self):
        self.barrier()
        self.phase_es.close()
        self.phase_es = None

    def dq(self):
        return True

    def op(self, eng, fn, reads=(), writes=(), dma=None):
        if dma is None:
            s = "e_" + eng
            inc = 1
        elif isinstance(dma, str):
            s = "d_" + dma
            inc = 16
        else:
            kb = writes[0] if writes else reads[0]
            pid = getattr(self, "phase_id", 0)
            if kb.q is None or kb.q[0] != pid:
                self.ndq += 1
                kb.q = (pid, f"q{self.ndq}")
            s = "d_" + kb.q[1]
            inc = 16
        own = "e_" + eng
        waits = {}
        wd = self.waited[eng]
        same_raw = eng in ("vector", "scalar", "gpsimd")

        def need(sv, same_ok):
            if sv is None:
                return
            sn, val = sv
            if sn == own and not same_ok:
                return
            if wd.get(sn, 0) >= val:
                return
            if waits.get(sn, 0) < val:
                waits[sn] = val

        for b in reads:
            need(b.w, same_raw)
        for b in writes:
            need(b.w, same_raw)
            for x in b.r:
                need(x, False)
        for k, v in waits.items():
            wd[k] = v
        self.cnt[s] = self.cnt.get(s, 0) + inc
        me = (s, self.cnt[s])
        for b in reads:
            b.r.append(me)
            if len(b.r) > 16:
                d = {}
                for sn, v in b.r:
                    if d.get(sn, 0) < v:
                        d[sn] = v
                b.r = list(d.items())
        for b in writes:
            b.w = me
            b.r = []
        self.ops[eng].append((list(waits.items()), fn, s, inc))
        return me

    def barrier(self):
        for eng in ENGS:
            waits = []
            for s, v in self.cnt.items():
                if self.waited[eng].get(s, 0) < v and s != "e_" + eng:
                    waits.append((s, v))
                    self.waited[eng][s] = v
            if waits:
                self.ops[eng].append((waits, None, None, 0))

    def emit(self):
        self.barrier()
        nc = self.nc
        for s in self.cnt:
            if s not in self.sems:
                self.sems[s] = self.es.enter_context(nc.semaphore(s))
        ops = self.ops
        sems = self.sems

        def run(e, lst):
            for waits, fn, s, inc in lst:
                for sn, v in waits:
                    e.wait_ge(sems[sn], v)
                if fn is not None:
                    fn(e).then_inc(sems[s], inc)

        with nc.Block() as block:
            @block.tensor
            def _(e):
                run(e, ops["tensor"])

            @block.vector
            def _(e):
                run(e, ops["vector"])

            @block.scalar
            def _(e):
                run(e, ops["scalar"])

            @block.gpsimd
            def _(e):
                run(e, ops["gpsimd"])

            @block.sync
            def _(e):
                run(e, ops["sync"])
        self.es.close()


def I(name, *a, **k):
    return lambda e: getattr(e, name)(*a, **k)


class Ring:
    def __init__(self, P, shape, dt, n, psum=False):
        self.items = []
        for _ in range(n):
            t = P.ps(shape, dt) if psum else P.sb(shape, dt)
            self.items.append((t, Buf()))
        self.i = 0

    def next(self):
        it = self.items[self.i % len(self.items)]
        self.i += 1
        return it


PV_SPECS = [("norm_mix_g", DEPTH, D), ("norm_xa_g", DEPTH, D), ("norm_ffn_g", DEPTH, D), ("final_norm_g", 1, D),
            ("ev_pool_scale", NEVEN, 512), ("ev_subln_g", NEVEN, 128),
            ("rw_mu", NODD * 6, D), ("rw_w0", NODD, D), ("rw_a0", NODD, D), ("rw_v0", 1, D),
            ("rw_k_k", NODD, D), ("rw_k_a", NODD, D), ("rw_r_k", NODD, D),
            ("rw_lnx_g", NODD, D), ("rw_lnx_b", NODD, D)]


def pv_layout():
    off = {}
    c = 0
    for name, n, ln in PV_SPECS:
        off[name] = (c, ln // 128)
        c += n * (ln // 128)
    return off, c


PV_OFF, PV_COLS = pv_layout()


def pack_pv(inp):
    out = np.zeros((128, PV_COLS), np.float32)
    for name, n, ln in PV_SPECS:
        a = np.asarray(inp[name], np.float32).reshape(n, ln // 128, 128)
        c0, w = PV_OFF[name]
        out[:, c0:c0 + n * w] = a.transpose(2, 0, 1).reshape(128, n * w)
    return out


def wl(w):
    K, F = w.shape
    return np.ascontiguousarray(w.reshape(K // 128, 128, F).transpose(1, 0, 2))


class Group:
    def __init__(self, name, nseq, T):
        self.name = name
        self.nseq = nseq
        self.T = T
        self.NT = nseq * T


import os
_SKIP = set(os.environ.get("BIS", "").split(","))


def build(TP, NSB, TS, PAST, NMEM=256, depth=DEPTH):
    nc = bass.Bass("TRN2", target_bir_lowering=False)
    P = Prog(nc)
    dram_in = {}
    dram_out = {}

    def din(name, shape, dt=F32):
        dram_in[name] = nc.dram_tensor(name, list(shape), dt, kind="ExternalInput").ap()
        return dram_in[name]

    def dout(name, shape, dt=F32):
        dram_out[name] = nc.dram_tensor(name, list(shape), dt, kind="ExternalOutput").ap()
        return dram_out[name]

    def dscr(name, shape, dt=F32):
        return nc.dram_tensor("scr_" + name, list(shape), dt, kind="Internal").ap()

    GP = Group("p", 1, TP)
    GS = Group("s", NSB, TS)
    groups = [GP, GS]
    NPB = PAST // 128

    xin = {"p": din("xT_p", [D, GP.NT]), "s": din("xT_s", [D, GS.NT])}
    pv_d = din("pvec", [128, PV_COLS])
    lam_d = din("lam", [1, NEVEN * 4 * 64])
    lnx_d = din("lnx", [NODD * 2, D])
    w_in_d = din("ev_w_in", [NEVEN, 128, KC, 2048])
    pool_w_d = din("ev_pool_w", [NEVEN, 128, 4, 128])
    w_out_d = din("ev_w_out", [NEVEN, 128, KC, D])
    rw_w_d = {k: din("rw_" + k, [NODD, 128, KC, D]) for k in ("wr", "wk", "wv", "wo")}
    rw_l1_d = {k: din("rw_" + k, [NODD if k != "v1" else 1, 128, KC, n]) for k, n in (("w1", 64), ("a1", 64), ("g1", 160), ("v1", 32))}
    rw_l2_d = {k: din("rw_" + k, [NODD if k != "v2" else 1, n, D]) for k, n in (("w2", 64), ("a2", 64), ("g2", 160), ("v2", 32))}
    xa_w_d = {k: din("xa_" + k, [DEPTH, 128, KC, D]) for k in ("wq", "wk", "wv", "wo")}
    ffn_g_d = din("ffn_wg", [DEPTH, 128, KC, DFF])
    ffn_u_d = din("ffn_wu", [DEPTH, 128, KC, DFF])
    ffn_d_d = din("ffn_wd", [DEPTH, 128, FC, D])
    memT_d = din("memT", [128, KC, NMEM])
    ckT_d = din("cache_kT", [NEVEN, NSB, 4, 128, PAST])
    cv_d = din("cache_v", [NEVEN, NSB, PAST, 4, 128])
    spool_d = din("state_pool", [NEVEN, NSB, 128, 4, 15])
    sshift_d = din("state_shift", [NODD, NSB, 128, KC])
    swkv_d = din("state_wkvT", [NODD, NSB, 128, KC, 64])
    cmkT_d = din("cache_mkT", [DEPTH, NSB, 128, KC, NMEM])
    cmv_d = din("cache_mv", [DEPTH, NSB, NMEM, D])

    yT_o = {"p": dout("yT_p", [D, GP.NT]), "s": dout("yT_s", [D, GS.NT])}
    kT_o = {"p": dout("kT_p", [NEVEN, 512, GP.NT]), "s": dout("kT_s", [NEVEN, 512, GS.NT])}
    v_o = {"p": dout("v_p", [NEVEN, GP.NT, 512]), "s": dout("v_s", [NEVEN, GS.NT, 512])}
    pool_o = {"p": dout("pool_p", [NEVEN, 1, 128, 4, 15]), "s": dout("pool_s", [NEVEN, NSB, 128, 4, 15])}
    shift_o = {"p": dout("shift_p", [NODD, 1, 128, KC]), "s": dout("shift_s", [NODD, NSB, 128, KC])}
    wkv_o = {"p": dout("wkv_p", [NODD, 1, 128, KC, 64]), "s": dout("wkv_s", [NODD, NSB, 128, KC, 64])}
    memk_o = dout("memkT", [DEPTH, 128, KC, NMEM])
    memv_o = dout("memv", [DEPTH, NMEM, D])

    xT = {g.name: dscr("x_" + g.name, [D, g.NT]) for g in groups}
    qT = {g.name: dscr("q_" + g.name, [512, g.NT], BF16) for g in groups}
    kTs = {g.name: dscr("k_" + g.name, [512, g.NT], BF16) for g in groups}
    vtm = {g.name: dscr("v_" + g.name, [g.NT, 512], BF16) for g in groups}
    mixT = {g.name: dscr("mix_" + g.name, [D, g.NT], BF16) for g in groups}
    vfirst = {g.name: dscr("vf_" + g.name, [D, g.NT]) for g in groups}
    mkT_s = dscr("mkT_s", [DEPTH, 128, KC, NMEM], BF16)
    mv_s = dscr("mv_s", [DEPTH, NMEM, D], BF16)

    def fm(ap):
        return ap.rearrange("(c p) t -> p c t", p=128)

    ident_f = P.sb([128, 128], F32)
    ident_b = P.sb([128, 128], BF16)
    ones_b = P.sb([128, 128], BF16)
    ones_f = P.sb([128, 128], F32)
    bones_b = P.sb([128, 128], BF16)
    bones_f = P.sb([128, 128], F32)
    mk3 = P.sb([128, 2, 128], F32)
    mkL = P.sb([128, 128], F32)
    mk3s = P.sb([16, 2, 16], F32)
    mkLs = P.sb([16, 16], F32)
    pv = P.sb([128, PV_COLS], F32)
    invcnt = P.sb([128, 16], F32)
    neglam = P.sb([128, NEVEN], F32)
    gsub = P.sb([128, NEVEN], F32)
    lamrow = P.sb([1, NEVEN * 4 * 64], F32)
    lamt = P.sb([1, 8], F32)
    cB = Buf()

    def gp(fn, **k):
        P.op("gpsimd", fn, **k)

    gp(I("memset", ones_f[:], 1.0), writes=[cB])
    gp(I("memset", ones_b[:], 1.0), writes=[cB])
    gp(I("memset", ident_f[:], 1.0), writes=[cB])
    gp(I("affine_select", out=ident_f[:], in_=ident_f[:], pattern=[[-1, 128]], compare_op=ALU.is_equal,
                                 fill=0.0, base=0, channel_multiplier=1), reads=[cB], writes=[cB])
    gp(I("tensor_copy", out=ident_b[:], in_=ident_f[:]), reads=[cB], writes=[cB])
    gp(I("memset", bones_f[:], 0.0), writes=[cB])
    gp(I("memset", bones_f[0:64, 0:64], 1.0), writes=[cB])
    gp(I("memset", bones_f[64:128, 64:128], 1.0), writes=[cB])
    gp(I("memset", bones_b[:], 0.0), writes=[cB])
    gp(I("memset", bones_b[0:64, 0:64], 1.0), writes=[cB])
    gp(I("memset", bones_b[64:128, 64:128], 1.0), writes=[cB])
    gp(I("memset", mk3[:], 1.0), writes=[cB])
    gp(I("memset", mkL[:], 1.0), writes=[cB])
    gp(I("affine_select", out=mk3[:, 0, :], in_=mk3[:, 0, :], pattern=[[1, 128]], compare_op=ALU.is_gt,
                                 fill=0.0, base=0, channel_multiplier=-1), reads=[cB], writes=[cB])
    gp(I("affine_select", out=mk3[:, 1, :], in_=mk3[:, 1, :], pattern=[[1, 128]], compare_op=ALU.is_ge,
                                 fill=0.0, base=0, channel_multiplier=-1), reads=[cB], writes=[cB])
    gp(I("affine_select", out=mkL[:], in_=mkL[:], pattern=[[-1, 128]], compare_op=ALU.is_gt,
                                 fill=0.0, base=0, channel_multiplier=1), reads=[cB], writes=[cB])
    gp(I("tensor_copy", out=mk3s[:], in_=mk3[0:16, :, 0:16]), reads=[cB], writes=[cB])
    gp(I("tensor_copy", out=mkLs[:], in_=mkL[0:16, 0:16]), reads=[cB], writes=[cB])
    gp(I("memset", mk3[0:64, :, 64:128], 0.0), reads=[cB], writes=[cB])
    gp(I("memset", mk3[64:128, :, 0:64], 0.0), reads=[cB], writes=[cB])
    gp(I("memset", mkL[0:64, 64:128], 0.0), reads=[cB], writes=[cB])
    gp(I("memset", mkL[64:128, 0:64], 0.0), reads=[cB], writes=[cB])
    gp(I("iota", invcnt[:], pattern=[[1, 16]], base=1, channel_multiplier=0,
                        allow_small_or_imprecise_dtypes=True), writes=[cB])
    P.op("vector", I("reciprocal", out=invcnt[:], in_=invcnt[:]), reads=[cB], writes=[cB])
    P.op("sync", I("dma_start", out=pv[:], in_=pv_d), writes=[cB], dma="c0")
    P.op("sync", I("dma_start", out=lamrow[:], in_=lam_d), writes=[cB], dma="c1")
    P.begin_phase()
    lamps = P.ps([128, 512], F32)
    for e_ in range(NEVEN if "lam" not in _SKIP else 0):
        b0 = e_ * 256
        for j in range(2):
            P.op("vector", I("tensor_tensor",
                out=lamrow[0:1, b0 + j * 128:b0 + j * 128 + 64], in0=lamrow[0:1, b0 + j * 128:b0 + j * 128 + 64],
                in1=lamrow[0:1, b0 + j * 128 + 64:b0 + j * 128 + 128], op=ALU.mult), reads=[cB], writes=[cB])
            P.op("vector", I("reduce_sum",
                out=lamt[0:1, e_ * 4 + j:e_ * 4 + j + 1], in_=lamrow[0:1, b0 + j * 128:b0 + j * 128 + 64], axis=AX.X),
                reads=[cB], writes=[cB])
        P.op("scalar", I("activation", out=lamt[0:1, e_ * 4:e_ * 4 + 2], in_=lamt[0:1, e_ * 4:e_ * 4 + 2],
                                                     func=AF.Exp), reads=[cB], writes=[cB])
        lam_init = 0.8 - 0.6 * math.exp(-0.3 * (2 * e_))
        P.op("vector", I("tensor_scalar", out=lamt[0:1, e_ * 4 + 2:e_ * 4 + 3], in0=lamt[0:1, e_ * 4 + 1:e_ * 4 + 2], scalar1=-lam_init,
                         scalar2=1.0, op0=ALU.add, op1=ALU.mult), reads=[cB], writes=[cB])
        P.op("vector", I("tensor_tensor", out=lamt[0:1, e_ * 4 + 2:e_ * 4 + 3], in0=lamt[0:1, e_ * 4 + 2:e_ * 4 + 3],
                         in1=lamt[0:1, e_ * 4:e_ * 4 + 1], op=ALU.subtract), reads=[cB], writes=[cB])
        P.op("tensor", I("matmul", lamps[:, e_:e_ + 1], ones_f[0:1, :], lamt[0:1, e_ * 4 + 2:e_ * 4 + 3],
                                                 start=True, stop=True), reads=[cB], writes=[cB])
        P.op("vector", I("tensor_copy", out=neglam[:, e_:e_ + 1], in_=lamps[:, e_:e_ + 1]),
             reads=[cB], writes=[cB])
        c0 = PV_OFF["ev_subln_g"][0] + e_
        P.op("scalar", I("mul", gsub[:, e_:e_ + 1], pv[:, c0:c0 + 1], 1.0 - lam_init), reads=[cB], writes=[cB])

    def pvc(name, idx=0):
        c0, w = PV_OFF[name]
        return c0 + idx * w

    def load_w(dst, src, buf, eng="gpsimd"):
        P.op(eng, I("dma_start", out=dst, in_=src), writes=[buf], dma=P.dq())

    def norm_tile(xt, xb, n, gcol0, out, ob, R, sq_ring, ps_ring, kc_n=KC, dnorm=D, eps=EPS, ones=None):
        ones = ones_b if ones is None else ones
        sq, sqb = sq_ring.next()
        P.op("scalar", I("activation", out=sq[:, :kc_n, :n], in_=xt[:, :kc_n, :n], func=AF.Square),
             reads=[xb], writes=[sqb])
        ps, psb = ps_ring.next()
        for kc in range(kc_n):
            P.op("tensor", I("matmul", ps[:, :n], ones[:], sq[:, kc, :n], start=(kc == 0),
                                                     stop=(kc == kc_n - 1)), reads=[sqb, cB], writes=[psb])
        rstd, rb = R.next()
        P.op("vector", I("tensor_scalar", out=rstd[:, :n], in0=ps[:, :n], scalar1=1.0 / dnorm, scalar2=eps,
                                                 op0=ALU.mult, op1=ALU.add), reads=[psb], writes=[rb])
        P.op("scalar", I("activation", out=rstd[:, :n], in_=rstd[:, :n], func=AF.Sqrt), reads=[rb], writes=[rb])
        P.op("vector", I("reciprocal", out=rstd[:, :n], in_=rstd[:, :n]), reads=[rb], writes=[rb])
        for kc in range(kc_n):
            eng = "vector"
            P.op(eng, I("scalar_tensor_tensor",
                out=out[:, kc, :n], in0=xt[:, kc, :n], scalar=pv[:, gcol0 + kc:gcol0 + kc + 1], in1=rstd[:, :n],
                op0=ALU.mult, op1=ALU.mult), reads=[xb, rb, cB], writes=[ob])

    def linear(W, wb, h, hb, n, f0, nfo, ps_ring, epi, kc_n=KC):
        for fo in range(nfo):
            ps, psb = ps_ring.next()
            for kc in range(kc_n):
                P.op("tensor", I("matmul",
                    ps[:, :n], W[:, kc, f0 + fo * 128:f0 + (fo + 1) * 128], h[:, kc, :n], start=(kc == 0),
                    stop=(kc == kc_n - 1)), reads=[wb, hb], writes=[psb])
            epi(fo, ps, psb)

    def tiles_of(G, nmax):
        res = []
        for s in range(G.nseq):
            t0 = 0
            while t0 < G.T:
                n = min(nmax, G.T - t0)
                res.append((s, t0, n, s * G.T + t0))
                t0 += n
        return res

    cp = Ring(P, [128, KC, 512], F32, 2)
    for G in (groups if "xcopy" not in _SKIP else []):
        for (s, t0, n, c0) in tiles_of(G, 512):
            t, tb = cp.next()
            P.op("sync", I("dma_start", out=t[:, :, :n], in_=fm(xin[G.name])[:, :, c0:c0 + n]),
                 writes=[tb], dma=P.dq())
            P.op("sync", I("dma_start", out=fm(xT[G.name])[:, :, c0:c0 + n], in_=t[:, :, :n]),
                 reads=[tb], dma=P.dq())
    memT = P.sb([128, KC, NMEM], BF16)
    memB = Buf()
    load_w(memT[:], memT_d, memB)
    wkr = Ring(P, [128, KC, D], BF16, 2)
    psr = Ring(P, [128, 512], F32, 4, psum=True)
    ev32 = Ring(P, [128, 512], F32, 3)
    ev16 = Ring(P, [128, 512], BF16, 3)
    for l in range(depth if "memkv" not in _SKIP else 0):
        wk, wkb = wkr.next()
        load_w(wk[:], xa_w_d["wk"][l], wkb)
        wv, wvb = wkr.next()
        load_w(wv[:], xa_w_d["wv"][l], wvb)

        def epi_k(fo, ps, psb, l=l):
            a, ab = ev32.next()
            b, bb = ev16.next()
            if "e1" not in _SKIP:
                P.op("scalar", I("copy", out=a[:, :NMEM], in_=ps[:, :NMEM]), reads=[psb], writes=[ab])
            if "e2" not in _SKIP:
                P.op("gpsimd", I("tensor_copy", out=b[:, :NMEM], in_=a[:, :NMEM]), reads=[ab], writes=[bb])
            if "e3" not in _SKIP:
                P.op("sync", I("dma_start", out=memk_o[l, :, fo, :], in_=a[:, :NMEM]), reads=[ab], dma=P.dq())
            if "e4" not in _SKIP:
                P.op("sync", I("dma_start", out=mkT_s[l, :, fo, :], in_=b[:, :NMEM]), reads=[bb], dma=P.dq())
        if "mk" not in _SKIP:
            linear(wk, wkb, memT, memB, NMEM, 0, KC, psr, epi_k)
        for kb in range(NMEM // 128 if "mvv" not in _SKIP else 0):
            for hf in range(2):
                ps, psb = psr.next()
                for kc in range(KC):
                    P.op("tensor", I("matmul",
                        ps[:, :], memT[:, kc, kb * 128:(kb + 1) * 128], wv[:, kc, hf * 512:(hf + 1) * 512],
                        start=(kc == 0), stop=(kc == KC - 1)), reads=[wvb, memB], writes=[psb])
                a, ab = ev32.next()
                b, bb = ev16.next()
                P.op("scalar", I("copy", out=a[:], in_=ps[:]), reads=[psb], writes=[ab])
                P.op("gpsimd", I("tensor_copy", out=b[:], in_=a[:]), reads=[ab], writes=[bb])
                P.op("sync", I("dma_start",
                    out=memv_o[l, kb * 128:(kb + 1) * 128, hf * 512:(hf + 1) * 512], in_=a[:]), reads=[ab], dma=P.dq())
                P.op("sync", I("dma_start",
                    out=mv_s[l, kb * 128:(kb + 1) * 128, hf * 512:(hf + 1) * 512], in_=b[:]), reads=[bb], dma=P.dq())
    P.end_phase()

    def phase_even_proj(l):
        e_ = l // 2
        P.begin_phase()
        w_in = P.sb([128, KC, 2048], BF16)
        wb = Buf()
        load_w(w_in[:], w_in_d[e_], wb)
        pw = P.sb([128, 4, 128], BF16)
        pwb = Buf()
        load_w(pw[:], pool_w_d[e_], pwb)
        xr = Ring(P, [128, KC, 512], F32, 2)
        hr = Ring(P, [128, KC, 512], BF16, 2)
        sqr = Ring(P, [128, KC, 512], BF16, 1)
        rr = Ring(P, [128, 512], F32, 2)
        psr = Ring(P, [128, 512], F32, 6, psum=True)
        ev32 = Ring(P, [128, 512], F32, 4)
        ev16 = Ring(P, [128, 512], BF16, 4)
        ubuf = P.sb([128, 4, 15 + 512], F32)
        ub = Buf()
        ta = P.sb([128, 15 + 512], F32)
        tb2 = P.sb([128, 15 + 512], F32)
        tB = Buf()
        for G in groups:
            for (s, t0, n, c0) in tiles_of(G, 512):
                if t0 == 0:
                    if G.name == "p":
                        P.op("gpsimd", I("memset", ubuf[:, :, 0:15], 0.0), writes=[ub])
                    else:
                        P.op("sync", I("dma_start", out=ubuf[:, :, 0:15], in_=spool_d[e_, s]), writes=[ub],
                             dma=P.dq())
                xt, xb = xr.next()
                P.op("sync", I("dma_start", out=xt[:, :, :n], in_=fm(xT[G.name])[:, :, c0:c0 + n]),
                     writes=[xb], dma=P.dq())
                h, hb = hr.next()
                norm_tile(xt, xb, n, pvc("norm_mix_g", l), h, hb, rr, sqr, psr)

                def epi_u(fo, ps, psb):
                    P.op("scalar", I("copy", out=ubuf[:, fo, 15:15 + n], in_=ps[:, :n]), reads=[psb], writes=[ub])
                linear(w_in, wb, h, hb, n, 0, 4, psr, epi_u)

                def epi_q(fo, ps, psb, G=G, c0=c0):
                    b, bb = ev16.next()
                    P.op("vector", I("tensor_copy", out=b[:, :n], in_=ps[:, :n]), reads=[psb], writes=[bb])
                    P.op("sync", I("dma_start", out=qT[G.name][fo * 128:(fo + 1) * 128, c0:c0 + n], in_=b[:, :n]),
                         reads=[bb], dma=P.dq())
                linear(w_in, wb, h, hb, n, 512, 4, psr, epi_q)

                def epi_k(fo, ps, psb, G=G, c0=c0):
                    a, ab = ev32.next()
                    b, bb = ev16.next()
                    P.op("scalar", I("copy", out=a[:, :n], in_=ps[:, :n]), reads=[psb], writes=[ab])
                    P.op("gpsimd", I("tensor_copy", out=b[:, :n], in_=a[:, :n]), reads=[ab], writes=[bb])
                    P.op("sync", I("dma_start", out=kT_o[G.name][e_, fo * 128:(fo + 1) * 128, c0:c0 + n], in_=a[:, :n]),
                         reads=[ab], dma=P.dq())
                    P.op("sync", I("dma_start", out=kTs[G.name][fo * 128:(fo + 1) * 128, c0:c0 + n], in_=b[:, :n]),
                         reads=[bb], dma=P.dq())
                linear(w_in, wb, h, hb, n, 1024, 4, psr, epi_k)
                for j in range((n + 127) // 128):
                    m = min(128, n - j * 128)
                    ps, psb = psr.next()
                    for kc in range(KC):
                        P.op("tensor", I("matmul",
                            ps[:m, :], h[:, kc, j * 128:j * 128 + m], w_in[:, kc, 1536:2048], start=(kc == 0),
                            stop=(kc == KC - 1)), reads=[wb, hb], writes=[psb])
                    a, ab = ev32.next()
                    b, bb = ev16.next()
                    P.op("scalar", I("copy", out=a[:m, :], in_=ps[:m, :]), reads=[psb], writes=[ab])
                    P.op("gpsimd", I("tensor_copy", out=b[:m, :], in_=a[:m, :]), reads=[ab], writes=[bb])
                    r0 = c0 + j * 128
                    P.op("sync", I("dma_start", out=v_o[G.name][e_, r0:r0 + m, :], in_=a[:m, :]),
                         reads=[ab], dma=P.dq())
                    P.op("sync", I("dma_start", out=vtm[G.name][r0:r0 + m, :], in_=b[:m, :]),
                         reads=[bb], dma=P.dq())
                L = 15 + n
                for g in range(4):
                    w = 2 << g
                    src = ubuf[:, g, :]
                    cur = None
                    sh = 1
                    for st in range(g + 1):
                        dst = ta if st % 2 == 0 else tb2
                        s_ap = src if cur is None else cur
                        lo = 2 * sh - 1
                        P.op("vector", I("tensor_tensor",
                            out=dst[:, lo:L], in0=s_ap[:, lo:L], in1=s_ap[:, lo - sh:L - sh], op=ALU.add),
                            reads=[ub, tB], writes=[tB])
                        cur = dst
                        sh *= 2
                    pl, plb = ev32.next()
                    P.op("vector", I("tensor_scalar", out=pl[:, :n], in0=cur[:, 15:15 + n], scalar1=1.0 / w, scalar2=0.0, op0=ALU.mult, op1=ALU.add),
                         reads=[tB], writes=[plb])
                    P.op("vector", I("tensor_tensor", out=pl[:, :n], in0=pl[:, :n], in1=ubuf[:, g, 15:15 + n], op=ALU.subtract),
                         reads=[ub, plb], writes=[plb])
                    if G.name == "p" and t0 == 0:
                        P.op("vector", I("tensor_tensor",
                            out=pl[:, 0:w - 1], in0=cur[:, 15:15 + w - 1], in1=invcnt[:, 0:w - 1], op=ALU.mult),
                            reads=[tB, cB, plb], writes=[plb])
                        P.op("vector", I("tensor_tensor",
                            out=pl[:, 0:w - 1], in0=pl[:, 0:w - 1], in1=ubuf[:, g, 15:15 + w - 1], op=ALU.subtract),
                            reads=[ub, plb], writes=[plb])
                    pb_, pbb = ev16.next()
                    P.op("gpsimd", I("tensor_copy", out=pb_[:, :n], in_=pl[:, :n]), reads=[plb], writes=[pbb])
                    ps, psb = psr.next()
                    P.op("tensor", I("matmul", ps[:, :n], pw[:, g, :], pb_[:, :n], start=True, stop=True),
                         reads=[pwb, pbb], writes=[psb])
                    ob_, obb = ev16.next()
                    sc = pvc("ev_pool_scale", e_) + g
                    P.op("scalar", I("mul", ob_[:, :n], ps[:, :n], pv[:, sc:sc + 1]),
                         reads=[psb, cB], writes=[obb])
                    P.op("sync", I("dma_start",
                        out=mixT[G.name][g * 128:(g + 1) * 128, c0:c0 + n], in_=ob_[:, :n]), reads=[obb], dma=P.dq())
                if t0 + n == G.T:
                    P.op("sync", I("dma_start", out=pool_o[G.name][e_, s], in_=ubuf[:, :, n:n + 15]),
                         reads=[ub], dma=P.dq())
                else:
                    P.op("vector", I("tensor_copy", out=ubuf[:, :, 0:15], in_=ubuf[:, :, n:n + 15]), reads=[ub], writes=[ub])
        P.end_phase()

    def phase_even_attn(l):
        e_ = l // 2
        scale = 64 ** -0.5
        P.begin_phase()
        psS = Ring(P, [128, 512], F32, 3, psum=True)
        psO = [Ring(P, [128, 512], F32, 1, psum=True) for _ in range(2)]
        psD = [Ring(P, [128, 512], F32, 1, psum=True) for _ in range(2)]
        ptr = Ring(P, [128, 512], BF16, 6)
        tmp = Ring(P, [128, 512], F32, 6)
        sqr = Ring(P, [128, 1, 512], BF16, 2)
        o16 = Ring(P, [128, 1, 512], BF16, 2)

        def finish(o_, ob, d_, db, n, dst_rows, G, c0):
            a = []
            for m in range(2):
                r, rb = tmp.next()
                P.op("vector", I("reciprocal", out=r[:, :n], in_=d_[m][:, :n]), reads=[db[m]], writes=[rb])
                t, tb = tmp.next()
                P.op("vector", I("tensor_tensor", out=t[:, :n], in0=o_[m][:, :n], in1=r[:, :n], op=ALU.mult),
                     reads=[ob[m], rb], writes=[tb])
                a.append((t, tb))
            av, avb = tmp.next()
            P.op("vector", I("scalar_tensor_tensor", out=av[:, :n], in0=a[1][0][:, :n], scalar=neglam[:, e_:e_ + 1],
                                                             in1=a[0][0][:, :n], op0=ALU.mult, op1=ALU.add),
                 reads=[a[0][1], a[1][1], cB], writes=[avb])
            out, outb = o16.next()
            sq, sqb = sqr.next()
            P.op("scalar", I("activation", out=sq[:, 0, :n], in_=av[:, :n], func=AF.Square), reads=[avb], writes=[sqb])
            ps, psb = psS.next()
            P.op("tensor", I("matmul", ps[:, :n], ones_b[:], sq[:, 0, :n], start=True, stop=True), reads=[sqb, cB], writes=[psb])
            rstd, rb = tmp.next()
            P.op("vector", I("tensor_scalar", out=rstd[:, :n], in0=ps[:, :n], scalar1=1.0 / 128, scalar2=EPS,
                                                     op0=ALU.mult, op1=ALU.add), reads=[psb], writes=[rb])
            P.op("scalar", I("activation", out=rstd[:, :n], in_=rstd[:, :n], func=AF.Sqrt), reads=[rb], writes=[rb])
            P.op("vector", I("reciprocal", out=rstd[:, :n], in_=rstd[:, :n]), reads=[rb], writes=[rb])
            P.op("vector", I("scalar_tensor_tensor", out=out[:, 0, :n], in0=av[:, :n], scalar=gsub[:, e_:e_ + 1],
                                                             in1=rstd[:, :n], op0=ALU.mult, op1=ALU.mult),
                 reads=[avb, rb, cB], writes=[outb])
            P.op("sync", I("dma_start", out=mixT[G.name][dst_rows:dst_rows + 128, c0:c0 + n], in_=out[:, 0, :n]),
                 reads=[outb], dma=P.dq())

        G = GP
        T = G.T
        kh_r = Ring(P, [128, T], BF16, 2)
        vh_r = Ring(P, [128, max(T // 128, 1), 128], BF16, 2)
        qt_r = Ring(P, [128, 512], BF16, 2)
        for hd in range(4):
            kh, khb = kh_r.next()
            P.op("sync", I("dma_start", out=kh[:, :], in_=kTs["p"][hd * 128:(hd + 1) * 128, :]), writes=[khb], dma=P.dq())
            vh, vhb = vh_r.next()
            P.op("sync", I("dma_start",
                out=vh[:, :, :], in_=vtm["p"][:, hd * 128:(hd + 1) * 128].rearrange("(j p) e -> p j e", p=128)), writes=[vhb], dma=P.dq())
            for (s, t0, n, c0) in tiles_of(G, 512):
                qt, qtb = qt_r.next()
                P.op("sync", I("dma_start", out=qt[:, :n], in_=qT["p"][hd * 128:(hd + 1) * 128, c0:c0 + n]),
                     writes=[qtb], dma=P.dq())
                o_ = [psO[m].next() for m in range(2)]
                d_ = [psD[m].next() for m in range(2)]
                nkb = (t0 + n) // 128

                def pv_block(items):
                    for (pt, ptb, m, jb, q0, diag, first, last) in items:
                        P.op("tensor", I("matmul", o_[m][0][:, q0:n], vh[:, jb, :], pt[:, q0:n], start=first, stop=last),
                             reads=[vhb, ptb], writes=[o_[m][1]])
                        P.op("tensor", I("matmul", d_[m][0][:, q0:n], ones_b[:], pt[:, q0:n], start=first, stop=last),
                             reads=[cB, ptb], writes=[d_[m][1]])

                pending = None
                for j in range(nkb):
                    jj = j - t0 // 128
                    q0 = 0 if jj < 0 else 128 * jj
                    first = (j == 0)
                    last = (j == nkb - 1)
                    items = []
                    for m in range(2):
                        ps, psb = psS.next()
                        P.op("tensor", I("matmul", ps[:, q0:n], kh[64 * m:64 * m + 64, j * 128:(j + 1) * 128], qt[64 * m:64 * m + 64, q0:n],
                                         start=True, stop=True), reads=[khb, qtb], writes=[psb])
                        pt, ptb = ptr.next()
                        P.op("scalar", I("activation", out=pt[:, q0:n], in_=ps[:, q0:n], func=AF.Exp, scale=scale),
                             reads=[psb], writes=[ptb])
                        if jj >= 0:
                            P.op("gpsimd", I("memset", pt[64:128, q0:q0 + 64], 0.0), reads=[ptb], writes=[ptb])
                        items.append((pt, ptb, m, j, q0, jj >= 0, first, last))
                    if pending is not None:
                        pv_block(pending)
                    pending = items
                pv_block(pending)
                finish([o_[0][0], o_[1][0]], [o_[0][1], o_[1][1]], [d_[0][0], d_[1][0]], [d_[0][1], d_[1][1]], n,
                       512 + hd * 128, G, c0)
        G = GS
        TSq = G.T
        kc_r = Ring(P, [128, PAST], BF16, 2)
        vc_r = Ring(P, [128, NPB, 128], BF16, 2)
        kn_r = Ring(P, [128, 16], BF16, 2)
        vn_r = Ring(P, [16, 128], BF16, 2)
        pts_r = Ring(P, [128, 2, NPB, 16], BF16, 2)
        ptn_r = Ring(P, [16, 2, 16], BF16, 2)
        psQ = Ring(P, [128, 512], F32, 1, psum=True)
        for s in range(G.nseq):
            c0 = s * TSq
            for hd in range(4):
                kc, kcb = kc_r.next()
                P.op("gpsimd", I("dma_start", out=kc[:, :], in_=ckT_d[e_, s, hd]), writes=[kcb], dma=P.dq())
                vc, vcb = vc_r.next()
                P.op("gpsimd", I("dma_start",
                    out=vc[:, :, :], in_=cv_d[e_, s, :, hd, :].rearrange("(j p) e -> p j e", p=128)), writes=[vcb], dma=P.dq())
                kn, knb = kn_r.next()
                P.op("sync", I("dma_start", out=kn[:, :], in_=kTs["s"][hd * 128:(hd + 1) * 128, c0:c0 + TSq]),
                     writes=[knb], dma=P.dq())
                vn, vnb = vn_r.next()
                P.op("sync", I("dma_start", out=vn[:, :], in_=vtm["s"][c0:c0 + TSq, hd * 128:(hd + 1) * 128]),
                     writes=[vnb], dma=P.dq())
                qt, qtb = qt_r.next()
                P.op("sync", I("dma_start", out=qt[:, :TSq], in_=qT["s"][hd * 128:(hd + 1) * 128, c0:c0 + TSq]),
                     writes=[qtb], dma=P.dq())
                pts, ptsb = pts_r.next()
                ptn, ptnb = ptn_r.next()
                psn, psnb = psQ.next()
                for m in range(2):
                    ps, psb = psS.next()
                    for j in range(NPB):
                        P.op("tensor", I("matmul",
                            ps[:, j * 16:(j + 1) * 16], kc[64 * m:64 * m + 64, j * 128:(j + 1) * 128], qt[64 * m:64 * m + 64, :TSq],
                            start=True, stop=True), reads=[kcb, qtb], writes=[psb])
                    P.op("scalar", I("activation",
                        out=pts[:, m, :, :], in_=ps[:, :NPB * 16].rearrange("p (j q) -> p j q", q=16), func=AF.Exp, scale=scale),
                        reads=[psb], writes=[ptsb])
                    P.op("tensor", I("matmul", psn[:16, m * 16:(m + 1) * 16], kn[64 * m:64 * m + 64, :], qt[64 * m:64 * m + 64, :TSq],
                                                           start=True, stop=True), reads=[knb, qtb], writes=[psnb])
                P.op("scalar", I("activation", out=ptn[:, :, :], in_=psn[:16, 0:32].rearrange("p (m q) -> p m q", q=16),
                                                      func=AF.Exp, scale=scale), reads=[psnb], writes=[ptnb])
                o_ = [psO[m].next() for m in range(2)]
                d_ = [psD[m].next() for m in range(2)]
                for m in range(2):
                    for j in range(NPB):
                        P.op("tensor", I("matmul", o_[m][0][:, :TSq], vc[:, j, :], pts[:, m, j, :], start=(j == 0), stop=False),
                             reads=[vcb, ptsb], writes=[o_[m][1]])
                    P.op("tensor", I("matmul", o_[m][0][:, :TSq], vn[:, :], ptn[:, m, :], start=False, stop=True),
                         reads=[vnb, ptnb], writes=[o_[m][1]])
                    for j in range(NPB):
                        P.op("tensor", I("matmul", d_[m][0][:, :TSq], ones_b[:], pts[:, m, j, :], start=(j == 0), stop=False),
                             reads=[cB, ptsb], writes=[d_[m][1]])
                    P.op("tensor", I("matmul", d_[m][0][:, :TSq], ones_b[0:16, :], ptn[:, m, :], start=False, stop=True),
                         reads=[cB, ptnb], writes=[d_[m][1]])
                finish([o_[0][0], o_[1][0]], [o_[0][1], o_[1][1]], [d_[0][0], d_[1][0]], [d_[0][1], d_[1][1]], TSq,
                       512 + hd * 128, G, c0)
        P.end_phase()

    def phase_out_xa(l, wout_src):
        P.begin_phase()
        wo_m = P.sb([128, KC, D], BF16)
        wq = P.sb([128, KC, D], BF16)
        wo = P.sb([128, KC, D], BF16)
        wB = [Buf(), Buf(), Buf()]
        load_w(wo_m[:], wout_src, wB[0])
        load_w(wq[:], xa_w_d["wq"][l], wB[1])
        load_w(wo[:], xa_w_d["wo"][l], wB[2])
        mk_r = Ring(P, [128, KC, NMEM], BF16, 2)
        mv_r = Ring(P, [128, NMEM // 128, D], BF16, 2)
        xr = Ring(P, [128, KC, 512], F32, 2)
        mr = Ring(P, [128, KC, 512], BF16, 2)
        hr = Ring(P, [128, KC, 512], BF16, 1)
        qr = Ring(P, [128, KC, 512], BF16, 1)
        otr = Ring(P, [128, KC, 512], BF16, 1)
        sqr = Ring(P, [128, KC, 512], BF16, 1)
        rr = Ring(P, [128, 512], F32, 2)
        ptr = Ring(P, [128, NMEM // 128, 512], BF16, 2)
        psr = Ring(P, [128, 512], F32, 4, psum=True)
        pso = Ring(P, [128, 512], F32, 3, psum=True)
        NKB = NMEM // 128
        for G in groups:
            nmax = 512 if G.name == "p" else G.T
            mk = mv = None
            for (s, t0, n, c0) in tiles_of(G, nmax):
                if G.name == "p":
                    if t0 == 0:
                        mk, mkb = mk_r.next()
                        P.op("sync", I("dma_start", out=mk[:], in_=mkT_s[l]), writes=[mkb], dma=P.dq())
                        mv, mvb = mv_r.next()
                        P.op("sync", I("dma_start", out=mv[:], in_=mv_s[l].rearrange("(j p) d -> p j d", p=128)),
                             writes=[mvb], dma=P.dq())
                else:
                    mk, mkb = mk_r.next()
                    P.op("gpsimd", I("dma_start", out=mk[:], in_=cmkT_d[l, s]), writes=[mkb], dma=P.dq())
                    mv, mvb = mv_r.next()
                    P.op("gpsimd", I("dma_start", out=mv[:], in_=cmv_d[l, s].rearrange("(j p) d -> p j d", p=128)),
                         writes=[mvb], dma=P.dq())
                xt, xb = xr.next()
                P.op("sync", I("dma_start", out=xt[:, :, :n], in_=fm(xT[G.name])[:, :, c0:c0 + n]),
                     writes=[xb], dma=P.dq())
                mt, mb = mr.next()
                P.op("sync", I("dma_start", out=mt[:, :, :n], in_=fm(mixT[G.name])[:, :, c0:c0 + n]),
                     writes=[mb], dma=P.dq())

                def epi_add(fo, ps, psb, xt=xt, xb=xb, n=n):
                    P.op("vector", I("tensor_tensor", out=xt[:, fo, :n], in0=ps[:, :n], in1=xt[:, fo, :n], op=ALU.add),
                         reads=[psb, xb], writes=[xb])
                linear(wo_m, wB[0], mt, mb, n, 0, KC, psr, epi_add)
                h, hb = hr.next()
                norm_tile(xt, xb, n, pvc("norm_xa_g", l), h, hb, rr, sqr, psr)
                q, qb = qr.next()

                def epi_q(fo, ps, psb, q=q, qb=qb, n=n):
                    P.op("scalar", I("copy", out=q[:, fo, :n], in_=ps[:, :n]), reads=[psb], writes=[qb])
                linear(wq, wB[1], h, hb, n, 0, KC, psr, epi_q)
                ot, otb = otr.next()
                for hh in range(4):
                    pt, ptb = ptr.next()
                    for kb in range(NKB):
                        ps, psb = psr.next()
                        for dc in range(2):
                            P.op("tensor", I("matmul",
                                ps[:, :n], mk[:, hh * 2 + dc, kb * 128:(kb + 1) * 128], q[:, hh * 2 + dc, :n], start=(dc == 0), stop=(dc == 1)),
                                reads=[mkb, qb], writes=[psb])
                        P.op("scalar", I("activation", out=pt[:, kb, :n], in_=ps[:, :n], func=AF.Exp, scale=1.0 / 16),
                             reads=[psb], writes=[ptb])
                    dn, dnb = pso.next()
                    for kb in range(NKB):
                        P.op("tensor", I("matmul", dn[:, :n], ones_b[:], pt[:, kb, :n], start=(kb == 0), stop=(kb == NKB - 1)),
                             reads=[cB, ptb], writes=[dnb])
                    rd, rdb = rr.next()
                    P.op("vector", I("reciprocal", out=rd[:, :n], in_=dn[:, :n]), reads=[dnb], writes=[rdb])
                    for dc in range(2):
                        po, pob = pso.next()
                        for kb in range(NKB):
                            P.op("tensor", I("matmul",
                                po[:, :n], mv[:, kb, hh * 256 + dc * 128:hh * 256 + (dc + 1) * 128], pt[:, kb, :n], start=(kb == 0), stop=(kb == NKB - 1)),
                                reads=[mvb, ptb], writes=[pob])
                        P.op("vector", I("tensor_tensor",
                            out=ot[:, hh * 2 + dc, :n], in0=po[:, :n], in1=rd[:, :n], op=ALU.mult), reads=[pob, rdb], writes=[otb])
                linear(wo, wB[2], ot, otb, n, 0, KC, psr, epi_add)
                P.op("sync", I("dma_start", out=fm(xT[G.name])[:, :, c0:c0 + n], in_=xt[:, :, :n]),
                     reads=[xb], dma=P.dq())
        P.end_phase()

    def phase_ffn(l, final):
        P.begin_phase()
        NTF = 256
        wg = P.sb([128, KC, DFF], BF16)
        wu = P.sb([128, KC, DFF], BF16)
        wd = P.sb([128, FC, D], BF16)
        wB = [Buf(), Buf(), Buf()]
        load_w(wg[:], ffn_g_d[l], wB[0])
        load_w(wu[:], ffn_u_d[l], wB[1])
        load_w(wd[:], ffn_d_d[l], wB[2])
        xr = Ring(P, [128, KC, NTF], F32, 2)
        hr = Ring(P, [128, KC, NTF], BF16, 1)
        sqr = Ring(P, [128, KC, NTF], BF16, 1)
        ar = Ring(P, [128, FC, NTF], BF16, 1)
        rr = Ring(P, [128, NTF], F32, 2)
        sg = Ring(P, [128, NTF], F32, 3)
        yr = Ring(P, [128, KC, NTF], F32, 1)
        psr = Ring(P, [128, 512], F32, 6, psum=True)
        for G in groups:
            for (s, t0, n, c0) in tiles_of(Group(G.name, 1, G.NT), NTF):
                xt, xb = xr.next()
                P.op("sync", I("dma_start", out=xt[:, :, :n], in_=fm(xT[G.name])[:, :, c0:c0 + n]),
                     writes=[xb], dma=P.dq())
                h, hb = hr.next()
                norm_tile(xt, xb, n, pvc("norm_ffn_g", l), h, hb, rr, sqr, psr)
                act, ab = ar.next()
                for fo in range(FC):
                    pg, pgb = psr.next()
                    pu, pub = psr.next()
                    for kc in range(KC):
                        P.op("tensor", I("matmul", pg[:, :n], wg[:, kc, fo * 128:(fo + 1) * 128], h[:, kc, :n],
                                                                                 start=(kc == 0), stop=(kc == KC - 1)), reads=[wB[0], hb], writes=[pgb])
                    for kc in range(KC):
                        P.op("tensor", I("matmul", pu[:, :n], wu[:, kc, fo * 128:(fo + 1) * 128], h[:, kc, :n],
                                                                                 start=(kc == 0), stop=(kc == KC - 1)), reads=[wB[1], hb], writes=[pub])
                    s_, sb_ = sg.next()
                    P.op("scalar", I("activation", out=s_[:, :n], in_=pg[:, :n], func=AF.Silu), reads=[pgb], writes=[sb_])
                    P.op("vector", I("tensor_tensor", out=act[:, fo, :n], in0=pu[:, :n], in1=s_[:, :n], op=ALU.mult),
                         reads=[pub, sb_], writes=[ab])

                def epi_add(fo, ps, psb, xt=xt, xb=xb, n=n):
                    P.op("vector", I("tensor_tensor", out=xt[:, fo, :n], in0=ps[:, :n], in1=xt[:, fo, :n], op=ALU.add),
                         reads=[psb, xb], writes=[xb])
                linear(wd, wB[2], act, ab, n, 0, KC, psr, epi_add, kc_n=FC)
                if not final:
                    P.op("sync", I("dma_start", out=fm(xT[G.name])[:, :, c0:c0 + n], in_=xt[:, :, :n]),
                         reads=[xb], dma=P.dq())
                else:
                    y, yb = yr.next()
                    norm_tile(xt, xb, n, pvc("final_norm_g"), y, yb, rr, sqr, psr)
                    P.op("sync", I("dma_start", out=fm(yT_o[G.name])[:, :, c0:c0 + n], in_=y[:, :, :n]),
                         reads=[yb], dma=P.dq())
        P.end_phase()

    def phase_rwkv(l):
        o_ = l // 2
        P.begin_phase()
        W = {}
        WB = {}
        for k in ("wr", "wk", "wv"):
            W[k] = P.sb([128, KC, D], BF16)
            WB[k] = Buf()
            load_w(W[k][:], rw_w_d[k][o_], WB[k])
        for k, nn in (("w1", 64), ("a1", 64), ("g1", 160), ("v1", 32)):
            if k == "v1" and o_ == 0:
                continue
            W[k] = P.sb([128, KC, nn], BF16)
            WB[k] = Buf()
            load_w(W[k][:], rw_l1_d[k][o_ if k != "v1" else o_ - 1], WB[k])
        for k, nn in (("w2", 64), ("a2", 64), ("v2", 32)):
            if k == "v2" and o_ == 0:
                continue
            W[k] = P.sb([nn, 1, D], BF16)
            WB[k] = Buf()
            load_w(W[k][:, 0, :], rw_l2_d[k][o_ if k != "v2" else o_ - 1], WB[k])
        W["g2a"] = P.sb([128, 1, D], BF16)
        W["g2b"] = P.sb([32, 1, D], BF16)
        WB["g2a"] = Buf()
        WB["g2b"] = Buf()
        load_w(W["g2a"][:, 0, :], rw_l2_d["g2"][o_, 0:128, :], WB["g2a"])
        load_w(W["g2b"][:, 0, :], rw_l2_d["g2"][o_, 128:160, :], WB["g2b"])

        NB = 128
        f32t = lambda k=1: P.sb([128, KC, NB + k - 1], F32)
        xh = f32t(2); xhB = Buf()
        hx = f32t(2); hxB = Buf()
        xx = f32t(2); xxB = Buf()
        mixr = Ring(P, [128, KC, NB], BF16, 2)
        sqr = Ring(P, [128, KC, NB + 1], BF16, 1)
        rr = Ring(P, [128, NB + 1], F32, 2)
        psr = Ring(P, [128, 512], F32, 5, psum=True)
        psY2 = [P.ps([128, 4, 128], F32), P.ps([128, 4, 128], F32)]; psYB = Buf()
        psSt = P.ps([128, KC, 64], F32); psStB = Buf()
        r_ = f32t(); k_ = f32t(); v_ = f32t(); sg_ = f32t(); a_ = f32t(); g_ = f32t()
        rB, kB, vB, sgB, aB, gB = [Buf() for _ in range(6)]
        lh = {k: P.sb([128, 2, NB], BF16) for k in ("w", "a", "g", "v")}
        lhB = {k: Buf() for k in lh}
        kk = f32t(); kkB = Buf()
        km = f32t(); kmB = Buf()
        bb_ = f32t(); bbB = Buf()
        t1 = f32t(); t1B = Buf()
        t2 = f32t(); t2B = Buf()
        vf = f32t(); vfB = Buf()
        csA = xh; csB_ = xx
        Eout = f32t(); EoutB = Buf()
        PC = P.sb([128, KC, 2], F32); PCB = Buf()
        bon = f32t(); bonB = Buf()
        AR = P.sb([128, KC, 2, NB], BF16); ARB = Buf()
        Bt = P.sb([128, KC, NB], BF16); Kt = P.sb([128, KC, NB], BF16); BKB = Buf()
        R32 = f32t(); R32B = Buf()
        tmf = Ring(P, [128, KC, NB], F32, 2)
        Z = [P.sb([128, 16, 128], BF16) for _ in range(2)]
        ZB = [[Buf() for _ in range(16)] for _ in range(2)]
        Vtm = P.sb([128, KC, 128], BF16); VtmB = Buf()
        BPtm = P.sb([128, KC, 128], BF16); BPB = Buf()
        KPtm = P.sb([128, KC, 128], BF16); KPB = Buf()
        XAs = Ring(P, [128, 2, NB], BF16, 8)
        XBs = Ring(P, [128, 2, NB], BF16, 8)
        MLr = Ring(P, [128, 2, NB], BF16, 10)
        Ls = Ring(P, [128, NB], BF16, 8)
        v16 = lambda t: t[:, :, 0:NB].rearrange("p k (a v) -> p (k a) v", v=64)
        v42 = lambda t: t[:, :, 0:NB].rearrange("p k (c v) -> p k c v", v=64)
        PhiT = v42(sg_); PhiB = sgB
        Psi = v42(t1); PsiB = t1B
        Om = P.sb([128, KC, NB], BF16); OmB = Buf()
        S16 = P.sb([128, KC, 64], BF16); S16B = Buf()
        Y0 = bb_; Y0B = bbB
        S = [P.sb([128, KC, 64], F32) for _ in range(2)]
        SB_ = [Buf(), Buf()]
        Ytm = r_; YB = rB
        st1 = P.sb([128, 16], F32); st2 = P.sb([128, 16], F32); stB = Buf()
        yc = k_; ycB = kB
        ysq = a_; ysqB = aB
        yo = Ring(P, [128, KC, NB], BF16, 2)
        yt32 = kk; yt32B = kkB

        mu0 = pvc("rw_mu", o_ * 6)
        lgc = pvc("rw_lnx_g", o_)
        lbc = pvc("rw_lnx_b", o_)

        def small_lin(Wt, wb, src, sb, nrows, n, ps, psb, f0, start=True, stop=True):
            P.op("tensor", I("matmul", ps[:, :n], Wt[:nrows, 0, f0:f0 + 128], src[:nrows, :n], start=start, stop=stop),
                 reads=[wb, sb], writes=[psb])

        cur = 0
        for G in groups:
            C = 64 if G.name == "p" else G.T
            for (s, t0, n, c0) in tiles_of(G, NB):
                nch = n // C
                nlog = int(math.log2(C))
                first = (t0 == 0)
                if first:
                    P.op("sync", I("dma_start", out=xh[:, :, 1:n + 1], in_=fm(xT[G.name])[:, :, c0:c0 + n]),
                         writes=[xhB], dma=P.dq())
                    P.op("gpsimd", I("memset", xh[:, :, 0:1], 1.0), writes=[xhB])
                else:
                    P.op("sync", I("dma_start", out=xh[:, :, 0:n + 1], in_=fm(xT[G.name])[:, :, c0 - 1:c0 + n]),
                         writes=[xhB], dma=P.dq())
                norm_tile(xh, xhB, n + 1, pvc("norm_mix_g", l), hx, hxB, rr, sqr, psr)
                if first:
                    if G.name == "p":
                        P.op("gpsimd", I("memset", hx[:, :, 0:1], 0.0), reads=[hxB], writes=[hxB])
                    else:
                        P.op("sync", I("dma_start", out=hx[:, :, 0:1], in_=sshift_d[o_, s].unsqueeze(2), allow_slow_non_contiguous=True), reads=[hxB], writes=[hxB], dma=P.dq())
                if t0 + n == G.T:
                    P.op("sync", I("dma_start", out=shift_o[G.name][o_, s].unsqueeze(2), in_=hx[:, :, n:n + 1], allow_slow_non_contiguous=True), reads=[hxB], dma=P.dq())
                P.op("vector", I("tensor_tensor", out=xx[:, :, :n], in0=hx[:, :, 0:n], in1=hx[:, :, 1:n + 1], op=ALU.subtract),
                     reads=[hxB], writes=[xxB])

                def bc(c0_):
                    return pv[:, c0_:c0_ + KC].unsqueeze(2).to_broadcast([128, KC, n])

                def mix(i, n=n):
                    m, mb = mixr.next()
                    tt, ttB = (t1, t1B) if i % 2 == 0 else (t2, t2B)
                    P.op("gpsimd", I("tensor_tensor", out=tt[:, :, :n], in0=xx[:, :, :n], in1=bc(mu0 + i * KC), op=ALU.mult),
                         reads=[xxB, cB], writes=[ttB])
                    P.op("vector", I("tensor_tensor", out=m[:, :, :n], in0=tt[:, :, :n], in1=hx[:, :, 1:n + 1], op=ALU.add),
                         reads=[ttB, hxB], writes=[mb])
                    return m, mb

                def epi_copy(dst, dB, n=n):
                    def f(fo, ps, psb):
                        P.op("scalar", I("copy", out=dst[:, fo, :n], in_=ps[:, :n]), reads=[psb], writes=[dB])
                    return f

                def lora_hidden(m, mb, key, nn, func, n=n):
                    for ci, (r0, rn) in enumerate([(0, min(nn, 128))] + ([(128, nn - 128)] if nn > 128 else [])):
                        ps, psb = psr.next()
                        for kc in range(KC):
                            P.op("tensor", I("matmul", ps[:rn, :n], W[key][:, kc, r0:r0 + rn], m[:, kc, :n],
                                                                                   start=(kc == 0), stop=(kc == KC - 1)), reads=[WB[key], mb], writes=[psb])
                        P.op("scalar", I("activation", out=lh[key[0]][:rn, ci, :n], in_=ps[:rn, :n], func=func),
                             reads=[psb], writes=[lhB[key[0]]])

                m, mb = mix(0)
                linear(W["wr"], WB["wr"], m, mb, n, 0, KC, psr, epi_copy(r_, rB))
                m, mb = mix(1)
                lora_hidden(m, mb, "w1", 64, AF.Tanh)
                w0c = pvc("rw_w0", o_)
                for fo in range(KC):
                    ps, psb = psr.next()
                    small_lin(W["w2"], WB["w2"], lh["w"][:, 0, :], lhB["w"], 64, n, ps, psb, fo * 128)
                    P.op("scalar", I("activation", out=sg_[:, fo, :n], in_=ps[:, :n], func=AF.Sigmoid,
                                                                        bias=pv[:, w0c + fo:w0c + fo + 1]), reads=[psb, cB], writes=[sgB])
                m, mb = mix(2)
                linear(W["wk"], WB["wk"], m, mb, n, 0, KC, psr, epi_copy(k_, kB))
                m, mb = mix(3)
                linear(W["wv"], WB["wv"], m, mb, n, 0, KC, psr, epi_copy(v_, vB))
                if o_ == 0:
                    P.op("sync", I("dma_start", out=fm(vfirst[G.name])[:, :, c0:c0 + n], in_=v_[:, :, :n]), reads=[vB], dma=P.dq())
                else:
                    P.op("sync", I("dma_start", out=vf[:, :, :n], in_=fm(vfirst[G.name])[:, :, c0:c0 + n]), writes=[vfB], dma=P.dq())
                    lora_hidden(m, mb, "v1", 32, AF.Copy)
                    v0c = pvc("rw_v0", 0)
                    for fo in range(KC):
                        ps, psb = psr.next()
                        small_lin(W["v2"], WB["v2"], lh["v"][:, 0, :], lhB["v"], 32, n, ps, psb, fo * 128)
                        P.op("scalar", I("activation", out=t1[:, fo, :n], in_=ps[:, :n], func=AF.Sigmoid,
                                                                            bias=pv[:, v0c + fo:v0c + fo + 1]), reads=[psb, cB], writes=[t1B])
                    P.op("vector", I("tensor_tensor", out=vf[:, :, :n], in0=vf[:, :, :n], in1=v_[:, :, :n], op=ALU.subtract),
                         reads=[vfB, vB], writes=[vfB])
                    P.op("vector", I("tensor_tensor", out=vf[:, :, :n], in0=vf[:, :, :n], in1=t1[:, :, :n], op=ALU.mult),
                         reads=[vfB, t1B], writes=[vfB])
                    P.op("vector", I("tensor_tensor", out=v_[:, :, :n], in0=v_[:, :, :n], in1=vf[:, :, :n], op=ALU.add),
                         reads=[vfB, vB], writes=[vB])
                m, mb = mix(4)
                lora_hidden(m, mb, "a1", 64, AF.Copy)
                a0c = pvc("rw_a0", o_)
                for fo in range(KC):
                    ps, psb = psr.next()
                    small_lin(W["a2"], WB["a2"], lh["a"][:, 0, :], lhB["a"], 64, n, ps, psb, fo * 128)
                    P.op("scalar", I("activation", out=a_[:, fo, :n], in_=ps[:, :n], func=AF.Sigmoid,
                                                                        bias=pv[:, a0c + fo:a0c + fo + 1]), reads=[psb, cB], writes=[aB])
                m, mb = mix(5)
                lora_hidden(m, mb, "g1", 160, AF.Sigmoid)
                for fo in range(KC):
                    ps, psb = psr.next()
                    small_lin(W["g2a"], WB["g2a"], lh["g"][:, 0, :], lhB["g"], 128, n, ps, psb, fo * 128, True, False)
                    small_lin(W["g2b"], WB["g2b"], lh["g"][:, 1, :], lhB["g"], 32, n, ps, psb, fo * 128, False, True)
                    P.op("scalar", I("copy", out=g_[:, fo, :n], in_=ps[:, :n]), reads=[psb], writes=[gB])
                if "r1" in _SKIP:
                    continue
                kkc = pvc("rw_k_k", o_)
                kac = pvc("rw_k_a", o_)
                rkc = pvc("rw_r_k", o_)
                sqb_, sqbB = sqr.next()
                P.op("gpsimd", I("tensor_tensor", out=kk[:, :, :n], in0=k_[:, :, :n], in1=bc(kkc), op=ALU.mult), reads=[kB, cB], writes=[kkB])
                P.op("scalar", I("activation", out=sqb_[:, :, :n], in_=kk[:, :, :n], func=AF.Square), reads=[kkB], writes=[sqbB])
                for g4 in range(2):
                    ps, psb = psr.next()
                    for k4 in range(4):
                        kc = g4 * 4 + k4
                        P.op("tensor", I("matmul", ps[:, k4 * 128:k4 * 128 + n], bones_b[:], sqb_[:, kc, :n], start=True, stop=True),
                             reads=[cB, sqbB], writes=[psb])
                    P.op("scalar", I("activation", out=t1[:, g4 * 4:g4 * 4 + 4, :n], in_=ps[:, :].rearrange("p (k t) -> p k t", t=128)[:, :, :n],
                                     func=AF.Sqrt), reads=[psb], writes=[t1B])
                P.op("vector", I("tensor_scalar", out=t1[:, :, :n], in0=t1[:, :, :n], scalar1=1e-12, scalar2=1.0, op0=ALU.max, op1=ALU.mult),
                     reads=[t1B], writes=[t1B])
                P.op("vector", I("reciprocal", out=t1[:, :, :n], in_=t1[:, :, :n]), reads=[t1B], writes=[t1B])
                P.op("vector", I("tensor_tensor", out=kk[:, :, :n], in0=kk[:, :, :n], in1=t1[:, :, :n], op=ALU.mult),
                     reads=[kkB, t1B], writes=[kkB])
                P.op("vector", I("tensor_scalar", out=t2[:, :, :n], in0=a_[:, :, :n], scalar1=-1.0, scalar2=1.0, op0=ALU.add, op1=ALU.mult),
                     reads=[aB], writes=[t2B])
                P.op("gpsimd", I("tensor_tensor", out=t2[:, :, :n], in0=t2[:, :, :n], in1=bc(kac), op=ALU.mult), reads=[t2B, cB], writes=[t2B])
                P.op("vector", I("scalar_tensor_tensor", out=km[:, :, :n], in0=t2[:, :, :n], scalar=1.0, in1=k_[:, :, :n],
                                 op0=ALU.add, op1=ALU.mult), reads=[t2B, kB], writes=[kmB])
                P.op("gpsimd", I("tensor_tensor", out=bb_[:, :, :n], in0=kk[:, :, :n], in1=a_[:, :, :n], op=ALU.mult),
                     reads=[kkB, aB], writes=[bbB])
                P.op("gpsimd", I("tensor_tensor", out=t2[:, :, :n], in0=r_[:, :, :n], in1=bc(rkc), op=ALU.mult), reads=[rB, cB, t2B], writes=[t2B])
                sqb2, sqb2B = sqr.next()
                P.op("vector", I("tensor_tensor", out=sqb2[:, :, :n], in0=t2[:, :, :n], in1=km[:, :, :n], op=ALU.mult), reads=[t2B, kmB], writes=[sqb2B])
                for g4 in range(2):
                    ps, psb = psr.next()
                    for k4 in range(4):
                        kc = g4 * 4 + k4
                        P.op("tensor", I("matmul", ps[:, k4 * 128:k4 * 128 + n], bones_b[:], sqb2[:, kc, :n], start=True, stop=True),
                             reads=[cB, sqb2B], writes=[psb])
                    P.op("vector", I("tensor_tensor", out=bon[:, g4 * 4:g4 * 4 + 4, :n], in0=ps[:, :].rearrange("p (k t) -> p k t", t=128)[:, :, :n],
                                     in1=v_[:, g4 * 4:g4 * 4 + 4, :n], op=ALU.mult), reads=[psb, vB], writes=[bonB])
                P.op("gpsimd", I("tensor_tensor", out=bon[:, :, :n], in0=bon[:, :, :n], in1=bc(lbc), op=ALU.add), reads=[bonB, cB], writes=[bonB])
                if "r2" in _SKIP:
                    continue
                def v4(t):
                    return t[:, :, :n].rearrange("p k (c t) -> p k c t", t=C)
                src, srcB = sg_, sgB
                sh = 1
                i = 0
                while sh < C:
                    dst, dstB = (csA, xhB) if i % 2 == 0 else (csB_, xxB)
                    P.op("vector", I("tensor_tensor", out=v4(dst)[:, :, :, sh:C], in0=v4(src)[:, :, :, sh:C], in1=v4(src)[:, :, :, 0:C - sh], op=ALU.add),
                         reads=[srcB], writes=[dstB])
                    P.op("scalar", I("copy", out=v4(dst)[:, :, :, 0:sh], in_=v4(src)[:, :, :, 0:sh]), reads=[srcB], writes=[dstB])
                    src, srcB = dst, dstB
                    sh *= 2
                    i += 1
                cs, csBuf = src, srcB
                Ein, EinB = hx, hxB
                Eprev, EprevB = t2, t2B
                Eend, EendB = vf, vfB
                P.op("scalar", I("activation", out=Ein[:, :, :n], in_=cs[:, :, :n], func=AF.Exp, scale=-CDEC), reads=[csBuf], writes=[EinB])
                P.op("scalar", I("activation", out=Eout[:, :, :n], in_=cs[:, :, :n], func=AF.Exp, scale=CDEC), reads=[csBuf], writes=[EoutB])
                P.op("vector", I("tensor_tensor", out=t1[:, :, :n], in0=cs[:, :, :n], in1=sg_[:, :, :n], op=ALU.subtract),
                     reads=[csBuf, sgB], writes=[t1B])
                P.op("scalar", I("activation", out=Eprev[:, :, :n], in_=t1[:, :, :n], func=AF.Exp, scale=-CDEC), reads=[t1B], writes=[EprevB])
                P.op("vector", I("tensor_tensor", out=v4(t1), in0=v4(cs), in1=v4(cs)[:, :, :, C - 1:C].to_broadcast([128, KC, nch, C]),
                                 op=ALU.subtract), reads=[csBuf], writes=[t1B])
                P.op("scalar", I("activation", out=Eend[:, :, :n], in_=t1[:, :, :n], func=AF.Exp, scale=CDEC), reads=[t1B], writes=[EendB])
                P.op("scalar", I("activation", out=PC[:, :, :nch], in_=v4(cs)[:, :, :, C - 1], func=AF.Exp, scale=-CDEC), reads=[csBuf], writes=[PCB])
                if "r3" in _SKIP:
                    continue
                P.op("vector", I("tensor_tensor", out=R32[:, :, :n], in0=r_[:, :, :n], in1=Ein[:, :, :n], op=ALU.mult), reads=[rB, EinB], writes=[R32B])
                P.op("gpsimd", I("tensor_copy", out=AR[:, :, 1, :n], in_=R32[:, :, :n]), reads=[R32B], writes=[ARB])
                ta_, taB = tmf.next()
                P.op("vector", I("scalar_tensor_tensor", out=ta_[:, :, :n], in0=kk[:, :, :n], scalar=-1.0, in1=Eprev[:, :, :n],
                                                                 op0=ALU.mult, op1=ALU.mult), reads=[kkB, EprevB], writes=[taB])
                P.op("gpsimd", I("tensor_copy", out=AR[:, :, 0, :n], in_=ta_[:, :, :n]), reads=[taB], writes=[ARB])
                P.op("vector", I("tensor_tensor", out=Bt[:, :, :n], in0=bb_[:, :, :n], in1=Eout[:, :, :n], op=ALU.mult), reads=[bbB, EoutB], writes=[BKB])
                P.op("gpsimd", I("tensor_tensor", out=Kt[:, :, :n], in0=km[:, :, :n], in1=Eout[:, :, :n], op=ALU.mult), reads=[kmB, EoutB], writes=[BKB])
                tb_, tbB = tmf.next()
                P.op("vector", I("tensor_tensor", out=tb_[:, :, :n], in0=bb_[:, :, :n], in1=Eend[:, :, :n], op=ALU.mult), reads=[bbB, EendB], writes=[tbB])

                def transp(src, sB, dst_fn, dB_fn, n=n):
                    for fc in range(KC):
                        ps, psb = psr.next()
                        P.op("tensor", I("transpose", out=ps[:n, 0:128], in_=src[:, fc, :n], identity=ident_f[:]),
                             reads=[sB, cB], writes=[psb])
                        dst_fn(fc, ps, psb)
                zc = cur

                def dst_A(fc, ps, psb, n=n):
                    P.op("vector", I("tensor_copy", out=Z[zc][:n, 2 * fc:2 * fc + 2, 0:64], in_=ps[:n, 0:128].rearrange("p (h k) -> p h k", k=64)),
                         reads=[psb], writes=[ZB[zc][2 * fc], ZB[zc][2 * fc + 1]])
                transp(ta_, taB, dst_A, None)

                def dst_simple(dst, dB, n=n):
                    def f(fc, ps, psb):
                        P.op("scalar", I("copy", out=dst[:n, fc, :], in_=ps[:n, 0:128]), reads=[psb], writes=[dB])
                    return f
                transp(tb_, tbB, dst_simple(BPtm, BPB), None)
                tc_, tcB = tmf.next()
                P.op("vector", I("tensor_tensor", out=tc_[:, :, :n], in0=km[:, :, :n], in1=Eend[:, :, :n], op=ALU.mult), reads=[kmB, EendB], writes=[tcB])
                transp(tc_, tcB, dst_simple(KPtm, KPB), None)
                transp(v_, vB, dst_simple(Vtm, VtmB), None)

                if "r4" in _SKIP:
                    continue
                mk3_ = mk3 if G.name == "p" else mk3s
                mkL_ = mkL if G.name == "p" else mkLs
                HG = 4

                def stage_a(hd):
                    hb0 = 64 * (hd % 2)
                    fc = hd // 2
                    hs = slice(hb0, hb0 + 64)
                    psA, psAb = psr.next()
                    P.op("tensor", I("matmul", psA[:n, 0:2 * n].rearrange("p (a t) -> p a t", a=2), Bt[hs, fc, :n], AR[hs, fc, :, :n], start=True, stop=True),
                         reads=[BKB, ARB], writes=[psAb])
                    xa_, xaB = XAs.next()
                    P.op("vector", I("tensor_tensor", out=xa_[:n, :, :n], in0=psA[:n, 0:2 * n].rearrange("p (a t) -> p a t", a=2),
                                     in1=mk3_[:n, :, :n], op=ALU.mult), reads=[psAb, cB], writes=[xaB])
                    psB_, psBb = psr.next()
                    P.op("tensor", I("matmul", psB_[:n, 0:2 * n].rearrange("p (a t) -> p a t", a=2), Kt[hs, fc, :n], AR[hs, fc, :, :n], start=True, stop=True),
                         reads=[BKB, ARB], writes=[psBb])
                    xb_, xbB = XBs.next()
                    P.op("vector", I("tensor_tensor", out=xb_[:n, :, :n], in0=psB_[:n, 0:2 * n].rearrange("p (a t) -> p a t", a=2),
                                     in1=mk3_[:n, :, :n], op=ALU.mult), reads=[psBb, cB], writes=[xbB])
                    psC, psCb = psr.next()
                    P.op("tensor", I("matmul", psC[:n, 0:n], AR[hs, fc, 0, :n], Bt[hs, fc, :n], start=True, stop=True),
                         reads=[BKB, ARB], writes=[psCb])
                    L0, L0B = Ls.next()
                    P.op("vector", I("tensor_tensor", out=L0[:n, :n], in0=psC[:n, 0:n], in1=mkL_[:n, :n], op=ALU.mult),
                         reads=[psCb, cB], writes=[L0B])
                    psG, psGb = psr.next()
                    P.op("tensor", I("matmul", psG[:n, 0:64], xb_[:n, 0, :n], Vtm[:n, fc, hs], start=True, stop=True),
                         reads=[xbB, VtmB], writes=[psGb])
                    P.op("scalar", I("copy", out=Z[zc][:n, hd, 64:128], in_=psG[:n, 0:64]), reads=[psGb], writes=[ZB[zc][hd]])
                    return dict(hd=hd, fc=fc, hs=hs, xa_=xa_, xaB=xaB, xb_=xb_, xbB=xbB, zi=zc,
                                Mj=xa_[:n, 0, :n], MjB=xaB, Lj=L0[:n, :n], LjB=L0B)

                def stage_d(st):
                    hd, fc, hs, xa_, xaB, xb_, xbB = st["hd"], st["fc"], st["hs"], st["xa_"], st["xaB"], st["xb_"], st["xbB"]
                    ZF = Z[st["zi"]]; ZFB = ZB[st["zi"]][hd]
                    psDs = []
                    for c in range(nch):
                        cs_ = slice(c * C, (c + 1) * C)
                        psD, psDb = psr.next()
                        psDs.append((psD, psDb))
                        P.op("tensor", I("matmul", psD[hs, c * 128:c * 128 + 64], ZF[cs_, hd, 0:64], BPtm[cs_, fc, hs], start=True, stop=True),
                             reads=[ZFB, BPB], writes=[psDb])
                        P.op("tensor", I("matmul", psD[hs, c * 128 + 64:c * 128 + 128], BPtm[cs_, fc, hs], ZF[cs_, hd, 64:128], start=True, stop=False),
                             reads=[ZFB, BPB], writes=[psDb])
                        P.op("tensor", I("matmul", psD[hs, c * 128 + 64:c * 128 + 128], KPtm[cs_, fc, hs], Vtm[cs_, fc, hs], start=False, stop=True),
                             reads=[KPB, VtmB], writes=[psDb])
                    for c in range(nch):
                        psD, psDb = psDs[c]
                        P.op("vector", I("scalar_tensor_tensor", out=PhiT[hs, fc, c, :], in0=ident_f[hs, hs], scalar=PC[hs, fc, c:c + 1],
                                         in1=psD[hs, c * 128:c * 128 + 64], op0=ALU.mult, op1=ALU.add), reads=[psDb, PCB, cB], writes=[PhiB])
                        P.op("vector", I("tensor_copy", out=Psi[hs, fc, c, :], in_=psD[hs, c * 128 + 64:c * 128 + 128]),
                             reads=[psDb], writes=[PsiB])
                    psO_, psOb = psr.next()
                    P.op("tensor", I("matmul", psO_[hs, 0:n], ZF[:n, hd, 0:64], xa_[:n, 1, :n], start=True, stop=True),
                         reads=[ZFB, xaB], writes=[psOb])
                    psY0, psY0b = psr.next()
                    P.op("tensor", I("matmul", psY0[hs, 0:n], ZF[:n, hd, 64:128], xa_[:n, 1, :n], start=True, stop=False),
                         reads=[ZFB, xaB], writes=[psY0b])
                    P.op("tensor", I("matmul", psY0[hs, 0:n], Vtm[:n, fc, hs], xb_[:n, 1, :n], start=False, stop=True),
                         reads=[xbB, VtmB], writes=[psY0b])
                    P.op("vector", I("tensor_tensor", out=Om[hs, fc, :n], in0=psO_[hs, 0:n], in1=R32[hs, fc, :n], op=ALU.add),
                         reads=[psOb, R32B], writes=[OmB])
                    P.op("scalar", I("copy", out=Y0[hs, fc, :n], in_=psY0[hs, 0:n]), reads=[psY0b], writes=[Y0B])

                nxt = [stage_a(hd) for hd in range(0, HG)]
                for g0 in range(0, 16, HG):
                    sts = nxt
                    for j in range(nlog):
                        pz = []
                        for st in sts:
                            psZ, psZb = psr.next()
                            P.op("tensor", I("matmul", psZ[:n, 0:128], st["Mj"], Z[st["zi"]][:n, st["hd"], :], start=True, stop=True),
                                 reads=[st["MjB"], ZB[st["zi"]][st["hd"]]], writes=[psZb])
                            pz.append((psZ, psZb))
                        for st, (psZ, psZb) in zip(sts, pz):
                            zi, hd = st["zi"], st["hd"]
                            P.op("vector", I("tensor_tensor", out=Z[1 - zi][:n, hd, :], in0=psZ[:n, 0:128], in1=Z[zi][:n, hd, :], op=ALU.add),
                                 reads=[psZb, ZB[zi][hd]], writes=[ZB[1 - zi][hd]])
                            st["zi"] = 1 - zi
                        if j < nlog - 1:
                            pq = []
                            for st in sts:
                                psS_, psSb = psr.next()
                                P.op("tensor", I("matmul", psS_[:n, 0:n], st["Lj"], st["Mj"], start=True, stop=True),
                                     reads=[st["MjB"], st["LjB"]], writes=[psSb])
                                if j < nlog - 2:
                                    P.op("tensor", I("matmul", psS_[:n, n:2 * n], st["Mj"], st["Lj"], start=True, stop=True),
                                         reads=[st["MjB"], st["LjB"]], writes=[psSb])
                                pq.append((psS_, psSb))
                            w_ = 2 if j < nlog - 2 else 1
                            for st, (psS_, psSb) in zip(sts, pq):
                                ml, mlB = MLr.next()
                                P.op("scalar", I("copy", out=ml[:n, 0:w_, :n], in_=psS_[:n, 0:w_ * n].rearrange("p (a t) -> p a t", a=w_)),
                                     reads=[psSb], writes=[mlB])
                                st["Mj"] = ml[:n, 0, :n]; st["MjB"] = mlB
                                st["Lj"] = ml[:n, 1, :n]; st["LjB"] = mlB
                    if g0 + HG < 16:
                        nxt = [stage_a(hd) for hd in range(g0 + HG, g0 + 2 * HG)]
                    for st in sts:
                        stage_d(st)
                if "r5" in _SKIP:
                    continue
                cur = zc
                if first:
                    if G.name == "p":
                        P.op("gpsimd", I("memset", S[0][:], 0.0), writes=[SB_[0]])
                    else:
                        P.op("sync", I("dma_start", out=S[0][:], in_=swkv_d[o_, s]), writes=[SB_[0]], dma=P.dq())
                    si = 0
                for c in range(nch):
                    cs_ = slice(c * C, (c + 1) * C)
                    P.op("scalar", I("copy", out=S16[:], in_=S[si][:]), reads=[SB_[si]], writes=[S16B])
                    for hd in range(16 if "cy" not in _SKIP else 0):
                        hb0 = 64 * (hd % 2); fc = hd // 2; hs = slice(hb0, hb0 + 64)
                        P.op("tensor", I("matmul", psY2[fc // 4][hs, fc % 4, cs_], S16[hs, fc, :], Om[hs, fc, cs_], start=True, stop=True),
                             reads=[OmB, S16B], writes=[psYB])
                    for hh_ in range(2 if "cy" not in _SKIP else 0):
                        hsl = slice(hh_ * 4, hh_ * 4 + 4)
                        P.op("vector", I("tensor_tensor", out=Ytm[:, hsl, cs_], in0=psY2[hh_][:, :, cs_], in1=Y0[:, hsl, cs_], op=ALU.add),
                             reads=[psYB, Y0B], writes=[YB])
                    if "cs" in _SKIP:
                        continue
                    for hd in range(16):
                        hb0 = 64 * (hd % 2); fc = hd // 2; hs = slice(hb0, hb0 + 64)
                        P.op("tensor", I("matmul", psSt[hs, fc, :], PhiT[hs, fc, c, :], S[si][hs, fc, :], start=True, stop=True),
                             reads=[PhiB, SB_[si]], writes=[psStB])
                    P.op("vector", I("tensor_tensor", out=S[1 - si][:, :, :], in0=psSt[:, :, :], in1=Psi[:, :, c, :], op=ALU.add),
                         reads=[psStB, PsiB], writes=[SB_[1 - si]])
                    si = 1 - si
                if t0 + n == G.T:
                    P.op("sync", I("dma_start", out=wkv_o[G.name][o_, s], in_=S[si][:]), reads=[SB_[si]], dma=P.dq())
                if "r6" in _SKIP:
                    continue
                yo_, yoB = yo.next()
                for fc in range(KC):
                    ps, psb = psr.next()
                    P.op("tensor", I("matmul", ps[:, :n], bones_f[:], Ytm[:, fc, :n], start=True, stop=True), reads=[cB, YB], writes=[psb])
                    P.op("vector", I("scalar_tensor_tensor", out=yc[:, fc, :n], in0=ps[:, :n], scalar=-1.0 / 64, in1=Ytm[:, fc, :n],
                                     op0=ALU.mult, op1=ALU.add), reads=[psb, YB], writes=[ycB])
                P.op("scalar", I("activation", out=ysq[:, :, :n], in_=yc[:, :, :n], func=AF.Square), reads=[ycB], writes=[ysqB])
                for fc in range(KC):
                    ps, psb = psr.next()
                    P.op("tensor", I("matmul", ps[:, :n], bones_f[:], ysq[:, fc, :n], start=True, stop=True), reads=[cB, ysqB], writes=[psb])
                    P.op("vector", I("tensor_scalar", out=yt32[:, fc, :n], in0=ps[:, :n], scalar1=1.0 / 64, scalar2=LNEPS, op0=ALU.mult, op1=ALU.add),
                         reads=[psb], writes=[yt32B])
                P.op("scalar", I("activation", out=yt32[:, :, :n], in_=yt32[:, :, :n], func=AF.Sqrt), reads=[yt32B], writes=[yt32B])
                P.op("vector", I("reciprocal", out=yt32[:, :, :n], in_=yt32[:, :, :n]), reads=[yt32B], writes=[yt32B])
                P.op("vector", I("tensor_tensor", out=yc[:, :, :n], in0=yc[:, :, :n], in1=yt32[:, :, :n], op=ALU.mult), reads=[ycB, yt32B], writes=[ycB])
                for fc in range(KC):
                    P.op("vector", I("scalar_tensor_tensor", out=yt32[:, fc, :n], in0=yc[:, fc, :n], scalar=pv[:, lgc + fc:lgc + fc + 1],
                                     in1=bon[:, fc, :n], op0=ALU.mult, op1=ALU.add), reads=[ycB, bonB, cB], writes=[yt32B])
                P.op("gpsimd", I("tensor_tensor", out=yo_[:, :, :n], in0=yt32[:, :, :n], in1=g_[:, :, :n], op=ALU.mult),
                     reads=[yt32B, gB], writes=[yoB])
                P.op("sync", I("dma_start", out=fm(mixT[G.name])[:, :, c0:c0 + n], in_=yo_[:, :, :n]), reads=[yoB], dma=P.dq())
        P.end_phase()

    plist = []
    for l in range(depth):
        if l % 2 == 0:
            plist.append((phase_even_proj, (l,)))
            plist.append((phase_even_attn, (l,)))
            plist.append((phase_out_xa, (l, w_out_d[l // 2])))
        else:
            plist.append((phase_rwkv, (l,)))
            plist.append((phase_out_xa, (l, rw_w_d["wo"][l // 2])))
        plist.append((phase_ffn, (l, l == depth - 1)))
    for f, a in plist[:_MAXPH]:
        f(*a)
    P.emit()
    return nc


_CACHE = {}
_DEPTH = DEPTH
_MAXPH = 1000
_NCORE = 8


def prep_common(inp):
    c = {}
    c["pvec"] = pack_pv(inp)
    lam = np.stack([np.concatenate([inp["ev_lam_q1"][e], inp["ev_lam_k1"][e], inp["ev_lam_q2"][e], inp["ev_lam_k2"][e]])
                    for e in range(NEVEN)]).reshape(1, -1)
    c["lam"] = np.ascontiguousarray(lam, np.float32)
    c["lnx"] = np.ascontiguousarray(np.stack([inp["rw_lnx_g"][0], inp["rw_lnx_b"][0], inp["rw_lnx_g"][1], inp["rw_lnx_b"][1]]), np.float32)
    c["ev_w_in"] = np.stack([wl(inp["ev_w_in"][e]) for e in range(NEVEN)])
    c["ev_pool_w"] = np.ascontiguousarray(np.asarray(inp["ev_pool_w"]).transpose(0, 2, 1, 3))
    c["ev_w_out"] = np.stack([wl(inp["ev_w_out"][e]) for e in range(NEVEN)])
    for k in ("wr", "wk", "wv", "wo"):
        c["rw_" + k] = np.stack([wl(inp["rw_" + k][o]) for o in range(NODD)])
    for k in ("w1", "a1", "g1", "v1"):
        a = inp["rw_" + k]
        c["rw_" + k] = np.stack([wl(a[o]) for o in range(a.shape[0])])
    for k in ("w2", "a2", "g2", "v2"):
        c["rw_" + k] = np.ascontiguousarray(inp["rw_" + k])
    for k in ("wq", "wk", "wv", "wo"):
        c["xa_" + k] = np.stack([wl(inp["xa_" + k][l]) for l in range(DEPTH)])
    c["ffn_wg"] = np.stack([wl(inp["ffn_wg"][l]) for l in range(DEPTH)])
    c["ffn_wu"] = np.stack([wl(inp["ffn_wu"][l]) for l in range(DEPTH)])
    c["ffn_wd"] = np.stack([wl(inp["ffn_wd"][l]) for l in range(DEPTH)])
    return c


def kernel(**inp):
    inp = {k: np.asarray(v) for k, v in inp.items()}
    B, TP, _ = inp["x_prompt"].shape
    SB, TS, _ = inp["x_sample"].shape
    PAST = inp["cache_diff_k"].shape[2]
    NMEM = inp["mem_prompt"].shape[1]
    NCORE = _NCORE
    NSB = SB // NCORE
    key = (TP, NSB, TS, PAST, NMEM, _DEPTH)
    if key not in _CACHE:
        _CACHE[key] = build(TP, NSB, TS, PAST, NMEM, _DEPTH)
    nc = _CACHE[key]
    common = prep_common(inp)
    in_maps = []
    for c in range(NCORE):
        b = c % B
        sb = slice(c * NSB, (c + 1) * NSB)
        m = dict(common)
        m["xT_p"] = np.ascontiguousarray(inp["x_prompt"][b].T)
        m["xT_s"] = np.ascontiguousarray(inp["x_sample"][sb].reshape(NSB * TS, D).T)
        m["memT"] = wl(np.ascontiguousarray(inp["mem_prompt"][b].T))
        m["cache_kT"] = np.ascontiguousarray(inp["cache_diff_k"][:, sb].transpose(0, 1, 3, 4, 2))
        m["cache_v"] = np.ascontiguousarray(inp["cache_diff_v"][:, sb])
        sp = inp["state_pool"][:, sb]
        m["state_pool"] = np.ascontiguousarray(sp.reshape(NEVEN, NSB, 15, 4, 128).transpose(0, 1, 4, 3, 2))
        m["state_shift"] = np.ascontiguousarray(inp["state_rw_shift"][:, sb].reshape(NODD, NSB, KC, 128).transpose(0, 1, 3, 2))
        sw = inp["state_rw_wkv"][:, sb]
        m["state_wkvT"] = np.ascontiguousarray(sw.reshape(NODD, NSB, 8, 2, 64, 64).transpose(0, 1, 3, 5, 2, 4).reshape(NODD, NSB, 128, 8, 64))
        mk = inp["cache_mem_k"][:, sb].reshape(DEPTH, NSB, NMEM, KC, 128)
        m["cache_mkT"] = np.ascontiguousarray(mk.transpose(0, 1, 4, 3, 2))
        m["cache_mv"] = np.ascontiguousarray(inp["cache_mem_v"][:, sb].reshape(DEPTH, NSB, NMEM, D))
        in_maps.append(m)
    res = run_bass_kernel_spmd(nc, in_maps, core_ids=list(range(NCORE))).results

    def unfm(a):
        return np.swapaxes(a, 0, 1).reshape((-1,) + a.shape[2:])

    y_p = np.stack([res[b]["yT_p"].T for b in range(B)])
    y_s = np.concatenate([res[c]["yT_s"].T.reshape(NSB, TS, D) for c in range(NCORE)])
    p_k = np.stack([np.stack([res[b]["kT_p"][e].T.reshape(TP, 4, 128) for b in range(B)]) for e in range(NEVEN)])
    p_v = np.stack([np.stack([res[b]["v_p"][e].reshape(TP, 4, 128) for b in range(B)]) for e in range(NEVEN)])
    s_k = np.stack([np.concatenate([res[c]["kT_s"][e].T.reshape(NSB, TS, 4, 128) for c in range(NCORE)]) for e in range(NEVEN)])
    s_v = np.stack([np.concatenate([res[c]["v_s"][e].reshape(NSB, TS, 4, 128) for c in range(NCORE)]) for e in range(NEVEN)])

    def pool_back(a):
        return a.transpose(0, 1, 4, 3, 2).reshape(a.shape[0], a.shape[1], 15, 512)

    p_pool = pool_back(np.concatenate([res[b]["pool_p"] for b in range(B)], axis=1))
    s_pool = pool_back(np.concatenate([res[c]["pool_s"] for c in range(NCORE)], axis=1))

    def shift_back(a):
        return a.transpose(0, 1, 3, 2).reshape(a.shape[0], a.shape[1], D)

    p_sh = shift_back(np.concatenate([res[b]["shift_p"] for b in range(B)], axis=1))
    s_sh = shift_back(np.concatenate([res[c]["shift_s"] for c in range(NCORE)], axis=1))

    def wkv_back(a):
        o, n = a.shape[:2]
        return a.reshape(o, n, 2, 64, 8, 64).transpose(0, 1, 4, 2, 5, 3).reshape(o, n, 16, 64, 64)

    p_wkv = wkv_back(np.concatenate([res[b]["wkv_p"] for b in range(B)], axis=1))
    s_wkv = wkv_back(np.concatenate([res[c]["wkv_s"] for c in range(NCORE)], axis=1))
    p_mk = np.stack([np.stack([unfm(res[b]["memkT"][l]).T.reshape(NMEM, 4, 256) for b in range(B)]) for l in range(DEPTH)])
    p_mv = np.stack([np.stack([res[b]["memv"][l].reshape(NMEM, 4, 256) for b in range(B)]) for l in range(DEPTH)])
    outs = (y_p, y_s, p_k, p_v, p_pool, p_sh, p_wkv, p_mk, p_mv, s_k, s_v, s_pool, s_sh, s_wkv)
    return tuple(np.ascontiguousarray(o, dtype=np.float32) for o in outs)
```

```python
import math
import numpy as np
from contextlib import ExitStack
import concourse.bass as bass
import concourse.mybir as mybir
from concourse.bass_utils import run_bass_kernel_spmd

F32 = mybir.dt.float32
BF16 = mybir.dt.bfloat16
AF = mybir.ActivationFunctionType
ALU = mybir.AluOpType
AX = mybir.AxisListType
ENGS = ["tensor", "vector", "scalar", "gpsimd", "sync"]

D = 1024
KC = 8
DFF = 2816
FC = 22
DEPTH = 4
NEVEN = 2
NODD = 2
EPS = 1e-6
LNEPS = 64e-5
CDEC = math.exp(-0.5)


class Buf:
    __slots__ = ("w", "r", "q")

    def __init__(self):
        self.w = None
        self.r = []
        self.q = None


class Prog:
    def __init__(self, nc):
        self.nc = nc
        self.es = ExitStack()
        self.ops = {e: [] for e in ENGS}
        self.sems = {}
        self.cnt = {}
        self.waited = {e: {} for e in ENGS}
        self.nsb = 0
        self.ndq = 0
        self.phase_es = None

    def sb(self, shape, dt=F32):
        self.nsb += 1
        st = self.phase_es if self.phase_es is not None else self.es
        return st.enter_context(self.nc.sbuf_tensor(f"sb{self.nsb}", list(shape), dt))

    def ps(self, shape, dt=F32):
        self.nsb += 1
        st = self.phase_es if self.phase_es is not None else self.es
        return st.enter_context(self.nc.psum_tensor(f"ps{self.nsb}", list(shape), dt))

    def begin_phase(self):
        self.barrier()
        self.phase_es = ExitStack()
        self.ndq = 0
        self.phase_id = getattr(self, "phase_id", 0) + 1

    def end_phase(self):
        self.barrier()
        self.phase_es.close()
        self.phase_es = None

    def dq(self):
        return True

    def op(self, eng, fn, reads=(), writes=(), dma=None):
        if dma is None:
            s = "e_" + eng
            inc = 1
        elif isinstance(dma, str):
            s = "d_" + dma
            inc = 16
        else:
            kb = writes[0] if writes else reads[0]
            pid = getattr(self, "phase_id", 0)
            if kb.q is None or kb.q[0] != pid:
                self.ndq += 1
                kb.q = (pid, f"q{self.ndq}")
            s = "d_" + kb.q[1]
            inc = 16
        own = "e_" + eng
        waits = {}
        wd = self.waited[eng]
        same_raw = eng in ("vector", "scalar", "gpsimd")

        def need(sv, same_ok):
            if sv is None:
                return
            sn, val = sv
            if sn == own and not same_ok:
                return
            if wd.get(sn, 0) >= val:
                return
            if waits.get(sn, 0) < val:
                waits[sn] = val

        for b in reads:
            need(b.w, same_raw)
        for b in writes:
            need(b.w, same_raw)
            for x in b.r:
                need(x, False)
        for k, v in waits.items():
            wd[k] = v
        self.cnt[s] = self.cnt.get(s, 0) + inc
        me = (s, self.cnt[s])
        for b in reads:
            b.r.append(me)
            if len(b.r) > 16:
                d = {}
                for sn, v in b.r:
                    if d.get(sn, 0) < v:
                        d[sn] = v
                b.r = list(d.items())
        for b in writes:
            b.w = me
            b.r = []
        self.ops[eng].append((list(waits.items()), fn, s, inc))
        return me

    def barrier(self):
        for eng in ENGS:
            waits = []
            for s, v in self.cnt.items():
                if self.waited[eng].get(s, 0) < v and s != "e_" + eng:
                    waits.append((s, v))
                    self.waited[eng][s] = v
            if waits:
                self.ops[eng].append((waits, None, None, 0))

    def emit(self):
        self.barrier()
        nc = self.nc
        for s in self.cnt:
            if s not in self.sems:
                self.sems[s] = self.es.enter_context(nc.semaphore(s))
        ops = self.ops
        sems = self.sems

        def run(e, lst):
            for waits, fn, s, inc in lst:
                for sn, v in waits:
                    e.wait_ge(sems[sn], v)
                if fn is not None:
                    fn(e).then_inc(sems[s], inc)

        with nc.Block() as block:
            @block.tensor
            def _(e):
                run(e, ops["tensor"])

            @block.vector
            def _(e):
                run(e, ops["vector"])

            @block.scalar
            def _(e):
                run(e, ops["scalar"])

            @block.gpsimd
            def _(e):
                run(e, ops["gpsimd"])

            @block.sync
            def _(e):
                run(e, ops["sync"])
        self.es.close()


def I(name, *a, **k):
    return lambda e: getattr(e, name)(*a, **k)


class Ring:
    def __init__(self, P, shape, dt, n, psum=False):
        self.items = []
        for _ in range(n):
            t = P.ps(shape, dt) if psum else P.sb(shape, dt)
            self.items.append((t, Buf()))
        self.i = 0

    def next(self):
        it = self.items[self.i % len(self.items)]
        self.i += 1
        return it


PV_SPECS = [("norm_mix_g", DEPTH, D), ("norm_xa_g", DEPTH, D), ("norm_ffn_g", DEPTH, D), ("final_norm_g", 1, D),
            ("ev_pool_scale", NEVEN, 512), ("ev_subln_g", NEVEN, 128),
            ("rw_mu", NODD * 6, D), ("rw_w0", NODD, D), ("rw_a0", NODD, D), ("rw_v0", 1, D),
            ("rw_k_k", NODD, D), ("rw_k_a", NODD, D), ("rw_r_k", NODD, D),
            ("rw_lnx_g", NODD, D), ("rw_lnx_b", NODD, D)]


def pv_layout():
    off = {}
    c = 0
    for name, n, ln in PV_SPECS:
        off[name] = (c, ln // 128)
        c += n * (ln // 128)
    return off, c


PV_OFF, PV_COLS = pv_layout()


def pack_pv(inp):
    out = np.zeros((128, PV_COLS), np.float32)
    for name, n, ln in PV_SPECS:
        a = np.asarray(inp[name], np.float32).reshape(n, ln // 128, 128)
        c0, w = PV_OFF[name]
        out[:, c0:c0 + n * w] = a.transpose(2, 0, 1).reshape(128, n * w)
    return out


def wl(w):
    K, F = w.shape
    return np.ascontiguousarray(w.reshape(K // 128, 128, F).transpose(1, 0, 2))


class Group:
    def __init__(self, name, nseq, T):
        self.name = name
        self.nseq = nseq
        self.T = T
        self.NT = nseq * T


import os
_SKIP = set(os.environ.get("BIS", "").split(","))


def build(TP, NSB, TS, PAST, NMEM=256, depth=DEPTH):
    nc = bass.Bass("TRN2", target_bir_lowering=False)
    P = Prog(nc)
    dram_in = {}
    dram_out = {}

    def din(name, shape, dt=F32):
        dram_in[name] = nc.dram_tensor(name, list(shape), dt, kind="ExternalInput").ap()
        return dram_in[name]

    def dout(name, shape, dt=F32):
        dram_out[name] = nc.dram_tensor(name, list(shape), dt, kind="ExternalOutput").ap()
        return dram_out[name]

    def dscr(name, shape, dt=F32):
        return nc.dram_tensor("scr_" + name, list(shape), dt, kind="Internal").ap()

    GP = Group("p", 1, TP)
    GS = Group("s", NSB, TS)
    groups = [GP, GS]
    NPB = PAST // 128

    xin = {"p": din("xT_p", [D, GP.NT]), "s": din("xT_s", [D, GS.NT])}
    pv_d = din("pvec", [128, PV_COLS])
    lam_d = din("lam", [1, NEVEN * 4 * 64])
    lnx_d = din("lnx", [NODD * 2, D])
    w_in_d = din("ev_w_in", [NEVEN, 128, KC, 2048])
    pool_w_d = din("ev_pool_w", [NEVEN, 128, 4, 128])
    w_out_d = din("ev_w_out", [NEVEN, 128, KC, D])
    rw_w_d = {k: din("rw_" + k, [NODD, 128, KC, D]) for k in ("wr", "wk", "wv", "wo")}
    rw_l1_d = {k: din("rw_" + k, [NODD if k != "v1" else 1, 128, KC, n]) for k, n in (("w1", 64), ("a1", 64), ("g1", 160), ("v1", 32))}
    rw_l2_d = {k: din("rw_" + k, [NODD if k != "v2" else 1, n, D]) for k, n in (("w2", 64), ("a2", 64), ("g2", 160), ("v2", 32))}
    xa_w_d = {k: din("xa_" + k, [DEPTH, 128, KC, D]) for k in ("wq", "wk", "wv", "wo")}
    ffn_g_d = din("ffn_wg", [DEPTH, 128, KC, DFF])
    ffn_u_d = din("ffn_wu", [DEPTH, 128, KC, DFF])
    ffn_d_d = din("ffn_wd", [DEPTH, 128, FC, D])
    memT_d = din("memT", [128, KC, NMEM])
    ckT_d = din("cache_kT", [NEVEN, NSB, 4, 128, PAST])
    cv_d = din("cache_v", [NEVEN, NSB, PAST, 4, 128])
    spool_d = din("state_pool", [NEVEN, NSB, 128, 4, 15])
    sshift_d = din("state_shift", [NODD, NSB, 128, KC])
    swkv_d = din("state_wkvT", [NODD, NSB, 128, KC, 64])
    cmkT_d = din("cache_mkT", [DEPTH, NSB, 128, KC, NMEM])
    cmv_d = din("cache_mv", [DEPTH, NSB, NMEM, D])

    yT_o = {"p": dout("yT_p", [D, GP.NT]), "s": dout("yT_s", [D, GS.NT])}
    kT_o = {"p": dout("kT_p", [NEVEN, 512, GP.NT]), "s": dout("kT_s", [NEVEN, 512, GS.NT])}
    v_o = {"p": dout("v_p", [NEVEN, GP.NT, 512]), "s": dout("v_s", [NEVEN, GS.NT, 512])}
    pool_o = {"p": dout("pool_p", [NEVEN, 1, 128, 4, 15]), "s": dout("pool_s", [NEVEN, NSB, 128, 4, 15])}
    shift_o = {"p": dout("shift_p", [NODD, 1, 128, KC]), "s": dout("shift_s", [NODD, NSB, 128, KC])}
    wkv_o = {"p": dout("wkv_p", [NODD, 1, 128, KC, 64]), "s": dout("wkv_s", [NODD, NSB, 128, KC, 64])}
    memk_o = dout("memkT", [DEPTH, 128, KC, NMEM])
    memv_o = dout("memv", [DEPTH, NMEM, D])

    xT = {g.name: dscr("x_" + g.name, [D, g.NT]) for g in groups}
    qT = {g.name: dscr("q_" + g.name, [512, g.NT], BF16) for g in groups}
    kTs = {g.name: dscr("k_" + g.name, [512, g.NT], BF16) for g in groups}
    vtm = {g.name: dscr("v_" + g.name, [g.NT, 512], BF16) for g in groups}
    mixT = {g.name: dscr("mix_" + g.name, [D, g.NT], BF16) for g in groups}
    vfirst = {g.name: dscr("vf_" + g.name, [D, g.NT]) for g in groups}
    mkT_s = dscr("mkT_s", [DEPTH, 128, KC, NMEM], BF16)
    mv_s = dscr("mv_s", [DEPTH, NMEM, D], BF16)

    def fm(ap):
        return ap.rearrange("(c p) t -> p c t", p=128)

    ident_f = P.sb([128, 128], F32)
    ident_b = P.sb([128, 128], BF16)
    ones_b = P.sb([128, 128], BF16)
    ones_f = P.sb([128, 128], F32)
    bones_b = P.sb([128, 128], BF16)
    bones_f = P.sb([128, 128], F32)
    mk3 = P.sb([128, 2, 128], F32)
    mkL = P.sb([128, 128], F32)
    mk3s = P.sb([16, 2, 16], F32)
    mkLs = P.sb([16, 16], F32)
    pv = P.sb([128, PV_COLS], F32)
    invcnt = P.sb([128, 16], F32)
    neglam = P.sb([128, NEVEN], F32)
    gsub = P.sb([128, NEVEN], F32)
    lamrow = P.sb([1, NEVEN * 4 * 64], F32)
    lamt = P.sb([1, 8], F32)
    cB = Buf()

    def gp(fn, **k):
        P.op("gpsimd", fn, **k)

    gp(I("memset", ones_f[:], 1.0), writes=[cB])
    gp(I("memset", ones_b[:], 1.0), writes=[cB])
    gp(I("memset", ident_f[:], 1.0), writes=[cB])
    gp(I("affine_select", out=ident_f[:], in_=ident_f[:], pattern=[[-1, 128]], compare_op=ALU.is_equal,
                                 fill=0.0, base=0, channel_multiplier=1), reads=[cB], writes=[cB])
    gp(I("tensor_copy", out=ident_b[:], in_=ident_f[:]), reads=[cB], writes=[cB])
    gp(I("memset", bones_f[:], 0.0), writes=[cB])
    gp(I("memset", bones_f[0:64, 0:64], 1.0), writes=[cB])
    gp(I("memset", bones_f[64:128, 64:128], 1.0), writes=[cB])
    gp(I("memset", bones_b[:], 0.0), writes=[cB])
    gp(I("memset", bones_b[0:64, 0:64], 1.0), writes=[cB])
    gp(I("memset", bones_b[64:128, 64:128], 1.0), writes=[cB])
    gp(I("memset", mk3[:], 1.0), writes=[cB])
    gp(I("memset", mkL[:], 1.0), writes=[cB])
    gp(I("affine_select", out=mk3[:, 0, :], in_=mk3[:, 0, :], pattern=[[1, 128]], compare_op=ALU.is_gt,
                                 fill=0.0, base=0, channel_multiplier=-1), reads=[cB], writes=[cB])
    gp(I("affine_select", out=mk3[:, 1, :], in_=mk3[:, 1, :], pattern=[[1, 128]], compare_op=ALU.is_ge,
                                 fill=0.0, base=0, channel_multiplier=-1), reads=[cB], writes=[cB])
    gp(I("affine_select", out=mkL[:], in_=mkL[:], pattern=[[-1, 128]], compare_op=ALU.is_gt,
                                 fill=0.0, base=0, channel_multiplier=1), reads=[cB], writes=[cB])
    gp(I("tensor_copy", out=mk3s[:], in_=mk3[0:16, :, 0:16]), reads=[cB], writes=[cB])
    gp(I("tensor_copy", out=mkLs[:], in_=mkL[0:16, 0:16]), reads=[cB], writes=[cB])
    gp(I("memset", mk3[0:64, :, 64:128], 0.0), reads=[cB], writes=[cB])
    gp(I("memset", mk3[64:128, :, 0:64], 0.0), reads=[cB], writes=[cB])
    gp(I("memset", mkL[0:64, 64:128], 0.0), reads=[cB], writes=[cB])
    gp(I("memset", mkL[64:128, 0:64], 0.0), reads=[cB], writes=[cB])
    gp(I("iota", invcnt[:], pattern=[[1, 16]], base=1, channel_multiplier=0,
                        allow_small_or_imprecise_dtypes=True), writes=[cB])
    P.op("vector", I("reciprocal", out=invcnt[:], in_=invcnt[:]), reads=[cB], writes=[cB])
    P.op("sync", I("dma_start", out=pv[:], in_=pv_d), writes=[cB], dma="c0")
    P.op("sync", I("dma_start", out=lamrow[:], in_=lam_d), writes=[cB], dma="c1")
    P.begin_phase()
    lamps = P.ps([128, 512], F32)
    for e_ in range(NEVEN if "lam" not in _SKIP else 0):
        b0 = e_ * 256
        for j in range(2):
            P.op("vector", I("tensor_tensor",
                out=lamrow[0:1, b0 + j * 128:b0 + j * 128 + 64], in0=lamrow[0:1, b0 + j * 128:b0 + j * 128 + 64],
                in1=lamrow[0:1, b0 + j * 128 + 64:b0 + j * 128 + 128], op=ALU.mult), reads=[cB], writes=[cB])
            P.op("vector", I("reduce_sum",
                out=lamt[0:1, e_ * 4 + j:e_ * 4 + j + 1], in_=lamrow[0:1, b0 + j * 128:b0 + j * 128 + 64], axis=AX.X),
                reads=[cB], writes=[cB])
        P.op("scalar", I("activation", out=lamt[0:1, e_ * 4:e_ * 4 + 2], in_=lamt[0:1, e_ * 4:e_ * 4 + 2],
                                                     func=AF.Exp), reads=[cB], writes=[cB])
        lam_init = 0.8 - 0.6 * math.exp(-0.3 * (2 * e_))
        P.op("vector", I("tensor_scalar", out=lamt[0:1, e_ * 4 + 2:e_ * 4 + 3], in0=lamt[0:1, e_ * 4 + 1:e_ * 4 + 2], scalar1=-lam_init,
                         scalar2=1.0, op0=ALU.add, op1=ALU.mult), reads=[cB], writes=[cB])
        P.op("vector", I("tensor_tensor", out=lamt[0:1, e_ * 4 + 2:e_ * 4 + 3], in0=lamt[0:1, e_ * 4 + 2:e_ * 4 + 3],
                         in1=lamt[0:1, e_ * 4:e_ * 4 + 1], op=ALU.subtract), reads=[cB], writes=[cB])
        P.op("tensor", I("matmul", lamps[:, e_:e_ + 1], ones_f[0:1, :], lamt[0:1, e_ * 4 + 2:e_ * 4 + 3],
                                                 start=True, stop=True), reads=[cB], writes=[cB])
        P.op("vector", I("tensor_copy", out=neglam[:, e_:e_ + 1], in_=lamps[:, e_:e_ + 1]),
             reads=[cB], writes=[cB])
        c0 = PV_OFF["ev_subln_g"][0] + e_
        P.op("scalar", I("mul", gsub[:, e_:e_ + 1], pv[:, c0:c0 + 1], 1.0 - lam_init), reads=[cB], writes=[cB])

    def pvc(name, idx=0):
        c0, w = PV_OFF[name]
        return c0 + idx * w

    def load_w(dst, src, buf, eng="gpsimd"):
        P.op(eng, I("dma_start", out=dst, in_=src), writes=[buf], dma=P.dq())

    def norm_tile(xt, xb, n, gcol0, out, ob, R, sq_ring, ps_ring, kc_n=KC, dnorm=D, eps=EPS, ones=None):
        ones = ones_b if ones is None else ones
        sq, sqb = sq_ring.next()
        P.op("scalar", I("activation", out=sq[:, :kc_n, :n], in_=xt[:, :kc_n, :n], func=AF.Square),
             reads=[xb], writes=[sqb])
        ps, psb = ps_ring.next()
        for kc in range(kc_n):
            P.op("tensor", I("matmul", ps[:, :n], ones[:], sq[:, kc, :n], start=(kc == 0),
                                                     stop=(kc == kc_n - 1)), reads=[sqb, cB], writes=[psb])
        rstd, rb = R.next()
        P.op("vector", I("tensor_scalar", out=rstd[:, :n], in0=ps[:, :n], scalar1=1.0 / dnorm, scalar2=eps,
                                                 op0=ALU.mult, op1=ALU.add), reads=[psb], writes=[rb])
        P.op("scalar", I("activation", out=rstd[:, :n], in_=rstd[:, :n], func=AF.Sqrt), reads=[rb], writes=[rb])
        P.op("vector", I("reciprocal", out=rstd[:, :n], in_=rstd[:, :n]), reads=[rb], writes=[rb])
        for kc in range(kc_n):
            eng = "vector"
            P.op(eng, I("scalar_tensor_tensor",
                out=out[:, kc, :n], in0=xt[:, kc, :n], scalar=pv[:, gcol0 + kc:gcol0 + kc + 1], in1=rstd[:, :n],
                op0=ALU.mult, op1=ALU.mult), reads=[xb, rb, cB], writes=[ob])

    def linear(W, wb, h, hb, n, f0, nfo, ps_ring, epi, kc_n=KC):
        for fo in range(nfo):
            ps, psb = ps_ring.next()
            for kc in range(kc_n):
                P.op("tensor", I("matmul",
                    ps[:, :n], W[:, kc, f0 + fo * 128:f0 + (fo + 1) * 128], h[:, kc, :n], start=(kc == 0),
                    stop=(kc == kc_n - 1)), reads=[wb, hb], writes=[psb])
            epi(fo, ps, psb)

    def tiles_of(G, nmax):
        res = []
        for s in range(G.nseq):
            t0 = 0
            while t0 < G.T:
                n = min(nmax, G.T - t0)
                res.append((s, t0, n, s * G.T + t0))
                t0 += n
        return res

    cp = Ring(P, [128, KC, 512], F32, 2)
    for G in (groups if "xcopy" not in _SKIP else []):
        for (s, t0, n, c0) in tiles_of(G, 512):
            t, tb = cp.next()
            P.op("sync", I("dma_start", out=t[:, :, :n], in_=fm(xin[G.name])[:, :, c0:c0 + n]),
                 writes=[tb], dma=P.dq())
            P.op("sync", I("dma_start", out=fm(xT[G.name])[:, :, c0:c0 + n], in_=t[:, :, :n]),
                 reads=[tb], dma=P.dq())
    memT = P.sb([128, KC, NMEM], BF16)
    memB = Buf()
    load_w(memT[:], memT_d, memB)
    wkr = Ring(P, [128, KC, D], BF16, 2)
    psr = Ring(P, [128, 512], F32, 4, psum=True)
    ev32 = Ring(P, [128, 512], F32, 3)
    ev16 = Ring(P, [128, 512], BF16, 3)
    for l in range(depth if "memkv" not in _SKIP else 0):
        wk, wkb = wkr.next()
        load_w(wk[:], xa_w_d["wk"][l], wkb)
        wv, wvb = wkr.next()
        load_w(wv[:], xa_w_d["wv"][l], wvb)

        def epi_k(fo, ps, psb, l=l):
            a, ab = ev32.next()
            b, bb = ev16.next()
            if "e1" not in _SKIP:
                P.op("scalar", I("copy", out=a[:, :NMEM], in_=ps[:, :NMEM]), reads=[psb], writes=[ab])
            if "e2" not in _SKIP:
                P.op("gpsimd", I("tensor_copy", out=b[:, :NMEM], in_=a[:, :NMEM]), reads=[ab], writes=[bb])
            if "e3" not in _SKIP:
                P.op("sync", I("dma_start", out=memk_o[l, :, fo, :], in_=a[:, :NMEM]), reads=[ab], dma=P.dq())
            if "e4" not in _SKIP:
                P.op("sync", I("dma_start", out=mkT_s[l, :, fo, :], in_=b[:, :NMEM]), reads=[bb], dma=P.dq())
        if "mk" not in _SKIP:
            linear(wk, wkb, memT, memB, NMEM, 0, KC, psr, epi_k)
        for kb in range(NMEM // 128 if "mvv" not in _SKIP else 0):
            for hf in range(2):
                ps, psb = psr.next()
                for kc in range(KC):
                    P.op("tensor", I("matmul",
                        ps[:, :], memT[:, kc, kb * 128:(kb + 1) * 128], wv[:, kc, hf * 512:(hf + 1) * 512],
                        start=(kc == 0), stop=(kc == KC - 1)), reads=[wvb, memB], writes=[psb])
                a, ab = ev32.next()
                b, bb = ev16.next()
                P.op("scalar", I("copy", out=a[:], in_=ps[:]), reads=[psb], writes=[ab])
                P.op("gpsimd", I("tensor_copy", out=b[:], in_=a[:]), reads=[ab], writes=[bb])
                P.op("sync", I("dma_start",
                    out=memv_o[l, kb * 128:(kb + 1) * 128, hf * 512:(hf + 1) * 512], in_=a[:]), reads=[ab], dma=P.dq())
                P.op("sync", I("dma_start",
                    out=mv_s[l, kb * 128:(kb + 1) * 128, hf * 512:(hf + 1) * 512], in_=b[:]), reads=[bb], dma=P.dq())
    P.end_phase()

    def phase_even_proj(l):
        e_ = l // 2
        P.begin_phase()
        w_in = P.sb([128, KC, 2048], BF16)
        wb = Buf()
        load_w(w_in[:], w_in_d[e_], wb)
        pw = P.sb([128, 4, 128], BF16)
        pwb = Buf()
        load_w(pw[:], pool_w_d[e_], pwb)
        xr = Ring(P, [128, KC, 512], F32, 2)
        hr = Ring(P, [128, KC, 512], BF16, 2)
        sqr = Ring(P, [128, KC, 512], BF16, 1)
        rr = Ring(P, [128, 512], F32, 2)
        psr = Ring(P, [128, 512], F32, 6, psum=True)
        ev32 = Ring(P, [128, 512], F32, 4)
        ev16 = Ring(P, [128, 512], BF16, 4)
        ubuf = P.sb([128, 4, 15 + 512], F32)
        ub = Buf()
        ta = P.sb([128, 15 + 512], F32)
        tb2 = P.sb([128, 15 + 512], F32)
        tB = Buf()
        for G in groups:
            for (s, t0, n, c0) in tiles_of(G, 512):
                if t0 == 0:
                    if G.name == "p":
                        P.op("gpsimd", I("memset", ubuf[:, :, 0:15], 0.0), writes=[ub])
                    else:
                        P.op("sync", I("dma_start", out=ubuf[:, :, 0:15], in_=spool_d[e_, s]), writes=[ub],
                             dma=P.dq())
                xt, xb = xr.next()
                P.op("sync", I("dma_start", out=xt[:, :, :n], in_=fm(xT[G.name])[:, :, c0:c0 + n]),
                     writes=[xb], dma=P.dq())
                h, hb = hr.next()
                norm_tile(xt, xb, n, pvc("norm_mix_g", l), h, hb, rr, sqr, psr)

                def epi_u(fo, ps, psb):
                    P.op("scalar", I("copy", out=ubuf[:, fo, 15:15 + n], in_=ps[:, :n]), reads=[psb], writes=[ub])
                linear(w_in, wb, h, hb, n, 0, 4, psr, epi_u)

                def epi_q(fo, ps, psb, G=G, c0=c0):
                    b, bb = ev16.next()
                    P.op("vector", I("tensor_copy", out=b[:, :n], in_=ps[:, :n]), reads=[psb], writes=[bb])
                    P.op("sync", I("dma_start", out=qT[G.name][fo * 128:(fo + 1) * 128, c0:c0 + n], in_=b[:, :n]),
                         reads=[bb], dma=P.dq())
                linear(w_in, wb, h, hb, n, 512, 4, psr, epi_q)

                def epi_k(fo, ps, psb, G=G, c0=c0):
                    a, ab = ev32.next()
                    b, bb = ev16.next()
                    P.op("scalar", I("copy", out=a[:, :n], in_=ps[:, :n]), reads=[psb], writes=[ab])
                    P.op("gpsimd", I("tensor_copy", out=b[:, :n], in_=a[:, :n]), reads=[ab], writes=[bb])
                    P.op("sync", I("dma_start", out=kT_o[G.name][e_, fo * 128:(fo + 1) * 128, c0:c0 + n], in_=a[:, :n]),
                         reads=[ab], dma=P.dq())
                    P.op("sync", I("dma_start", out=kTs[G.name][fo * 128:(fo + 1) * 128, c0:c0 + n], in_=b[:, :n]),
                         reads=[bb], dma=P.dq())
                linear(w_in, wb, h, hb, n, 1024, 4, psr, epi_k)
                for j in range((n + 127) // 128):
                    m = min(128, n - j * 128)
                    ps, psb = psr.next()
                    for kc in range(KC):
                        P.op("tensor", I("matmul",
                            ps[:m, :], h[:, kc, j * 128:j * 128 + m], w_in[:, kc, 1536:2048], start=(kc == 0),
                            stop=(kc == KC - 1)), reads=[wb, hb], writes=[psb])
                    a, ab = ev32.next()
                    b, bb = ev16.next()
                    P.op("scalar", I("copy", out=a[:m, :], in_=ps[:m, :]), reads=[psb], writes=[ab])
                    P.op("gpsimd", I("tensor_copy", out=b[:m, :], in_=a[:m, :]), reads=[ab], writes=[bb])
                    r0 = c0 + j * 128
                    P.op("sync", I("dma_start", out=v_o[G.name][e_, r0:r0 + m, :], in_=a[:m, :]),
                         reads=[ab], dma=P.dq())
                    P.op("sync", I("dma_start", out=vtm[G.name][r0:r0 + m, :], in_=b[:m, :]),
                         reads=[bb], dma=P.dq())
                L = 15 + n
                for g in range(4):
                    w = 2 << g
                    src = ubuf[:, g, :]
                    cur = None
                    sh = 1
                    for st in range(g + 1):
                        dst = ta if st % 2 == 0 else tb2
                        s_ap = src if cur is None else cur
                        lo = 2 * sh - 1
                        P.op("vector", I("tensor_tensor",
                            out=dst[:, lo:L], in0=s_ap[:, lo:L], in1=s_ap[:, lo - sh:L - sh], op=ALU.add),
                            reads=[ub, tB], writes=[tB])
                        cur = dst
                        sh *= 2
                    pl, plb = ev32.next()
                    P.op("vector", I("tensor_scalar", out=pl[:, :n], in0=cur[:, 15:15 + n], scalar1=1.0 / w, scalar2=0.0, op0=ALU.mult, op1=ALU.add),
                         reads=[tB], writes=[plb])
                    P.op("vector", I("tensor_tensor", out=pl[:, :n], in0=pl[:, :n], in1=ubuf[:, g, 15:15 + n], op=ALU.subtract),
                         reads=[ub, plb], writes=[plb])
                    if G.name == "p" and t0 == 0:
                        P.op("vector", I("tensor_tensor",
                            out=pl[:, 0:w - 1], in0=cur[:, 15:15 + w - 1], in1=invcnt[:, 0:w - 1], op=ALU.mult),
                            reads=[tB, cB, plb], writes=[plb])
                        P.op("vector", I("tensor_tensor",
                            out=pl[:, 0:w - 1], in0=pl[:, 0:w - 1], in1=ubuf[:, g, 15:15 + w - 1], op=ALU.subtract),
                            reads=[ub, plb], writes=[plb])
                    pb_, pbb = ev16.next()
                    P.op("gpsimd", I("tensor_copy", out=pb_[:, :n], in_=pl[:, :n]), reads=[plb], writes=[pbb])
                    ps, psb = psr.next()
                    P.op("tensor", I("matmul", ps[:, :n], pw[:, g, :], pb_[:, :n], start=True, stop=True),
                         reads=[pwb, pbb], writes=[psb])
                    ob_, obb = ev16.next()
                    sc = pvc("ev_pool_scale", e_) + g
                    P.op("scalar", I("mul", ob_[:, :n], ps[:, :n], pv[:, sc:sc + 1]),
                         reads=[psb, cB], writes=[obb])
                    P.op("sync", I("dma_start",
                        out=mixT[G.name][g * 128:(g + 1) * 128, c0:c0 + n], in_=ob_[:, :n]), reads=[obb], dma=P.dq())
                if t0 + n == G.T:
                    P.op("sync", I("dma_start", out=pool_o[G.name][e_, s], in_=ubuf[:, :, n:n + 15]),
                         reads=[ub], dma=P.dq())
                else:
                    P.op("vector", I("tensor_copy", out=ubuf[:, :, 0:15], in_=ubuf[:, :, n:n + 15]), reads=[ub], writes=[ub])
        P.end_phase()

    def phase_even_attn(l):
        e_ = l // 2
        scale = 64 ** -0.5
        P.begin_phase()
        psS = Ring(P, [128, 512], F32, 3, psum=True)
        psO = [Ring(P, [128, 512], F32, 1, psum=True) for _ in range(2)]
        psD = [Ring(P, [128, 512], F32, 1, psum=True) for _ in range(2)]
        ptr = Ring(P, [128, 512], BF16, 6)
        tmp = Ring(P, [128, 512], F32, 6)
        sqr = Ring(P, [128, 1, 512], BF16, 2)
        o16 = Ring(P, [128, 1, 512], BF16, 2)

        def finish(o_, ob, d_, db, n, dst_rows, G, c0):
            a = []
            for m in range(2):
                r, rb = tmp.next()
                P.op("vector", I("reciprocal", out=r[:, :n], in_=d_[m][:, :n]), reads=[db[m]], writes=[rb])
                t, tb = tmp.next()
                P.op("vector", I("tensor_tensor", out=t[:, :n], in0=o_[m][:, :n], in1=r[:, :n], op=ALU.mult),
                     reads=[ob[m], rb], writes=[tb])
                a.append((t, tb))
            av, avb = tmp.next()
            P.op("vector", I("scalar_tensor_tensor", out=av[:, :n], in0=a[1][0][:, :n], scalar=neglam[:, e_:e_ + 1],
                                                             in1=a[0][0][:, :n], op0=ALU.mult, op1=ALU.add),
                 reads=[a[0][1], a[1][1], cB], writes=[avb])
            out, outb = o16.next()
            sq, sqb = sqr.next()
            P.op("scalar", I("activation", out=sq[:, 0, :n], in_=av[:, :n], func=AF.Square), reads=[avb], writes=[sqb])
            ps, psb = psS.next()
            P.op("tensor", I("matmul", ps[:, :n], ones_b[:], sq[:, 0, :n], start=True, stop=True), reads=[sqb, cB], writes=[psb])
            rstd, rb = tmp.next()
            P.op("vector", I("tensor_scalar", out=rstd[:, :n], in0=ps[:, :n], scalar1=1.0 / 128, scalar2=EPS,
                                                     op0=ALU.mult, op1=ALU.add), reads=[psb], writes=[rb])
            P.op("scalar", I("activation", out=rstd[:, :n], in_=rstd[:, :n], func=AF.Sqrt), reads=[rb], writes=[rb])
            P.op("vector", I("reciprocal", out=rstd[:, :n], in_=rstd[:, :n]), reads=[rb], writes=[rb])
            P.op("vector", I("scalar_tensor_tensor", out=out[:, 0, :n], in0=av[:, :n], scalar=gsub[:, e_:e_ + 1],
                                                             in1=rstd[:, :n], op0=ALU.mult, op1=ALU.mult),
                 reads=[avb, rb, cB], writes=[outb])
            P.op("sync", I("dma_start", out=mixT[G.name][dst_rows:dst_rows + 128, c0:c0 + n], in_=out[:, 0, :n]),
                 reads=[outb], dma=P.dq())

        G = GP
        T = G.T
        kh_r = Ring(P, [128, T], BF16, 2)
        vh_r = Ring(P, [128, max(T // 128, 1), 128], BF16, 2)
        qt_r = Ring(P, [128, 512], BF16, 2)
        for hd in range(4):
            kh, khb = kh_r.next()
            P.op("sync", I("dma_start", out=kh[:, :], in_=kTs["p"][hd * 128:(hd + 1) * 128, :]), writes=[khb], dma=P.dq())
            vh, vhb = vh_r.next()
            P.op("sync", I("dma_start",
                out=vh[:, :, :], in_=vtm["p"][:, hd * 128:(hd + 1) * 128].rearrange("(j p) e -> p j e", p=128)), writes=[vhb], dma=P.dq())
            for (s, t0, n, c0) in tiles_of(G, 512):
                qt, qtb = qt_r.next()
                P.op("sync", I("dma_start", out=qt[:, :n], in_=qT["p"][hd * 128:(hd + 1) * 128, c0:c0 + n]),
                     writes=[qtb], dma=P.dq())
                o_ = [psO[m].next() for m in range(2)]
                d_ = [psD[m].next() for m in range(2)]
                nkb = (t0 + n) // 128

                def pv_block(items):
                    for (pt, ptb, m, jb, q0, diag, first, last) in items:
                        P.op("tensor", I("matmul", o_[m][0][:, q0:n], vh[:, jb, :], pt[:, q0:n], start=first, stop=last),
                             reads=[vhb, ptb], writes=[o_[m][1]])
                        P.op("tensor", I("matmul", d_[m][0][:, q0:n], ones_b[:], pt[:, q0:n], start=first, stop=last),
                             reads=[cB, ptb], writes=[d_[m][1]])

                pending = None
                for j in range(nkb):
                    jj = j - t0 // 128
                    q0 = 0 if jj < 0 else 128 * jj
                    first = (j == 0)
                    last = (j == nkb - 1)
                    items = []
                    for m in range(2):
                        ps, psb = psS.next()
                        P.op("tensor", I("matmul", ps[:, q0:n], kh[64 * m:64 * m + 64, j * 128:(j + 1) * 128], qt[64 * m:64 * m + 64, q0:n],
                                         start=True, stop=True), reads=[khb, qtb], writes=[psb])
                        pt, ptb = ptr.next()
                        P.op("scalar", I("activation", out=pt[:, q0:n], in_=ps[:, q0:n], func=AF.Exp, scale=scale),
                             reads=[psb], writes=[ptb])
                        if jj >= 0:
                            P.op("gpsimd", I("memset", pt[64:128, q0:q0 + 64], 0.0), reads=[ptb], writes=[ptb])
                        items.append((pt, ptb, m, j, q0, jj >= 0, first, last))
                    if pending is not None:
                        pv_block(pending)
                    pending = items
                pv_block(pending)
                finish([o_[0][0], o_[1][0]], [o_[0][1], o_[1][1]], [d_[0][0], d_[1][0]], [d_[0][1], d_[1][1]], n,
                       512 + hd * 128, G, c0)
        G = GS
        TSq = G.T
        kc_r = Ring(P, [128, PAST], BF16, 2)
        vc_r = Ring(P, [128, NPB, 128], BF16, 2)
        kn_r = Ring(P, [128, 16], BF16, 2)
        vn_r = Ring(P, [16, 128], BF16, 2)
        pts_r = Ring(P, [128, 2, NPB, 16], BF16, 2)
        ptn_r = Ring(P, [16, 2, 16], BF16, 2)
        psQ = Ring(P, [128, 512], F32, 1, psum=True)
        for s in range(G.nseq):
            c0 = s * TSq
            for hd in range(4):
                kc, kcb = kc_r.next()
                P.op("gpsimd", I("dma_start", out=kc[:, :], in_=ckT_d[e_, s, hd]), writes=[kcb], dma=P.dq())
                vc, vcb = vc_r.next()
                P.op("gpsimd", I("dma_start",
                    out=vc[:, :, :], in_=cv_d[e_, s, :, hd, :].rearrange("(j p) e -> p j e", p=128)), writes=[vcb], dma=P.dq())
                kn, knb = kn_r.next()
                P.op("sync", I("dma_start", out=kn[:, :], in_=kTs["s"][hd * 128:(hd + 1) * 128, c0:c0 + TSq]),
                     writes=[knb], dma=P.dq())
                vn, vnb = vn_r.next()
                P.op("sync", I("dma_start", out=vn[:, :], in_=vtm["s"][c0:c0 + TSq, hd * 128:(hd + 1) * 128]),
                     writes=[vnb], dma=P.dq())
                qt, qtb = qt_r.next()
                P.op("sync", I("dma_start", out=qt[:, :TSq], in_=qT["s"][hd * 128:(hd + 1) * 128, c0:c0 + TSq]),
                     writes=[qtb], dma=P.dq())
                pts, ptsb = pts_r.next()
                ptn, ptnb = ptn_r.next()
                psn, psnb = psQ.next()
                for m in range(2):
                    ps, psb = psS.next()
                    for j in range(NPB):
                        P.op("tensor", I("matmul",
                            ps[:, j * 16:(j + 1) * 16], kc[64 * m:64 * m + 64, j * 128:(j + 1) * 128], qt[64 * m:64 * m + 64, :TSq],
                            start=True, stop=True), reads=[kcb, qtb], writes=[psb])
                    P.op("scalar", I("activation",
                        out=pts[:, m, :, :], in_=ps[:, :NPB * 16].rearrange("p (j q) -> p j q", q=16), func=AF.Exp, scale=scale),
                        reads=[psb], writes=[ptsb])
                    P.op("tensor", I("matmul", psn[:16, m * 16:(m + 1) * 16], kn[64 * m:64 * m + 64, :], qt[64 * m:64 * m + 64, :TSq],
                                                           start=True, stop=True), reads=[knb, qtb], writes=[psnb])
                P.op("scalar", I("activation", out=ptn[:, :, :], in_=psn[:16, 0:32].rearrange("p (m q) -> p m q", q=16),
                                                      func=AF.Exp, scale=scale), reads=[psnb], writes=[ptnb])
                o_ = [psO[m].next() for m in range(2)]
                d_ = [psD[m].next() for m in range(2)]
                for m in range(2):
                    for j in range(NPB):
                        P.op("tensor", I("matmul", o_[m][0][:, :TSq], vc[:, j, :], pts[:, m, j, :], start=(j == 0), stop=False),
                             reads=[vcb, ptsb], writes=[o_[m][1]])
                    P.op("tensor", I("matmul", o_[m][0][:, :TSq], vn[:, :], ptn[:, m, :], start=False, stop=True),
                         reads=[vnb, ptnb], writes=[o_[m][1]])
                    for j in range(NPB):
                        P.op("tensor", I("matmul", d_[m][0][:, :TSq], ones_b[:], pts[:, m, j, :], start=(j == 0), stop=False),
                             reads=[cB, ptsb], writes=[d_[m][1]])
                    P.op("tensor", I("matmul", d_[m][0][:, :TSq], ones_b[0:16, :], ptn[:, m, :], start=False, stop=True),
                         reads=[cB, ptnb], writes=[d_[m][1]])
                finish([o_[0][0], o_[1][0]], [o_[0][1], o_[1][1]], [d_[0][0], d_[1][0]], [d_[0][1], d_[1][1]], TSq,
                       512 + hd * 128, G, c0)
        P.end_phase()

    def phase_out_xa(l, wout_src):
        P.begin_phase()
        wo_m = P.sb([128, KC, D], BF16)
        wq = P.sb([128, KC, D], BF16)
        wo = P.sb([128, KC, D], BF16)
        wB = [Buf(), Buf(), Buf()]
        load_w(wo_m[:], wout_src, wB[0])
        load_w(wq[:], xa_w_d["wq"][l], wB[1])
        load_w(wo[:], xa_w_d["wo"][l], wB[2])
        mk_r = Ring(P, [128, KC, NMEM], BF16, 2)
        mv_r = Ring(P, [128, NMEM // 128, D], BF16, 2)
        xr = Ring(P, [128, KC, 512], F32, 2)
        mr = Ring(P, [128, KC, 512], BF16, 2)
        hr = Ring(P, [128, KC, 512], BF16, 1)
        qr = Ring(P, [128, KC, 512], BF16, 1)
        otr = Ring(P, [128, KC, 512], BF16, 1)
        sqr = Ring(P, [128, KC, 512], BF16, 1)
        rr = Ring(P, [128, 512], F32, 2)
        ptr = Ring(P, [128, NMEM // 128, 512], BF16, 2)
        psr = Ring(P, [128, 512], F32, 4, psum=True)
        pso = Ring(P, [128, 512], F32, 3, psum=True)
        NKB = NMEM // 128
        for G in groups:
            nmax = 512 if G.name == "p" else G.T
            mk = mv = None
            for (s, t0, n, c0) in tiles_of(G, nmax):
                if G.name == "p":
                    if t0 == 0:
                        mk, mkb = mk_r.next()
                        P.op("sync", I("dma_start", out=mk[:], in_=mkT_s[l]), writes=[mkb], dma=P.dq())
                        mv, mvb = mv_r.next()
                        P.op("sync", I("dma_start", out=mv[:], in_=mv_s[l].rearrange("(j p) d -> p j d", p=128)),
                             writes=[mvb], dma=P.dq())
                else:
                    mk, mkb = mk_r.next()
                    P.op("gpsimd", I("dma_start", out=mk[:], in_=cmkT_d[l, s]), writes=[mkb], dma=P.dq())
                    mv, mvb = mv_r.next()
                    P.op("gpsimd", I("dma_start", out=mv[:], in_=cmv_d[l, s].rearrange("(j p) d -> p j d", p=128)),
                         writes=[mvb], dma=P.dq())
                xt, xb = xr.next()
                P.op("sync", I("dma_start", out=xt[:, :, :n], in_=fm(xT[G.name])[:, :, c0:c0 + n]),
                     writes=[xb], dma=P.dq())
                mt, mb = mr.next()
                P.op("sync", I("dma_start", out=mt[:, :, :n], in_=fm(mixT[G.name])[:, :, c0:c0 + n]),
                     writes=[mb], dma=P.dq())

                def epi_add(fo, ps, psb, xt=xt, xb=xb, n=n):
                    P.op("vector", I("tensor_tensor", out=xt[:, fo, :n], in0=ps[:, :n], in1=xt[:, fo, :n], op=ALU.add),
                         reads=[psb, xb], writes=[xb])
                linear(wo_m, wB[0], mt, mb, n, 0, KC, psr, epi_add)
                h, hb = hr.next()
                norm_tile(xt, xb, n, pvc("norm_xa_g", l), h, hb, rr, sqr, psr)
                q, qb = qr.next()

                def epi_q(fo, ps, psb, q=q, qb=qb, n=n):
                    P.op("scalar", I("copy", out=q[:, fo, :n], in_=ps[:, :n]), reads=[psb], writes=[qb])
                linear(wq, wB[1], h, hb, n, 0, KC, psr, epi_q)
                ot, otb = otr.next()
                def xa_scores(hh):
                    pt, ptb = ptr.next()
                    for kb in range(NKB):
                        ps, psb = psr.next()
                        for dc in range(2):
                            P.op("tensor", I("matmul", ps[:, :n], mk[:, hh * 2 + dc, kb * 128:(kb + 1) * 128], q[:, hh * 2 + dc, :n],
                                             start=(dc == 0), stop=(dc == 1)), reads=[mkb, qb], writes=[psb])
                        P.op("scalar", I("activation", out=pt[:, kb, :n], in_=ps[:, :n], func=AF.Exp, scale=1.0 / 16),
                             reads=[psb], writes=[ptb])
                    return pt, ptb

                def xa_pv(hh, pt, ptb):
                    dn, dnb = pso.next()
                    for kb in range(NKB):
                        P.op("tensor", I("matmul", dn[:, :n], ones_b[:], pt[:, kb, :n], start=(kb == 0), stop=(kb == NKB - 1)),
                             reads=[cB, ptb], writes=[dnb])
                    rd, rdb = rr.next()
                    P.op("vector", I("reciprocal", out=rd[:, :n], in_=dn[:, :n]), reads=[dnb], writes=[rdb])
                    for dc in range(2):
                        po, pob = pso.next()
                        for kb in range(NKB):
                            P.op("tensor", I("matmul", po[:, :n], mv[:, kb, hh * 256 + dc * 128:hh * 256 + (dc + 1) * 128], pt[:, kb, :n],
                                             start=(kb == 0), stop=(kb == NKB - 1)), reads=[mvb, ptb], writes=[pob])
                        P.op("vector", I("tensor_tensor", out=ot[:, hh * 2 + dc, :n], in0=po[:, :n], in1=rd[:, :n], op=ALU.mult),
                             reads=[pob, rdb], writes=[otb])

                prev = xa_scores(0)
                for hh in range(4):
                    nxt_ = xa_scores(hh + 1) if hh + 1 < 4 else None
                    xa_pv(hh, *prev)
                    prev = nxt_
                linear(wo, wB[2], ot, otb, n, 0, KC, psr, epi_add)
                P.op("sync", I("dma_start", out=fm(xT[G.name])[:, :, c0:c0 + n], in_=xt[:, :, :n]),
                     reads=[xb], dma=P.dq())
        P.end_phase()

    def phase_ffn(l, final):
        P.begin_phase()
        NTF = 512
        wg = P.sb([128, KC, DFF], BF16)
        wu = P.sb([128, KC, DFF], BF16)
        wd = P.sb([128, FC, D], BF16)
        wB = [Buf(), Buf(), Buf()]
        load_w(wg[:], ffn_g_d[l], wB[0])
        load_w(wu[:], ffn_u_d[l], wB[1])
        load_w(wd[:], ffn_d_d[l], wB[2])
        xr = Ring(P, [128, KC, NTF], F32, 2)
        hr = Ring(P, [128, KC, NTF], BF16, 1)
        ar = Ring(P, [128, FC, NTF], BF16, 1)
        sqr = ar
        rr = Ring(P, [128, NTF], F32, 1)
        sg = Ring(P, [128, NTF], F32, 2)
        psr = Ring(P, [128, 512], F32, 6, psum=True)
        for G in groups:
            for (s, t0, n, c0) in tiles_of(Group(G.name, 1, G.NT), NTF):
                xt, xb = xr.next()
                P.op("sync", I("dma_start", out=xt[:, :, :n], in_=fm(xT[G.name])[:, :, c0:c0 + n]),
                     writes=[xb], dma=P.dq())
                h, hb = hr.next()
                norm_tile(xt, xb, n, pvc("norm_ffn_g", l), h, hb, rr, sqr, psr)
                act, ab = ar.next()
                for fo in range(FC):
                    pg, pgb = psr.next()
                    pu, pub = psr.next()
                    for kc in range(KC):
                        P.op("tensor", I("matmul", pg[:, :n], wg[:, kc, fo * 128:(fo + 1) * 128], h[:, kc, :n],
                                                                                 start=(kc == 0), stop=(kc == KC - 1)), reads=[wB[0], hb], writes=[pgb])
                    for kc in range(KC):
                        P.op("tensor", I("matmul", pu[:, :n], wu[:, kc, fo * 128:(fo + 1) * 128], h[:, kc, :n],
                                                                                 start=(kc == 0), stop=(kc == KC - 1)), reads=[wB[1], hb], writes=[pub])
                    s_, sb_ = sg.next()
                    P.op("scalar", I("activation", out=s_[:, :n], in_=pg[:, :n], func=AF.Silu), reads=[pgb], writes=[sb_])
                    P.op("vector", I("tensor_tensor", out=act[:, fo, :n], in0=pu[:, :n], in1=s_[:, :n], op=ALU.mult),
                         reads=[pub, sb_], writes=[ab])

                def epi_add(fo, ps, psb, xt=xt, xb=xb, n=n):
                    P.op("vector", I("tensor_tensor", out=xt[:, fo, :n], in0=ps[:, :n], in1=xt[:, fo, :n], op=ALU.add),
                         reads=[psb, xb], writes=[xb])
                linear(wd, wB[2], act, ab, n, 0, KC, psr, epi_add, kc_n=FC)
                if not final:
                    P.op("sync", I("dma_start", out=fm(xT[G.name])[:, :, c0:c0 + n], in_=xt[:, :, :n]),
                         reads=[xb], dma=P.dq())
                else:
                    y, yb = xt, xb
                    norm_tile(xt, xb, n, pvc("final_norm_g"), y, yb, rr, sqr, psr)
                    P.op("sync", I("dma_start", out=fm(yT_o[G.name])[:, :, c0:c0 + n], in_=y[:, :, :n]),
                         reads=[yb], dma=P.dq())
        P.end_phase()

    def phase_rwkv(l):
        o_ = l // 2
        P.begin_phase()
        W = {}
        WB = {}
        for k in ("wr", "wk", "wv"):
            W[k] = P.sb([128, KC, D], BF16)
            WB[k] = Buf()
            load_w(W[k][:], rw_w_d[k][o_], WB[k])
        for k, nn in (("w1", 64), ("a1", 64), ("g1", 160), ("v1", 32)):
            if k == "v1" and o_ == 0:
                continue
            W[k] = P.sb([128, KC, nn], BF16)
            WB[k] = Buf()
            load_w(W[k][:], rw_l1_d[k][o_ if k != "v1" else o_ - 1], WB[k])
        for k, nn in (("w2", 64), ("a2", 64), ("v2", 32)):
            if k == "v2" and o_ == 0:
                continue
            W[k] = P.sb([nn, 1, D], BF16)
            WB[k] = Buf()
            load_w(W[k][:, 0, :], rw_l2_d[k][o_ if k != "v2" else o_ - 1], WB[k])
        W["g2a"] = P.sb([128, 1, D], BF16)
        W["g2b"] = P.sb([32, 1, D], BF16)
        WB["g2a"] = Buf()
        WB["g2b"] = Buf()
        load_w(W["g2a"][:, 0, :], rw_l2_d["g2"][o_, 0:128, :], WB["g2a"])
        load_w(W["g2b"][:, 0, :], rw_l2_d["g2"][o_, 128:160, :], WB["g2b"])

        NB = 128
        f32t = lambda k=1: P.sb([128, KC, NB + k - 1], F32)
        xh = f32t(2); xhB = Buf()
        hx = f32t(2); hxB = Buf()
        xx = f32t(2); xxB = Buf()
        mixr = Ring(P, [128, KC, NB], BF16, 2)
        sqr = Ring(P, [128, KC, NB + 1], BF16, 1)
        rr = Ring(P, [128, NB + 1], F32, 2)
        psr = Ring(P, [128, 512], F32, 5, psum=True)
        psY2 = [P.ps([128, 4, 128], F32), P.ps([128, 4, 128], F32)]; psYB = Buf()
        psSt = P.ps([128, KC, 64], F32); psStB = Buf()
        r_ = f32t(); k_ = f32t(); v_ = f32t(); sg_ = f32t(); a_ = f32t(); g_ = f32t()
        rB, kB, vB, sgB, aB, gB = [Buf() for _ in range(6)]
        lh = {k: P.sb([128, 2, NB], BF16) for k in ("w", "a", "g", "v")}
        lhB = {k: Buf() for k in lh}
        kk = f32t(); kkB = Buf()
        km = f32t(); kmB = Buf()
        bb_ = f32t(); bbB = Buf()
        t1 = f32t(); t1B = Buf()
        t2 = f32t(); t2B = Buf()
        vf = f32t(); vfB = Buf()
        csA = xh; csB_ = xx
        Eout = f32t(); EoutB = Buf()
        PC = P.sb([128, KC, 2], F32); PCB = Buf()
        bon = f32t(); bonB = Buf()
        AR = P.sb([128, KC, 2, NB], BF16); ARB = Buf()
        Bt = P.sb([128, KC, NB], BF16); Kt = P.sb([128, KC, NB], BF16); BKB = Buf()
        R32 = f32t(); R32B = Buf()
        tmf = Ring(P, [128, KC, NB], F32, 2)
        Z = [P.sb([128, 16, 128], BF16) for _ in range(2)]
        ZB = [[Buf() for _ in range(16)] for _ in range(2)]
        Vtm = P.sb([128, KC, 128], BF16); VtmB = Buf()
        BPtm = P.sb([128, KC, 128], BF16); BPB = Buf()
        KPtm = P.sb([128, KC, 128], BF16); KPB = Buf()
        XAs = Ring(P, [128, 2, NB], BF16, 8)
        XBs = Ring(P, [128, 2, NB], BF16, 8)
        MLr = Ring(P, [128, 2, NB], BF16, 10)
        Ls = Ring(P, [128, NB], BF16, 8)
        v16 = lambda t: t[:, :, 0:NB].rearrange("p k (a v) -> p (k a) v", v=64)
        v42 = lambda t: t[:, :, 0:NB].rearrange("p k (c v) -> p k c v", v=64)
        PhiT = v42(sg_); PhiB = sgB
        Psi = v42(t1); PsiB = t1B
        Om = P.sb([128, KC, NB], BF16); OmB = Buf()
        S16 = P.sb([128, KC, 64], BF16); S16B = Buf()
        Y0 = bb_; Y0B = bbB
        S = [P.sb([128, KC, 64], F32) for _ in range(2)]
        SB_ = [Buf(), Buf()]
        Ytm = r_; YB = rB
        st1 = P.sb([128, 16], F32); st2 = P.sb([128, 16], F32); stB = Buf()
        yc = k_; ycB = kB
        ysq = a_; ysqB = aB
        yo = Ring(P, [128, KC, NB], BF16, 2)
        yt32 = kk; yt32B = kkB

        mu0 = pvc("rw_mu", o_ * 6)
        lgc = pvc("rw_lnx_g", o_)
        lbc = pvc("rw_lnx_b", o_)

        def small_lin(Wt, wb, src, sb, nrows, n, ps, psb, f0, start=True, stop=True):
            P.op("tensor", I("matmul", ps[:, :n], Wt[:nrows, 0, f0:f0 + 128], src[:nrows, :n], start=start, stop=stop),
                 reads=[wb, sb], writes=[psb])

        cur = 0
        for G in groups:
            C = 64 if G.name == "p" else G.T
            for (s, t0, n, c0) in tiles_of(G, NB):
                nch = n // C
                nlog = int(math.log2(C))
                first = (t0 == 0)
                if first:
                    P.op("sync", I("dma_start", out=xh[:, :, 1:n + 1], in_=fm(xT[G.name])[:, :, c0:c0 + n]),
                         writes=[xhB], dma=P.dq())
                    P.op("gpsimd", I("memset", xh[:, :, 0:1], 1.0), writes=[xhB])
                else:
                    P.op("sync", I("dma_start", out=xh[:, :, 0:n + 1], in_=fm(xT[G.name])[:, :, c0 - 1:c0 + n]),
                         writes=[xhB], dma=P.dq())
                norm_tile(xh, xhB, n + 1, pvc("norm_mix_g", l), hx, hxB, rr, sqr, psr)
                if first:
                    if G.name == "p":
                        P.op("gpsimd", I("memset", hx[:, :, 0:1], 0.0), reads=[hxB], writes=[hxB])
                    else:
                        P.op("sync", I("dma_start", out=hx[:, :, 0:1], in_=sshift_d[o_, s].unsqueeze(2), allow_slow_non_contiguous=True), reads=[hxB], writes=[hxB], dma=P.dq())
                if t0 + n == G.T:
                    P.op("sync", I("dma_start", out=shift_o[G.name][o_, s].unsqueeze(2), in_=hx[:, :, n:n + 1], allow_slow_non_contiguous=True), reads=[hxB], dma=P.dq())
                P.op("vector", I("tensor_tensor", out=xx[:, :, :n], in0=hx[:, :, 0:n], in1=hx[:, :, 1:n + 1], op=ALU.subtract),
                     reads=[hxB], writes=[xxB])

                def bc(c0_):
                    return pv[:, c0_:c0_ + KC].unsqueeze(2).to_broadcast([128, KC, n])

                def mix(i, n=n):
                    m, mb = mixr.next()
                    tt, ttB = (t1, t1B) if i % 2 == 0 else (t2, t2B)
                    P.op("gpsimd", I("tensor_tensor", out=tt[:, :, :n], in0=xx[:, :, :n], in1=bc(mu0 + i * KC), op=ALU.mult),
                         reads=[xxB, cB], writes=[ttB])
                    P.op("vector", I("tensor_tensor", out=m[:, :, :n], in0=tt[:, :, :n], in1=hx[:, :, 1:n + 1], op=ALU.add),
                         reads=[ttB, hxB], writes=[mb])
                    return m, mb

                def epi_copy(dst, dB, n=n):
                    def f(fo, ps, psb):
                        P.op("scalar", I("copy", out=dst[:, fo, :n], in_=ps[:, :n]), reads=[psb], writes=[dB])
                    return f

                def lora_hidden(m, mb, key, nn, func, n=n):
                    for ci, (r0, rn) in enumerate([(0, min(nn, 128))] + ([(128, nn - 128)] if nn > 128 else [])):
                        ps, psb = psr.next()
                        for kc in range(KC):
                            P.op("tensor", I("matmul", ps[:rn, :n], W[key][:, kc, r0:r0 + rn], m[:, kc, :n],
                                                                                   start=(kc == 0), stop=(kc == KC - 1)), reads=[WB[key], mb], writes=[psb])
                        P.op("scalar", I("activation", out=lh[key[0]][:rn, ci, :n], in_=ps[:rn, :n], func=func),
                             reads=[psb], writes=[lhB[key[0]]])

                m, mb = mix(0)
                linear(W["wr"], WB["wr"], m, mb, n, 0, KC, psr, epi_copy(r_, rB))
                m, mb = mix(1)
                lora_hidden(m, mb, "w1", 64, AF.Tanh)
                w0c = pvc("rw_w0", o_)
                for fo in range(KC):
                    ps, psb = psr.next()
                    small_lin(W["w2"], WB["w2"], lh["w"][:, 0, :], lhB["w"], 64, n, ps, psb, fo * 128)
                    P.op("scalar", I("activation", out=sg_[:, fo, :n], in_=ps[:, :n], func=AF.Sigmoid,
                                                                        bias=pv[:, w0c + fo:w0c + fo + 1]), reads=[psb, cB], writes=[sgB])
                m, mb = mix(2)
                linear(W["wk"], WB["wk"], m, mb, n, 0, KC, psr, epi_copy(k_, kB))
                m, mb = mix(3)
                linear(W["wv"], WB["wv"], m, mb, n, 0, KC, psr, epi_copy(v_, vB))
                if o_ == 0:
                    P.op("sync", I("dma_start", out=fm(vfirst[G.name])[:, :, c0:c0 + n], in_=v_[:, :, :n]), reads=[vB], dma=P.dq())
                else:
                    P.op("sync", I("dma_start", out=vf[:, :, :n], in_=fm(vfirst[G.name])[:, :, c0:c0 + n]), writes=[vfB], dma=P.dq())
                    lora_hidden(m, mb, "v1", 32, AF.Copy)
                    v0c = pvc("rw_v0", 0)
                    for fo in range(KC):
                        ps, psb = psr.next()
                        small_lin(W["v2"], WB["v2"], lh["v"][:, 0, :], lhB["v"], 32, n, ps, psb, fo * 128)
                        P.op("scalar", I("activation", out=t1[:, fo, :n], in_=ps[:, :n], func=AF.Sigmoid,
                                                                            bias=pv[:, v0c + fo:v0c + fo + 1]), reads=[psb, cB], writes=[t1B])
                    P.op("vector", I("tensor_tensor", out=vf[:, :, :n], in0=vf[:, :, :n], in1=v_[:, :, :n], op=ALU.subtract),
                         reads=[vfB, vB], writes=[vfB])
                    P.op("vector", I("tensor_tensor", out=vf[:, :, :n], in0=vf[:, :, :n], in1=t1[:, :, :n], op=ALU.mult),
                         reads=[vfB, t1B], writes=[vfB])
                    P.op("vector", I("tensor_tensor", out=v_[:, :, :n], in0=v_[:, :, :n], in1=vf[:, :, :n], op=ALU.add),
                         reads=[vfB, vB], writes=[vB])
                m, mb = mix(4)
                lora_hidden(m, mb, "a1", 64, AF.Copy)
                a0c = pvc("rw_a0", o_)
                for fo in range(KC):
                    ps, psb = psr.next()
                    small_lin(W["a2"], WB["a2"], lh["a"][:, 0, :], lhB["a"], 64, n, ps, psb, fo * 128)
                    P.op("scalar", I("activation", out=a_[:, fo, :n], in_=ps[:, :n], func=AF.Sigmoid,
                                                                        bias=pv[:, a0c + fo:a0c + fo + 1]), reads=[psb, cB], writes=[aB])
                m, mb = mix(5)
                lora_hidden(m, mb, "g1", 160, AF.Sigmoid)
                for fo in range(KC):
                    ps, psb = psr.next()
                    small_lin(W["g2a"], WB["g2a"], lh["g"][:, 0, :], lhB["g"], 128, n, ps, psb, fo * 128, True, False)
                    small_lin(W["g2b"], WB["g2b"], lh["g"][:, 1, :], lhB["g"], 32, n, ps, psb, fo * 128, False, True)
                    P.op("scalar", I("copy", out=g_[:, fo, :n], in_=ps[:, :n]), reads=[psb], writes=[gB])
                if "r1" in _SKIP:
                    continue
                kkc = pvc("rw_k_k", o_)
                kac = pvc("rw_k_a", o_)
                rkc = pvc("rw_r_k", o_)
                sqb_, sqbB = sqr.next()
                P.op("gpsimd", I("tensor_tensor", out=kk[:, :, :n], in0=k_[:, :, :n], in1=bc(kkc), op=ALU.mult), reads=[kB, cB], writes=[kkB])
                P.op("scalar", I("activation", out=sqb_[:, :, :n], in_=kk[:, :, :n], func=AF.Square), reads=[kkB], writes=[sqbB])
                for g4 in range(2):
                    ps, psb = psr.next()
                    for k4 in range(4):
                        kc = g4 * 4 + k4
                        P.op("tensor", I("matmul", ps[:, k4 * 128:k4 * 128 + n], bones_b[:], sqb_[:, kc, :n], start=True, stop=True),
                             reads=[cB, sqbB], writes=[psb])
                    P.op("scalar", I("activation", out=t1[:, g4 * 4:g4 * 4 + 4, :n], in_=ps[:, :].rearrange("p (k t) -> p k t", t=128)[:, :, :n],
                                     func=AF.Sqrt), reads=[psb], writes=[t1B])
                P.op("vector", I("tensor_scalar", out=t1[:, :, :n], in0=t1[:, :, :n], scalar1=1e-12, scalar2=1.0, op0=ALU.max, op1=ALU.mult),
                     reads=[t1B], writes=[t1B])
                P.op("vector", I("reciprocal", out=t1[:, :, :n], in_=t1[:, :, :n]), reads=[t1B], writes=[t1B])
                P.op("vector", I("tensor_tensor", out=kk[:, :, :n], in0=kk[:, :, :n], in1=t1[:, :, :n], op=ALU.mult),
                     reads=[kkB, t1B], writes=[kkB])
                P.op("vector", I("tensor_scalar", out=t2[:, :, :n], in0=a_[:, :, :n], scalar1=-1.0, scalar2=1.0, op0=ALU.add, op1=ALU.mult),
                     reads=[aB], writes=[t2B])
                P.op("gpsimd", I("tensor_tensor", out=t2[:, :, :n], in0=t2[:, :, :n], in1=bc(kac), op=ALU.mult), reads=[t2B, cB], writes=[t2B])
                P.op("vector", I("scalar_tensor_tensor", out=km[:, :, :n], in0=t2[:, :, :n], scalar=1.0, in1=k_[:, :, :n],
                                 op0=ALU.add, op1=ALU.mult), reads=[t2B, kB], writes=[kmB])
                P.op("gpsimd", I("tensor_tensor", out=bb_[:, :, :n], in0=kk[:, :, :n], in1=a_[:, :, :n], op=ALU.mult),
                     reads=[kkB, aB], writes=[bbB])
                P.op("gpsimd", I("tensor_tensor", out=t2[:, :, :n], in0=r_[:, :, :n], in1=bc(rkc), op=ALU.mult), reads=[rB, cB, t2B], writes=[t2B])
                sqb2, sqb2B = sqr.next()
                P.op("vector", I("tensor_tensor", out=sqb2[:, :, :n], in0=t2[:, :, :n], in1=km[:, :, :n], op=ALU.mult), reads=[t2B, kmB], writes=[sqb2B])
                for g4 in range(2):
                    ps, psb = psr.next()
                    for k4 in range(4):
                        kc = g4 * 4 + k4
                        P.op("tensor", I("matmul", ps[:, k4 * 128:k4 * 128 + n], bones_b[:], sqb2[:, kc, :n], start=True, stop=True),
                             reads=[cB, sqb2B], writes=[psb])
                    P.op("vector", I("tensor_tensor", out=bon[:, g4 * 4:g4 * 4 + 4, :n], in0=ps[:, :].rearrange("p (k t) -> p k t", t=128)[:, :, :n],
                                     in1=v_[:, g4 * 4:g4 * 4 + 4, :n], op=ALU.mult), reads=[psb, vB], writes=[bonB])
                P.op("gpsimd", I("tensor_tensor", out=bon[:, :, :n], in0=bon[:, :, :n], in1=bc(lbc), op=ALU.add), reads=[bonB, cB], writes=[bonB])
                if "r2" in _SKIP:
                    continue
                def v4(t):
                    return t[:, :, :n].rearrange("p k (c t) -> p k c t", t=C)
                src, srcB = sg_, sgB
                sh = 1
                i = 0
                while sh < C:
                    dst, dstB = (csA, xhB) if i % 2 == 0 else (csB_, xxB)
                    P.op("vector", I("tensor_tensor", out=v4(dst)[:, :, :, sh:C], in0=v4(src)[:, :, :, sh:C], in1=v4(src)[:, :, :, 0:C - sh], op=ALU.add),
                         reads=[srcB], writes=[dstB])
                    P.op("scalar", I("copy", out=v4(dst)[:, :, :, 0:sh], in_=v4(src)[:, :, :, 0:sh]), reads=[srcB], writes=[dstB])
                    src, srcB = dst, dstB
                    sh *= 2
                    i += 1
                cs, csBuf = src, srcB
                Ein, EinB = hx, hxB
                Eprev, EprevB = t2, t2B
                Eend, EendB = vf, vfB
                P.op("scalar", I("activation", out=Ein[:, :, :n], in_=cs[:, :, :n], func=AF.Exp, scale=-CDEC), reads=[csBuf], writes=[EinB])
                P.op("scalar", I("activation", out=Eout[:, :, :n], in_=cs[:, :, :n], func=AF.Exp, scale=CDEC), reads=[csBuf], writes=[EoutB])
                P.op("vector", I("tensor_tensor", out=t1[:, :, :n], in0=cs[:, :, :n], in1=sg_[:, :, :n], op=ALU.subtract),
                     reads=[csBuf, sgB], writes=[t1B])
                P.op("scalar", I("activation", out=Eprev[:, :, :n], in_=t1[:, :, :n], func=AF.Exp, scale=-CDEC), reads=[t1B], writes=[EprevB])
                P.op("vector", I("tensor_tensor", out=v4(t1), in0=v4(cs), in1=v4(cs)[:, :, :, C - 1:C].to_broadcast([128, KC, nch, C]),
                                 op=ALU.subtract), reads=[csBuf], writes=[t1B])
                P.op("scalar", I("activation", out=Eend[:, :, :n], in_=t1[:, :, :n], func=AF.Exp, scale=CDEC), reads=[t1B], writes=[EendB])
                P.op("scalar", I("activation", out=PC[:, :, :nch], in_=v4(cs)[:, :, :, C - 1], func=AF.Exp, scale=-CDEC), reads=[csBuf], writes=[PCB])
                if "r3" in _SKIP:
                    continue
                P.op("vector", I("tensor_tensor", out=R32[:, :, :n], in0=r_[:, :, :n], in1=Ein[:, :, :n], op=ALU.mult), reads=[rB, EinB], writes=[R32B])
                P.op("gpsimd", I("tensor_copy", out=AR[:, :, 1, :n], in_=R32[:, :, :n]), reads=[R32B], writes=[ARB])
                ta_, taB = tmf.next()
                P.op("vector", I("scalar_tensor_tensor", out=ta_[:, :, :n], in0=kk[:, :, :n], scalar=-1.0, in1=Eprev[:, :, :n],
                                                                 op0=ALU.mult, op1=ALU.mult), reads=[kkB, EprevB], writes=[taB])
                P.op("gpsimd", I("tensor_copy", out=AR[:, :, 0, :n], in_=ta_[:, :, :n]), reads=[taB], writes=[ARB])
                P.op("vector", I("tensor_tensor", out=Bt[:, :, :n], in0=bb_[:, :, :n], in1=Eout[:, :, :n], op=ALU.mult), reads=[bbB, EoutB], writes=[BKB])
                P.op("gpsimd", I("tensor_tensor", out=Kt[:, :, :n], in0=km[:, :, :n], in1=Eout[:, :, :n], op=ALU.mult), reads=[kmB, EoutB], writes=[BKB])
                tb_, tbB = tmf.next()
                P.op("vector", I("tensor_tensor", out=tb_[:, :, :n], in0=bb_[:, :, :n], in1=Eend[:, :, :n], op=ALU.mult), reads=[bbB, EendB], writes=[tbB])

                def transp(src, sB, dst_fn, dB_fn, n=n):
                    for fc in range(KC):
                        ps, psb = psr.next()
                        P.op("tensor", I("transpose", out=ps[:n, 0:128], in_=src[:, fc, :n], identity=ident_f[:]),
                             reads=[sB, cB], writes=[psb])
                        dst_fn(fc, ps, psb)
                zc = cur

                def dst_A(fc, ps, psb, n=n):
                    P.op("vector", I("tensor_copy", out=Z[zc][:n, 2 * fc:2 * fc + 2, 0:64], in_=ps[:n, 0:128].rearrange("p (h k) -> p h k", k=64)),
                         reads=[psb], writes=[ZB[zc][2 * fc], ZB[zc][2 * fc + 1]])
                transp(ta_, taB, dst_A, None)

                def dst_simple(dst, dB, n=n):
                    def f(fc, ps, psb):
                        P.op("scalar", I("copy", out=dst[:n, fc, :], in_=ps[:n, 0:128]), reads=[psb], writes=[dB])
                    return f
                transp(tb_, tbB, dst_simple(BPtm, BPB), None)
                tc_, tcB = tmf.next()
                P.op("vector", I("tensor_tensor", out=tc_[:, :, :n], in0=km[:, :, :n], in1=Eend[:, :, :n], op=ALU.mult), reads=[kmB, EendB], writes=[tcB])
                transp(tc_, tcB, dst_simple(KPtm, KPB), None)
                transp(v_, vB, dst_simple(Vtm, VtmB), None)

                if "r4" in _SKIP:
                    continue
                mk3_ = mk3 if G.name == "p" else mk3s
                mkL_ = mkL if G.name == "p" else mkLs
                HG = 4

                def stage_a(hd):
                    hb0 = 64 * (hd % 2)
                    fc = hd // 2
                    hs = slice(hb0, hb0 + 64)
                    psA, psAb = psr.next()
                    P.op("tensor", I("matmul", psA[:n, 0:2 * n].rearrange("p (a t) -> p a t", a=2), Bt[hs, fc, :n], AR[hs, fc, :, :n], start=True, stop=True),
                         reads=[BKB, ARB], writes=[psAb])
                    xa_, xaB = XAs.next()
                    P.op("vector", I("tensor_tensor", out=xa_[:n, :, :n], in0=psA[:n, 0:2 * n].rearrange("p (a t) -> p a t", a=2),
                                     in1=mk3_[:n, :, :n], op=ALU.mult), reads=[psAb, cB], writes=[xaB])
                    psB_, psBb = psr.next()
                    P.op("tensor", I("matmul", psB_[:n, 0:2 * n].rearrange("p (a t) -> p a t", a=2), Kt[hs, fc, :n], AR[hs, fc, :, :n], start=True, stop=True),
                         reads=[BKB, ARB], writes=[psBb])
                    xb_, xbB = XBs.next()
                    P.op("vector", I("tensor_tensor", out=xb_[:n, :, :n], in0=psB_[:n, 0:2 * n].rearrange("p (a t) -> p a t", a=2),
                                     in1=mk3_[:n, :, :n], op=ALU.mult), reads=[psBb, cB], writes=[xbB])
                    psC, psCb = psr.next()
                    P.op("tensor", I("matmul", psC[:n, 0:n], AR[hs, fc, 0, :n], Bt[hs, fc, :n], start=True, stop=True),
                         reads=[BKB, ARB], writes=[psCb])
                    L0, L0B = Ls.next()
                    P.op("vector", I("tensor_tensor", out=L0[:n, :n], in0=psC[:n, 0:n], in1=mkL_[:n, :n], op=ALU.mult),
                         reads=[psCb, cB], writes=[L0B])
                    psG, psGb = psr.next()
                    P.op("tensor", I("matmul", psG[:n, 0:64], xb_[:n, 0, :n], Vtm[:n, fc, hs], start=True, stop=True),
                         reads=[xbB, VtmB], writes=[psGb])
                    P.op("scalar", I("copy", out=Z[zc][:n, hd, 64:128], in_=psG[:n, 0:64]), reads=[psGb], writes=[ZB[zc][hd]])
                    return dict(hd=hd, fc=fc, hs=hs, xa_=xa_, xaB=xaB, xb_=xb_, xbB=xbB, zi=zc,
                                Mj=xa_[:n, 0, :n], MjB=xaB, Lj=L0[:n, :n], LjB=L0B)

                def stage_d(st):
                    hd, fc, hs, xa_, xaB, xb_, xbB = st["hd"], st["fc"], st["hs"], st["xa_"], st["xaB"], st["xb_"], st["xbB"]
                    ZF = Z[st["zi"]]; ZFB = ZB[st["zi"]][hd]
                    psDs = []
                    for c in range(nch):
                        cs_ = slice(c * C, (c + 1) * C)
                        psD, psDb = psr.next()
                        psDs.append((psD, psDb))
                        P.op("tensor", I("matmul", psD[hs, c * 128:c * 128 + 64], ZF[cs_, hd, 0:64], BPtm[cs_, fc, hs], start=True, stop=True),
                             reads=[ZFB, BPB], writes=[psDb])
                        P.op("tensor", I("matmul", psD[hs, c * 128 + 64:c * 128 + 128], BPtm[cs_, fc, hs], ZF[cs_, hd, 64:128], start=True, stop=False),
                             reads=[ZFB, BPB], writes=[psDb])
                        P.op("tensor", I("matmul", psD[hs, c * 128 + 64:c * 128 + 128], KPtm[cs_, fc, hs], Vtm[cs_, fc, hs], start=False, stop=True),
                             reads=[KPB, VtmB], writes=[psDb])
                    for c in range(nch):
                        psD, psDb = psDs[c]
                        P.op("vector", I("scalar_tensor_tensor", out=PhiT[hs, fc, c, :], in0=ident_f[hs, hs], scalar=PC[hs, fc, c:c + 1],
                                         in1=psD[hs, c * 128:c * 128 + 64], op0=ALU.mult, op1=ALU.add), reads=[psDb, PCB, cB], writes=[PhiB])
                        P.op("vector", I("tensor_copy", out=Psi[hs, fc, c, :], in_=psD[hs, c * 128 + 64:c * 128 + 128]),
                             reads=[psDb], writes=[PsiB])
                    psO_, psOb = psr.next()
                    P.op("tensor", I("matmul", psO_[hs, 0:n], ZF[:n, hd, 0:64], xa_[:n, 1, :n], start=True, stop=True),
                         reads=[ZFB, xaB], writes=[psOb])
                    psY0, psY0b = psr.next()
                    P.op("tensor", I("matmul", psY0[hs, 0:n], ZF[:n, hd, 64:128], xa_[:n, 1, :n], start=True, stop=False),
                         reads=[ZFB, xaB], writes=[psY0b])
                    P.op("tensor", I("matmul", psY0[hs, 0:n], Vtm[:n, fc, hs], xb_[:n, 1, :n], start=False, stop=True),
                         reads=[xbB, VtmB], writes=[psY0b])
                    P.op("vector", I("tensor_tensor", out=Om[hs, fc, :n], in0=psO_[hs, 0:n], in1=R32[hs, fc, :n], op=ALU.add),
                         reads=[psOb, R32B], writes=[OmB])
                    P.op("scalar", I("copy", out=Y0[hs, fc, :n], in_=psY0[hs, 0:n]), reads=[psY0b], writes=[Y0B])

                nxt = [stage_a(hd) for hd in range(0, HG)]
                for g0 in range(0, 16, HG):
                    sts = nxt
                    for j in range(nlog):
                        pz = []
                        for st in sts:
                            psZ, psZb = psr.next()
                            P.op("tensor", I("matmul", psZ[:n, 0:128], st["Mj"], Z[st["zi"]][:n, st["hd"], :], start=True, stop=True),
                                 reads=[st["MjB"], ZB[st["zi"]][st["hd"]]], writes=[psZb])
                            pz.append((psZ, psZb))
                        for st, (psZ, psZb) in zip(sts, pz):
                            zi, hd = st["zi"], st["hd"]
                            P.op("vector", I("tensor_tensor", out=Z[1 - zi][:n, hd, :], in0=psZ[:n, 0:128], in1=Z[zi][:n, hd, :], op=ALU.add),
                                 reads=[psZb, ZB[zi][hd]], writes=[ZB[1 - zi][hd]])
                            st["zi"] = 1 - zi
                        if j < nlog - 1:
                            pq = []
                            for st in sts:
                                psS_, psSb = psr.next()
                                P.op("tensor", I("matmul", psS_[:n, 0:n], st["Lj"], st["Mj"], start=True, stop=True),
                                     reads=[st["MjB"], st["LjB"]], writes=[psSb])
                                if j < nlog - 2:
                                    P.op("tensor", I("matmul", psS_[:n, n:2 * n], st["Mj"], st["Lj"], start=True, stop=True),
                                         reads=[st["MjB"], st["LjB"]], writes=[psSb])
                                pq.append((psS_, psSb))
                            w_ = 2 if j < nlog - 2 else 1
                            for st, (psS_, psSb) in zip(sts, pq):
                                ml, mlB = MLr.next()
                                P.op("scalar", I("copy", out=ml[:n, 0:w_, :n], in_=psS_[:n, 0:w_ * n].rearrange("p (a t) -> p a t", a=w_)),
                                     reads=[psSb], writes=[mlB])
                                st["Mj"] = ml[:n, 0, :n]; st["MjB"] = mlB
                                st["Lj"] = ml[:n, 1, :n]; st["LjB"] = mlB
                    if g0 + HG < 16:
                        nxt = [stage_a(hd) for hd in range(g0 + HG, g0 + 2 * HG)]
                    for st in sts:
                        stage_d(st)
                if "r5" in _SKIP:
                    continue
                cur = zc
                if first:
                    if G.name == "p":
                        P.op("gpsimd", I("memset", S[0][:], 0.0), writes=[SB_[0]])
                    else:
                        P.op("sync", I("dma_start", out=S[0][:], in_=swkv_d[o_, s]), writes=[SB_[0]], dma=P.dq())
                    si = 0
                for c in range(nch):
                    cs_ = slice(c * C, (c + 1) * C)
                    P.op("scalar", I("copy", out=S16[:], in_=S[si][:]), reads=[SB_[si]], writes=[S16B])
                    for hd in range(16 if "cy" not in _SKIP else 0):
                        hb0 = 64 * (hd % 2); fc = hd // 2; hs = slice(hb0, hb0 + 64)
                        P.op("tensor", I("matmul", psY2[fc // 4][hs, fc % 4, cs_], S16[hs, fc, :], Om[hs, fc, cs_], start=True, stop=True),
                             reads=[OmB, S16B], writes=[psYB])
                    for hh_ in range(2 if "cy" not in _SKIP else 0):
                        hsl = slice(hh_ * 4, hh_ * 4 + 4)
                        P.op("vector", I("tensor_tensor", out=Ytm[:, hsl, cs_], in0=psY2[hh_][:, :, cs_], in1=Y0[:, hsl, cs_], op=ALU.add),
                             reads=[psYB, Y0B], writes=[YB])
                    if "cs" in _SKIP:
                        continue
                    for hd in range(16):
                        hb0 = 64 * (hd % 2); fc = hd // 2; hs = slice(hb0, hb0 + 64)
                        P.op("tensor", I("matmul", psSt[hs, fc, :], PhiT[hs, fc, c, :], S[si][hs, fc, :], start=True, stop=True),
                             reads=[PhiB, SB_[si]], writes=[psStB])
                    P.op("vector", I("tensor_tensor", out=S[1 - si][:, :, :], in0=psSt[:, :, :], in1=Psi[:, :, c, :], op=ALU.add),
                         reads=[psStB, PsiB], writes=[SB_[1 - si]])
                    si = 1 - si
                if t0 + n == G.T:
                    P.op("sync", I("dma_start", out=wkv_o[G.name][o_, s], in_=S[si][:]), reads=[SB_[si]], dma=P.dq())
                if "r6" in _SKIP:
                    continue
                yo_, yoB = yo.next()
                for fc in range(KC):
                    ps, psb = psr.next()
                    P.op("tensor", I("matmul", ps[:, :n], bones_f[:], Ytm[:, fc, :n], start=True, stop=True), reads=[cB, YB], writes=[psb])
                    P.op("vector", I("scalar_tensor_tensor", out=yc[:, fc, :n], in0=ps[:, :n], scalar=-1.0 / 64, in1=Ytm[:, fc, :n],
                                     op0=ALU.mult, op1=ALU.add), reads=[psb, YB], writes=[ycB])
                P.op("scalar", I("activation", out=ysq[:, :, :n], in_=yc[:, :, :n], func=AF.Square), reads=[ycB], writes=[ysqB])
                for fc in range(KC):
                    ps, psb = psr.next()
                    P.op("tensor", I("matmul", ps[:, :n], bones_f[:], ysq[:, fc, :n], start=True, stop=True), reads=[cB, ysqB], writes=[psb])
                    P.op("vector", I("tensor_scalar", out=yt32[:, fc, :n], in0=ps[:, :n], scalar1=1.0 / 64, scalar2=LNEPS, op0=ALU.mult, op1=ALU.add),
                         reads=[psb], writes=[yt32B])
                P.op("scalar", I("activation", out=yt32[:, :, :n], in_=yt32[:, :, :n], func=AF.Sqrt), reads=[yt32B], writes=[yt32B])
                P.op("vector", I("reciprocal", out=yt32[:, :, :n], in_=yt32[:, :, :n]), reads=[yt32B], writes=[yt32B])
                P.op("vector", I("tensor_tensor", out=yc[:, :, :n], in0=yc[:, :, :n], in1=yt32[:, :, :n], op=ALU.mult), reads=[ycB, yt32B], writes=[ycB])
                for fc in range(KC):
                    P.op("vector", I("scalar_tensor_tensor", out=yt32[:, fc, :n], in0=yc[:, fc, :n], scalar=pv[:, lgc + fc:lgc + fc + 1],
                                     in1=bon[:, fc, :n], op0=ALU.mult, op1=ALU.add), reads=[ycB, bonB, cB], writes=[yt32B])
                P.op("gpsimd", I("tensor_tensor", out=yo_[:, :, :n], in0=yt32[:, :, :n], in1=g_[:, :, :n], op=ALU.mult),
                     reads=[yt32B, gB], writes=[yoB])
                P.op("sync", I("dma_start", out=fm(mixT[G.name])[:, :, c0:c0 + n], in_=yo_[:, :, :n]), reads=[yoB], dma=P.dq())
        P.end_phase()

    plist = []
    for l in range(depth):
        if l % 2 == 0:
            plist.append((phase_even_proj, (l,)))
            plist.append((phase_even_attn, (l,)))
            plist.append((phase_out_xa, (l, w_out_d[l // 2])))
        else:
            plist.append((phase_rwkv, (l,)))
            plist.append((phase_out_xa, (l, rw_w_d["wo"][l // 2])))
        plist.append((phase_ffn, (l, l == depth - 1)))
    for f, a in plist[:_MAXPH]:
        f(*a)
    P.emit()
    return nc


_CACHE = {}
_DEPTH = DEPTH
_MAXPH = 1000
_NCORE = 8


def prep_common(inp):
    c = {}
    c["pvec"] = pack_pv(inp)
    lam = np.stack([np.concatenate([inp["ev_lam_q1"][e], inp["ev_lam_k1"][e], inp["ev_lam_q2"][e], inp["ev_lam_k2"][e]])
                    for e in range(NEVEN)]).reshape(1, -1)
    c["lam"] = np.ascontiguousarray(lam, np.float32)
    c["lnx"] = np.ascontiguousarray(np.stack([inp["rw_lnx_g"][0], inp["rw_lnx_b"][0], inp["rw_lnx_g"][1], inp["rw_lnx_b"][1]]), np.float32)
    c["ev_w_in"] = np.stack([wl(inp["ev_w_in"][e]) for e in range(NEVEN)])
    c["ev_pool_w"] = np.ascontiguousarray(np.asarray(inp["ev_pool_w"]).transpose(0, 2, 1, 3))
    c["ev_w_out"] = np.stack([wl(inp["ev_w_out"][e]) for e in range(NEVEN)])
    for k in ("wr", "wk", "wv", "wo"):
        c["rw_" + k] = np.stack([wl(inp["rw_" + k][o]) for o in range(NODD)])
    for k in ("w1", "a1", "g1", "v1"):
        a = inp["rw_" + k]
        c["rw_" + k] = np.stack([wl(a[o]) for o in range(a.shape[0])])
    for k in ("w2", "a2", "g2", "v2"):
        c["rw_" + k] = np.ascontiguousarray(inp["rw_" + k])
    for k in ("wq", "wk", "wv", "wo"):
        c["xa_" + k] = np.stack([wl(inp["xa_" + k][l]) for l in range(DEPTH)])
    c["ffn_wg"] = np.stack([wl(inp["ffn_wg"][l]) for l in range(DEPTH)])
    c["ffn_wu"] = np.stack([wl(inp["ffn_wu"][l]) for l in range(DEPTH)])
    c["ffn_wd"] = np.stack([wl(inp["ffn_wd"][l]) for l in range(DEPTH)])
    return c


def kernel(**inp):
    inp = {k: np.asarray(v) for k, v in inp.items()}
    B, TP, _ = inp["x_prompt"].shape
    SB, TS, _ = inp["x_sample"].shape
    PAST = inp["cache_diff_k"].shape[2]
    NMEM = inp["mem_prompt"].shape[1]
    NCORE = _NCORE
    NSB = SB // NCORE
    key = (TP, NSB, TS, PAST, NMEM, _DEPTH)
    if key not in _CACHE:
        _CACHE[key] = build(TP, NSB, TS, PAST, NMEM, _DEPTH)
    nc = _CACHE[key]
    common = prep_common(inp)
    in_maps = []
    for c in range(NCORE):
        b = c % B
        sb = slice(c * NSB, (c + 1) * NSB)
        m = dict(common)
        m["xT_p"] = np.ascontiguousarray(inp["x_prompt"][b].T)
        m["xT_s"] = np.ascontiguousarray(inp["x_sample"][sb].reshape(NSB * TS, D).T)
        m["memT"] = wl(np.ascontiguousarray(inp["mem_prompt"][b].T))
        m["cache_kT"] = np.ascontiguousarray(inp["cache_diff_k"][:, sb].transpose(0, 1, 3, 4, 2))
        m["cache_v"] = np.ascontiguousarray(inp["cache_diff_v"][:, sb])
        sp = inp["state_pool"][:, sb]
        m["state_pool"] = np.ascontiguousarray(sp.reshape(NEVEN, NSB, 15, 4, 128).transpose(0, 1, 4, 3, 2))
        m["state_shift"] = np.ascontiguousarray(inp["state_rw_shift"][:, sb].reshape(NODD, NSB, KC, 128).transpose(0, 1, 3, 2))
        sw = inp["state_rw_wkv"][:, sb]
        m["state_wkvT"] = np.ascontiguousarray(sw.reshape(NODD, NSB, 8, 2, 64, 64).transpose(0, 1, 3, 5, 2, 4).reshape(NODD, NSB, 128, 8, 64))
        mk = inp["cache_mem_k"][:, sb].reshape(DEPTH, NSB, NMEM, KC, 128)
        m["cache_mkT"] = np.ascontiguousarray(mk.transpose(0, 1, 4, 3, 2))
        m["cache_mv"] = np.ascontiguousarray(inp["cache_mem_v"][:, sb].reshape(DEPTH, NSB, NMEM, D))
        in_maps.append(m)
    res = run_bass_kernel_spmd(nc, in_maps, core_ids=list(range(NCORE))).results

    def unfm(a):
        return np.swapaxes(a, 0, 1).reshape((-1,) + a.shape[2:])

    y_p = np.stack([res[b]["yT_p"].T for b in range(B)])
    y_s = np.concatenate([res[c]["yT_s"].T.reshape(NSB, TS, D) for c in range(NCORE)])
    p_k = np.stack([np.stack([res[b]["kT_p"][e].T.reshape(TP, 4, 128) for b in range(B)]) for e in range(NEVEN)])
    p_v = np.stack([np.stack([res[b]["v_p"][e].reshape(TP, 4, 128) for b in range(B)]) for e in range(NEVEN)])
    s_k = np.stack([np.concatenate([res[c]["kT_s"][e].T.reshape(NSB, TS, 4, 128) for c in range(NCORE)]) for e in range(NEVEN)])
    s_v = np.stack([np.concatenate([res[c]["v_s"][e].reshape(NSB, TS, 4, 128) for c in range(NCORE)]) for e in range(NEVEN)])

    def pool_back(a):
        return a.transpose(0, 1, 4, 3, 2).reshape(a.shape[0], a.shape[1], 15, 512)

    p_pool = pool_back(np.concatenate([res[b]["pool_p"] for b in range(B)], axis=1))
    s_pool = pool_back(np.concatenate([res[c]["pool_s"] for c in range(NCORE)], axis=1))

    def shift_back(a):
        return a.transpose(0, 1, 3, 2).reshape(a.shape[0], a.shape[1], D)

    p_sh = shift_back(np.concatenate([res[b]["shift_p"] for b in range(B)], axis=1))
    s_sh = shift_back(np.concatenate([res[c]["shift_s"] for c in range(NCORE)], axis=1))

    def wkv_back(a):
        o, n = a.shape[:2]
        return a.reshape(o, n, 2, 64, 8, 64).transpose(0, 1, 4, 2, 5, 3).reshape(o, n, 16, 64, 64)

    p_wkv = wkv_back(np.concatenate([res[b]["wkv_p"] for b in range(B)], axis=1))
    s_wkv = wkv_back(np.concatenate([res[c]["wkv_s"] for c in range(NCORE)], axis=1))
    p_mk = np.stack([np.stack([unfm(res[b]["memkT"][l]).T.reshape(NMEM, 4, 256) for b in range(B)]) for l in range(DEPTH)])
    p_mv = np.stack([np.stack([res[b]["memv"][l].reshape(NMEM, 4, 256) for b in range(B)]) for l in range(DEPTH)])
    outs = (y_p, y_s, p_k, p_v, p_pool, p_sh, p_wkv, p_mk, p_mv, s_k, s_v, s_pool, s_sh, s_wkv)
    return tuple(np.ascontiguousarray(o, dtype=np.float32) for o in outs)
```
